# Optimizing a Trainium2 kernel written in Bass

```python
import math
import jax, jax.numpy as jnp
from jax import lax
import numpy as np

D_MODEL = 1024
BATCH = 2
SEQ = 8192
DEPTH = 1
DEC_BATCH = 128
DEC_SEQ = 8
PAST_LEN = 2048
PAGE_SIZE = 128

D_MIX = D_MODEL
D_ATT = D_MIX // 2
D_CONV = D_MIX - D_ATT
N_HEADS = 4
HEAD_DIM = D_ATT // (2 * N_HEADS)
V_DIM = 2 * HEAD_DIM
CONV_W = 31
Q_BLOCK = 128
EPS = 1e-5
SPLITS = (D_ATT, 2 * D_ATT, 3 * D_ATT, 4 * D_ATT, 4 * D_ATT + D_CONV, 4 * D_ATT + 2 * D_CONV)
D_IN = 4 * D_ATT + 3 * D_CONV

kernel_name = 'hymba_diffattn_conformer_step'


def rmsnorm(x, g):
    xf = x.astype(jnp.float32)
    y = xf * lax.rsqrt(jnp.mean(xf * xf, axis=-1, keepdims=True) + EPS)
    return y.astype(x.dtype) * g


def layernorm(x, g, b):
    xf = x.astype(jnp.float32)
    mu = jnp.mean(xf, axis=-1, keepdims=True)
    var = jnp.mean(jnp.square(xf - mu), axis=-1, keepdims=True)
    return ((xf - mu) * lax.rsqrt(var + EPS)).astype(x.dtype) * g + b


def in_project(h, w_in):
    n, t = h.shape[0], h.shape[1]
    z = h @ w_in
    q, k, v, g_att, a, b, g_conv = jnp.split(z, SPLITS, axis=-1)
    q = q.reshape(n, t, N_HEADS, 2, HEAD_DIM)
    k = k.reshape(n, t, N_HEADS, V_DIM)
    v = v.reshape(n, t, N_HEADS, V_DIM)
    u = a * jax.nn.sigmoid(b)
    return q, k, v, g_att, u, g_conv


def diff_attend(q, k, v, q_pos, k_pos, lam):
    k = k.reshape(k.shape[:3] + (2, HEAD_DIM))
    s = jnp.einsum('nqhmd,nkhmd->nhmqk', q, k).astype(jnp.float32) * (HEAD_DIM ** -0.5)
    mask = k_pos[None, :] <= q_pos[:, None]
    s = jnp.where(mask, s, -jnp.inf)
    p = jax.nn.softmax(s, axis=-1)
    a = p[:, :, 0] - lam * p[:, :, 1]
    return jnp.einsum('nhqk,nkhe->nqhe', a.astype(v.dtype), v)


def attn_post(o, subln_g, lam_init):
    n, t = o.shape[0], o.shape[1]
    o = rmsnorm(o, subln_g) * (1.0 - lam_init)
    return o.reshape(n, t, D_ATT)


def conv_branch(u, buf, dw_w, dw_b, ln_g, ln_b, w_pw2, b_pw2):
    u_ext = jnp.concatenate([buf, u], axis=1)
    c = lax.conv_general_dilated(u_ext, dw_w[:, None, :], window_strides=(1,), padding='VALID',
                                 dimension_numbers=('NWC', 'WIO', 'NWC'),
                                 feature_group_count=D_CONV) + dw_b
    c = jax.nn.silu(layernorm(c, ln_g, ln_b))
    return c @ w_pw2 + b_pw2, u_ext[:, -(CONV_W - 1):]


def mix_out(att, conv, g_att, g_conv, w_out):
    m = jnp.concatenate([att * jax.nn.silu(g_att), conv * jax.nn.silu(g_conv)], axis=-1)
    return m @ w_out


def setup_inputs(seed: int = 0) -> dict:
    key = jax.random.key(seed)
    ks = jax.random.split(key, 24)
    n_pages = PAST_LEN // PAGE_SIZE
    n_used = DEC_BATCH * n_pages
    n_pool = n_used + max(1, n_used // 4)
    f32 = jnp.float32
    nrm = lambda k, s, sc: jax.random.normal(k, s, f32) * sc
    page_table = jax.random.permutation(ks[5], n_pool)[:n_used].reshape(DEC_BATCH, n_pages).astype(jnp.int32)
    return {
        'x_prompt': nrm(ks[0], (BATCH, SEQ, D_MODEL), 1.0),
        'x_sample': nrm(ks[1], (DEC_BATCH, DEC_SEQ, D_MODEL), 1.0),
        'cache_k': nrm(ks[2], (DEPTH, n_pool, PAGE_SIZE, N_HEADS, V_DIM), 1.0),
        'cache_v': nrm(ks[3], (DEPTH, n_pool, PAGE_SIZE, N_HEADS, V_DIM), 1.0),
        'state_conv': nrm(ks[4], (DEPTH, DEC_BATCH, CONV_W - 1, D_CONV), 0.5),
        'page_table': page_table,
        'norm_g': 1.0 + nrm(ks[6], (DEPTH, D_MODEL), 0.02),
        'w_in': nrm(ks[7], (DEPTH, D_MODEL, D_IN), D_MODEL ** -0.5),
        'lambda_q1': nrm(ks[8], (DEPTH, HEAD_DIM), 0.1),
        'lambda_k1': nrm(ks[9], (DEPTH, HEAD_DIM), 0.1),
        'lambda_q2': nrm(ks[10], (DEPTH, HEAD_DIM), 0.1),
        'lambda_k2': nrm(ks[11], (DEPTH, HEAD_DIM), 0.1),
        'subln_g': 1.0 + nrm(ks[12], (DEPTH, V_DIM), 0.02),
        'dw_w': nrm(ks[13], (DEPTH, CONV_W, D_CONV), CONV_W ** -0.5),
        'dw_b': nrm(ks[14], (DEPTH, D_CONV), 0.02),
        'conv_ln_g': 1.0 + nrm(ks[15], (DEPTH, D_CONV), 0.02),
        'conv_ln_b': nrm(ks[16], (DEPTH, D_CONV), 0.02),
        'w_pw2': nrm(ks[17], (DEPTH, D_CONV, D_CONV), D_CONV ** -0.5),
        'b_pw2': nrm(ks[18], (DEPTH, D_CONV), 0.02),
        'w_out': nrm(ks[19], (DEPTH, D_MIX, D_MODEL), D_MIX ** -0.5),
        'final_norm_g': 1.0 + nrm(ks[20], (D_MODEL,), 0.02),
    }


def reference(x_prompt, x_sample, cache_k, cache_v, state_conv, page_table, norm_g, w_in,
              lambda_q1, lambda_k1, lambda_q2, lambda_k2, subln_g, dw_w, dw_b, conv_ln_g,
              conv_ln_b, w_pw2, b_pw2, w_out, final_norm_g):
    yp, ys = x_prompt, x_sample
    kp_l, vp_l, cp_l, ks_l, vs_l, cs_l = [], [], [], [], [], []
    n_blocks = SEQ // Q_BLOCK
    for l in range(DEPTH):
        lam_init = 0.8 - 0.6 * math.exp(-0.3 * l)
        f = lambda t: t.astype(jnp.float32)
        lam = (jnp.exp(jnp.sum(f(lambda_q1[l]) * f(lambda_k1[l])))
               - jnp.exp(jnp.sum(f(lambda_q2[l]) * f(lambda_k2[l]))) + lam_init)
        conv_p = (dw_w[l], dw_b[l], conv_ln_g[l], conv_ln_b[l], w_pw2[l], b_pw2[l])

        h = rmsnorm(yp, norm_g[l])
        q, k, v, g_att, u, g_conv = in_project(h, w_in[l])
        k_pos = jnp.arange(SEQ)

        def q_block(i, q=q, k=k, v=v, k_pos=k_pos, lam=lam):
            qb = lax.dynamic_slice_in_dim(q, i * Q_BLOCK, Q_BLOCK, axis=1)
            return diff_attend(qb, k, v, i * Q_BLOCK + jnp.arange(Q_BLOCK), k_pos, lam)

        o = lax.map(q_block, jnp.arange(n_blocks))
        o = jnp.moveaxis(o, 0, 1).reshape(BATCH, SEQ, N_HEADS, V_DIM)
        att = attn_post(o, subln_g[l], lam_init)
        zero_buf = jnp.zeros((BATCH, CONV_W - 1, D_CONV), u.dtype)
        conv, buf_p = conv_branch(u, zero_buf, *conv_p)
        yp = yp + mix_out(att, conv, g_att, g_conv, w_out[l])
        kp_l.append(k); vp_l.append(v); cp_l.append(buf_p)

        h = rmsnorm(ys, norm_g[l])
        q, k, v, g_att, u, g_conv = in_project(h, w_in[l])
        k_past = cache_k[l][page_table].reshape(DEC_BATCH, PAST_LEN, N_HEADS, V_DIM)
        v_past = cache_v[l][page_table].reshape(DEC_BATCH, PAST_LEN, N_HEADS, V_DIM)
        keys = jnp.concatenate([k_past, k], axis=1)
        vals = jnp.concatenate([v_past, v], axis=1)
        o = diff_attend(q, keys, vals, PAST_LEN + jnp.arange(DEC_SEQ),
                        jnp.arange(PAST_LEN + DEC_SEQ), lam)
        att = attn_post(o, subln_g[l], lam_init)
        conv, buf_s = conv_branch(u, state_conv[l], *conv_p)
        ys = ys + mix_out(att, conv, g_att, g_conv, w_out[l])
        ks_l.append(k); vs_l.append(v); cs_l.append(buf_s)

    y_prompt = rmsnorm(yp, final_norm_g)
    y_sample = rmsnorm(ys, final_norm_g)
    return (y_prompt, y_sample, jnp.stack(kp_l), jnp.stack(vp_l), jnp.stack(cp_l),
            jnp.stack(ks_l), jnp.stack(vs_l), jnp.stack(cs_l))
```

```python
import math
import numpy as np
import concourse.bass as bass
import concourse.mybir as mybir
from concourse.bass_utils import run_bass_kernel_spmd

F32 = mybir.dt.float32
BF16 = mybir.dt.bfloat16
I32 = mybir.dt.int32
ALU = mybir.AluOpType
AF = mybir.ActivationFunctionType

D = 1024
DC = 8
NH = 4
CW = 31
EPS = 1e-5
NEG = -30000.0
LAM_INIT = 0.8 - 0.6 * math.exp(-0.3 * 0)

CFG_FULL = dict(U=512, NSQ=16, NPG=16, NPOOL=2560, DSEQ=8)


class Sched:
    ENG = ("pe", "act", "dve", "pool", "sp")
    CAP = 20000
    NDS = 80

    def __init__(self):
        self.ops = []
        self.last_w = {}
        self.readers = {}
        self.bars = []

    EXCL = ("PSB", "ST", "OB", "SST", "SOT", "SLT")

    def add(self, eng, fn, r=(), w=(), dma=False):
        r = list(r)
        w = list(w)
        for t in list(r):
            if t.startswith(self.EXCL):
                r.remove(t)
                if t not in w:
                    w.append(t)
        deps = set()
        for t in r:
            if t in self.last_w:
                deps.add(self.last_w[t])
        for t in w:
            if t in self.last_w:
                deps.add(self.last_w[t])
            for x in self.readers.get(t, ()):
                deps.add(x)
        idx = len(self.ops)
        deps.discard(idx)
        self.ops.append(dict(eng=eng, fn=fn, deps=deps, dma=dma, sig=False, waits=[]))
        for t in r:
            self.readers.setdefault(t, []).append(idx)
        for t in w:
            self.last_w[t] = idx
            self.readers[t] = []
        return idx

    def barrier(self):
        self.bars.append(len(self.ops))

    def finalize(self, nc, stack):
        import os
        mx = int(os.environ.get("KMAXOPS", "0"))
        if mx:
            self.ops = self.ops[:mx]
            self.bars = [b for b in self.bars if b <= mx]
            self.bars.append(len(self.ops))
            self.ops.append(dict(eng="sp", fn=lambda e: None, deps=set(), dma=False, sig=False, waits=[]))
        ops = self.ops
        seq = {e: 0 for e in self.ENG}
        ndma = {e: 0 for e in self.ENG}
        half = self.NDS // 2
        for o in ops:
            if o["dma"]:
                base = 0 if o["eng"] == "pool" else half
                k = ndma[o["eng"]]
                o["dsem"] = base + k % half
                o["dval"] = 16 * (k // half + 1)
                ndma[o["eng"]] += 1
                o["sig"] = True
            else:
                seq[o["eng"]] += 1
                o["seq"] = seq[o["eng"]]
        known = {e: {x: 0 for x in self.ENG} for e in self.ENG}
        kdma = {e: [0] * self.NDS for e in self.ENG}
        snaps = []
        for bi in self.bars:
            sc = {x: 0 for x in self.ENG}
            sd = {}
            for p in ops[:bi]:
                if p["dma"]:
                    sd[p["dsem"]] = max(sd.get(p["dsem"], 0), p["dval"])
                else:
                    sc[p["eng"]] = max(sc[p["eng"]], p["seq"])
            snaps.append((bi, sc, sd))
        for oi, o in enumerate(ops):
            e = o["eng"]
            need = {}
            needd = {}
            for (bi, sc, sd) in snaps:
                if oi >= bi:
                    for x, v in sc.items():
                        if v > 0:
                            need[x] = (max(need.get(x, (0, None))[0], v), None)
                    for x, v in sd.items():
                        needd[x] = max(needd.get(x, 0), v)
            if o["dma"] and o["dval"] > 16:
                needd[o["dsem"]] = max(needd.get(o["dsem"], 0), o["dval"] - 16)
            for di in o["deps"]:
                p = ops[di]
                if p["dma"]:
                    needd[p["dsem"]] = max(needd.get(p["dsem"], 0), p["dval"])
                else:
                    if p["eng"] == "pe" and e == "pe":
                        continue
                    need[p["eng"]] = (max(need.get(p["eng"], (0, None))[0], p["seq"]), di)
            for pe_, (sq, _) in need.items():
                if known[e][pe_] < sq:
                    known[e][pe_] = sq
                    o["waits"].append(("c", pe_, sq))
            for ds, dv in needd.items():
                if kdma[e][ds] < dv:
                    kdma[e][ds] = dv
                    o["waits"].append(("d", ds, dv))
        byseq = {e: {} for e in self.ENG}
        for o in ops:
            if not o["dma"]:
                byseq[o["eng"]][o["seq"]] = o
        for o in ops:
            for wt in o["waits"]:
                if wt[0] == "c":
                    byseq[wt[1]][wt[2]]["sig"] = True
        cnt = {e: 0 for e in self.ENG}
        for o in ops:
            if not o["dma"] and o["sig"]:
                cnt[o["eng"]] += 1
                o["cval"] = cnt[o["eng"]]
        nsem = {e: cnt[e] // self.CAP + 1 for e in self.ENG}
        self.csems = {e: [stack.enter_context(nc.semaphore(f"c_{e}_{k}")) for k in range(nsem[e])]
                      for e in self.ENG}
        self.dsems = [stack.enter_context(nc.semaphore(f"d_{k}")) for k in range(self.NDS)]
        self.byseq = byseq

    def emit(self, engname, e):
        for o in self.ops:
            if o["eng"] != engname:
                continue
            for wt in o["waits"]:
                if wt[0] == "c":
                    cv = self.byseq[wt[1]][wt[2]]["cval"]
                    e.wait_ge(self.csems[wt[1]][(cv - 1) // self.CAP], (cv - 1) % self.CAP + 1)
                else:
                    e.wait_ge(self.dsems[wt[1]], wt[2])
            try:
                ins = o["fn"](e)
            except Exception:
                print("EMIT FAIL", engname, "op#", self.ops.index(o), "of", len(self.ops), flush=True)
                raise
            if ins is None:
                continue
            if o["dma"]:
                ins.then_inc(self.dsems[o["dsem"]], 16)
            elif o["sig"]:
                cv = o["cval"]
                ins.then_inc(self.csems[engname][(cv - 1) // self.CAP], 1)


class Arena:
    def __init__(self, nc, name, words):
        self.t = nc.alloc_sbuf_tensor(name, [128, words], F32)
        self.words = words
        self.off = 0

    def mark(self):
        return self.off

    def release(self, m):
        self.off = m

    def alloc(self, shape, dtype):
        n = int(np.prod(shape[1:]))
        w = (n + 1) // 2 if dtype == BF16 else n
        w = (w + 7) // 8 * 8
        assert self.off + w <= self.words, f"SBUF arena overflow {self.off + w} > {self.words}"
        ap = self.t.ap()[:, self.off:self.off + w]
        self.off += w
        if dtype != F32:
            ap = ap.bitcast(dtype)
        ap = ap[0:shape[0], 0:n]
        if len(shape) == 3:
            ap = ap.rearrange("p (a b) -> p a b", a=shape[1])
        elif len(shape) == 4:
            ap = ap.rearrange("p (a b c) -> p a b c", a=shape[1], b=shape[2])
        elif len(shape) == 5:
            ap = ap.rearrange("p (a b c d) -> p a b c d", a=shape[1], b=shape[2], c=shape[3])
        return ap


def build_nc(cfg):
    U = cfg["U"]; NSQ = cfg["NSQ"]; NPG = cfg["NPG"]; NPOOL = cfg["NPOOL"]; DSEQ = cfg["DSEQ"]
    NKB = U // 128
    QW = min(U, 256)
    NQG = U // QW
    NU = 16
    NSL = 4
    NTS = NSQ * DSEQ
    HW = 32
    nc = bass.Bass("TRN2", target_bir_lowering=False)

    def din(name, shape, dt=F32):
        return nc.dram_tensor(name, list(shape), dt, kind="ExternalInput").ap()

    def dout(name, shape, dt=F32):
        return nc.dram_tensor(name, list(shape), dt, kind="ExternalOutput").ap()

    xT = din("xT", [NU, 128, DC, U])
    xn = din("xn", [NSL, U, D])
    xh = din("xh", [NSL, 128, DC, HW])
    bsel = din("bsel", [128, NSL * 3])
    w_in = din("w_in", [D, 3584])
    w_out = din("w_out", [D, D])
    w_pw2 = din("w_pw2", [512, 512])
    vecs = din("vecs", [128, 8 + 4 * CW + 4 * 4 + 2])
    reps = din("reps", [128, D + 128 + 4 * 64])
    ident_in = din("ident", [128, 128])
    tri_in = din("tri", [128, 128])
    smask_in = din("smask", [128, NSQ, 2 * DSEQ])
    xsT = din("xsT", [128, DC, NTS])
    xsn = din("xsn", [NTS, D])
    stT = din("stT", [128, 4, NSQ, CW - 1])
    stn = din("stn", [NSQ, CW - 1, 512])
    cache_k = din("cache_k", [NPOOL * 128, 512])
    cache_v = din("cache_v", [NPOOL * 128, 512])
    ptab = din("ptab", [128, NSQ * NPG], I32)

    y_o = dout("y", [NSL, U, D])
    nk_o = dout("nk", [NSL, U, 512])
    nv_o = dout("nv", [NSL, U, 512])
    ctail_o = dout("ctail", [HW, 512])
    ys_o = dout("ys", [NTS, D])
    nks_o = dout("nks", [NTS, 512])
    nvs_o = dout("nvs", [NTS, 512])
    cs_o = dout("cs", [NSQ, CW - 1, 512])

    w_in_v = w_in.rearrange("(dc p) c -> p dc c", p=128)
    w_out_v = w_out.rearrange("(dc p) c -> p dc c", p=128)
    w_pw2_v = w_pw2.rearrange("(dc p) c -> p dc c", p=128)

    S = Sched()
    A = Arena(nc, "arena", 53000)
    PS = nc.alloc_psum_tensor("psum", [128, 8 * 512], F32).ap()

    def bank(b, n=512, off=0):
        return PS[:, b * 512 + off: b * 512 + off + n]

    misc_rr = [0]

    def misc_bank():
        b = 5 + (misc_rr[0] % 3)
        misc_rr[0] += 1
        return b, f"PSB{b}"

    ident = A.alloc([128, 128], F32)
    identb = A.alloc([128, 128], BF16)
    tri = A.alloc([128, 128], BF16)
    onesb = A.alloc([128, 128], BF16)
    onesf = A.alloc([128, 128], F32)
    vec = A.alloc([128, 8 + 4 * CW + 18], F32)
    sgcol = A.alloc([128, 1], F32)
    rep = A.alloc([128, D + 128 + 256], F32)
    bs = A.alloc([128, NSL * 3], F32)
    lam = A.alloc([128, 4], F32)
    sgr = A.alloc([128, 128], F32)
    zero1 = A.alloc([128, 1], F32)
    g_nrm = vec[:, 0:8]
    dww = vec[:, 8:8 + 4 * CW].rearrange("p (c w) -> p c w", c=4)
    dwb = vec[:, 8 + 4 * CW: 8 + 4 * CW + 4]
    lng = vec[:, 8 + 4 * CW + 4: 8 + 4 * CW + 8]
    lnb = vec[:, 8 + 4 * CW + 8: 8 + 4 * CW + 12]
    bp2 = vec[:, 8 + 4 * CW + 12: 8 + 4 * CW + 16]
    fgr = rep[:, 0:D]

    S.add("sp", lambda e: e.dma_start(out=ident, in_=ident_in), w=["ident"], dma=True)
    S.add("sp", lambda e: e.dma_start(out=vec, in_=vecs), w=["vec"], dma=True)
    S.add("sp", lambda e: e.dma_start(out=rep, in_=reps), w=["rep"], dma=True)
    S.add("sp", lambda e: e.dma_start(out=bs, in_=bsel), w=["bs"], dma=True)
    S.add("pool", lambda e: e.dma_start(out=tri, in_=tri_in), w=["tri"], dma=True)
    S.add("pool", lambda e: e.dma_start(out=identb, in_=ident_in), w=["identb"], dma=True)
    S.add("dve", lambda e: e.memset(onesb, 1.0), w=["onesb"])
    S.add("dve", lambda e: e.memset(onesf, 1.0), w=["onesf"])
    S.add("dve", lambda e: e.memset(zero1, 0.0), w=["zero1"])
    lq = rep[:, D + 128: D + 128 + 256].rearrange("p (a b) -> p a b", a=4)
    lscr = A.alloc([128, 2, 64], F32)
    S.add("dve", lambda e: e.tensor_tensor(out=lscr[:, 0, :], in0=lq[:, 0, :], in1=lq[:, 1, :], op=ALU.mult),
          r=["rep"], w=["lscr0"])
    S.add("dve", lambda e: e.tensor_tensor(out=lscr[:, 1, :], in0=lq[:, 2, :], in1=lq[:, 3, :], op=ALU.mult),
          r=["rep"], w=["lscr1"])
    S.add("dve", lambda e: e.reduce_sum(out=lam[:, 2:3], in_=lscr[:, 0, :], axis=mybir.AxisListType.X),
          r=["lscr0"], w=["lam2"])
    S.add("dve", lambda e: e.reduce_sum(out=lam[:, 3:4], in_=lscr[:, 1, :], axis=mybir.AxisListType.X),
          r=["lscr1"], w=["lam3"])
    S.add("act", lambda e: e.activation(out=lam[:, 2:4], in_=lam[:, 2:4], func=AF.Exp), r=["lam2", "lam3"],
          w=["lam23"])
    S.add("dve", lambda e: e.tensor_tensor(out=lam[:, 0:1], in0=lam[:, 2:3], in1=lam[:, 3:4], op=ALU.subtract),
          r=["lam23"], w=["lam0a"])
    S.add("dve", lambda e: e.tensor_scalar(out=lam[:, 0:1], in0=lam[:, 0:1], scalar1=LAM_INIT, scalar2=None,
                                           op0=ALU.add), r=["lam0a"], w=["lam0"])
    S.add("dve", lambda e: e.tensor_scalar(out=lam[:, 1:2], in0=lam[:, 0:1], scalar1=-1.0, scalar2=None,
                                           op0=ALU.mult), r=["lam0"], w=["lam"])
    S.add("dve", lambda e: e.tensor_scalar(out=sgcol, in0=vec[:, 8 + 4 * CW + 16:8 + 4 * CW + 17], scalar1=1.0 - LAM_INIT,
                                           scalar2=None, op0=ALU.mult), r=["vec"], w=["sgcol"])
    S.add("dve", lambda e: e.tensor_scalar(out=sgr, in0=rep[:, D:D + 128], scalar1=1.0 - LAM_INIT, scalar2=None,
                                           op0=ALU.mult), r=["rep"], w=["sgr"])

    def rstd_from(eng, out, in_, n, rtok, wtok):
        S.add(eng, lambda e: e.tensor_scalar(out=out, in0=in_, scalar1=1.0 / n, scalar2=EPS, op0=ALU.mult,
                                             op1=ALU.add), r=rtok, w=[wtok])
        S.add("act", lambda e: e.activation(out=out, in_=out, func=AF.Ln), r=[wtok], w=[wtok])
        S.add("act", lambda e: e.activation(out=out, in_=out, func=AF.Exp, scale=-0.5), r=[wtok], w=[wtok])

    def silu_mul(src, srctok, dst, dsttok, other, othertok, tmp, tmptok, shape_eng="dve"):
        S.add("act", lambda e: e.activation(out=tmp, in_=src, func=AF.Exp, scale=-1.0), r=[srctok], w=[tmptok])
        S.add(shape_eng, lambda e: e.tensor_scalar(out=tmp, in0=tmp, scalar1=1.0, scalar2=None, op0=ALU.add),
              r=[tmptok], w=[tmptok])
        S.add("dve", lambda e: e.reciprocal(out=tmp, in_=tmp), r=[tmptok], w=[tmptok])
        if other is None:
            S.add(shape_eng, lambda e: e.tensor_tensor(out=dst, in0=tmp, in1=src, op=ALU.mult),
                  r=[tmptok, srctok], w=[dsttok])
        else:
            S.add(shape_eng, lambda e: e.tensor_tensor(out=tmp, in0=tmp, in1=src, op=ALU.mult),
                  r=[tmptok, srctok], w=[tmptok])
            S.add(shape_eng, lambda e: e.tensor_tensor(out=dst, in0=tmp, in1=other, op=ALU.mult),
                  r=[tmptok, othertok], w=[dsttok])

    def load_w(dst, src, tok):
        S.add("pool", lambda e: e.dma_start(out=dst, in_=src), w=[tok], dma=True)

    def prep_x(xf, xftok, xb, xbtok, xsq, xsqtok, ntok):
        for dc in range(DC):
            S.add("pool", lambda e, dc=dc: e.tensor_scalar(out=xb[:, dc, :], in0=xf[:, dc, :],
                                                           scalar1=g_nrm[:, dc:dc + 1], scalar2=None, op0=ALU.mult),
                  r=[xftok, "vec"], w=[f"{xbtok}_{dc}"])
        S.add("dve", lambda e: e.tensor_tensor(out=xsq, in0=xf, in1=xf, op=ALU.mult), r=[xftok], w=[xsqtok])

    def xb_toks(xbtok):
        return [f"{xbtok}_{dc}" for dc in range(DC)]

    def stats_rep(xsq, xsqtok, ntok, out, outtok):
        b, bt = misc_bank()
        ps = bank(b, ntok)
        for dc in range(DC):
            S.add("pe", lambda e, dc=dc: e.matmul(ps, lhsT=onesb, rhs=xsq[:, dc, :], start=(dc == 0),
                                                   stop=(dc == DC - 1)), r=[xsqtok, "onesb"], w=[bt])
        rstd_from("dve", out, ps, D, [bt], outtok)

    def stats_nat(xsq, xsqtok, nblk, out, outtok, blk=128):
        b, bt = misc_bank()
        ps = bank(b, nblk)
        for tb in range(nblk):
            for dc in range(DC):
                S.add("pe", lambda e, dc=dc, tb=tb: e.matmul(ps[0:blk, tb:tb + 1], lhsT=xsq[:, dc, tb * blk:(tb + 1) * blk],
                                                             rhs=onesb[:, 0:1], start=(dc == 0), stop=(dc == DC - 1)),
                      r=[xsqtok, "onesb"], w=[bt])
        rstd_from("dve", out[0:blk, :], ps[0:blk, :], D, [bt], outtok)

    def proj_T(wt, wtok, c0, xb, xbtok, ntok, rrep, rreptok, out, outtok, out_eng="dve"):
        b, bt = misc_bank()
        ps = bank(b, ntok)
        for dc in range(DC):
            S.add("pe", lambda e, dc=dc: e.matmul(ps, lhsT=wt[:, dc, c0:c0 + 128], rhs=xb[:, dc, :],
                                                   start=(dc == 0), stop=(dc == DC - 1)),
                  r=[wtok] + xb_toks(xbtok), w=[bt])
        S.add(out_eng, lambda e: e.tensor_tensor(out=out, in0=ps, in1=rrep, op=ALU.mult), r=[bt, rreptok],
              w=[outtok])

    def row_phase(tag, xb, xbtok, xhb, xhbtok, rrep, rreptok, rreph, rrephtok, nseq, T, HWS,
                  state_ap, mconv_out, mconvtok, wpw2, ctail=None, post=None, scr_f32=None, scr_f32_tok=None,
                  scr_bf=None, scr_bf_tok=None):
        ntok = nseq * T
        m0 = A.mark()
        UT = A.alloc([128, 4, nseq, HWS + T], F32)
        CN = scr_bf[:, 0:4, :]
        wst = [A.alloc([128, DC, 128], BF16) for _ in range(2)]
        tA = A.alloc([128, ntok], F32)
        tB = A.alloc([128, ntok], F32)
        tD = A.alloc([128, ntok], F32)
        tE = A.alloc([128, max(ntok, 512)], F32)
        cb = A.alloc([128, ntok], BF16)
        csq = A.alloc([128, ntok], BF16)
        hA = A.alloc([128, max(HWS, 8)], F32)
        hB = A.alloc([128, max(HWS, 8)], F32)
        Cs = [scr_f32[:, cc, :].rearrange("p (s t) -> p s t", s=nseq) for cc in range(4)]
        def tk(n):
            if n.startswith("CN"):
                return scr_bf_tok
            if n.startswith("C") and n[1:].isdigit():
                return scr_f32_tok
            return f"{tag}_{n}"
        wrr = [0]

        def wload(c0):
            k = wrr[0] % 2
            wrr[0] += 1
            load_w(wst[k], w_in_v[:, :, c0:c0 + 128], tk(f"wst{k}"))
            return wst[k], tk(f"wst{k}")

        mb, mbt = misc_bank()
        sb, sbt = misc_bank()
        for cc in range(4):
            wa, wat = wload(2048 + cc * 128)
            wb, wbt = wload(2560 + cc * 128)
            proj_T(wa, wat, 0, xb, xbtok, ntok, rrep, rreptok, tA, tk("tA"))
            proj_T(wb, wbt, 0, xb, xbtok, ntok, rrep, rreptok, tB, tk("tB"))
            S.add("act", lambda e: e.activation(out=tD, in_=tB, func=AF.Exp, scale=-1.0), r=[tk("tB")], w=[tk("tD")])
            S.add("dve", lambda e: e.tensor_scalar(out=tD, in0=tD, scalar1=1.0, scalar2=None, op0=ALU.add),
                  r=[tk("tD")], w=[tk("tD")])
            S.add("dve", lambda e: e.reciprocal(out=tD, in_=tD), r=[tk("tD")], w=[tk("tD")])
            S.add("dve", lambda e, cc=cc: e.tensor_tensor(out=UT[:, cc, :, HWS:HWS + T],
                                                          in0=tD.rearrange("p (s t) -> p s t", s=nseq),
                                                          in1=tA.rearrange("p (s t) -> p s t", s=nseq), op=ALU.mult),
                  r=[tk("tD"), tk("tA")], w=[tk(f"UT{cc}")])
            if state_ap is not None:
                S.add("sp", lambda e, cc=cc: e.dma_start(out=UT[:, cc, :, 0:HWS], in_=state_ap[:, cc, :, :]),
                      w=[tk(f"UTh{cc}")], dma=True)
            else:
                proj_T(wa, wat, 0, xhb, xhbtok, HWS, rreph, rrephtok, hA[:, 0:HWS], tk("hA"))
                proj_T(wb, wbt, 0, xhb, xhbtok, HWS, rreph, rrephtok, hB[:, 0:HWS], tk("hB"))
                S.add("act", lambda e: e.activation(out=hB[:, 0:HWS], in_=hB[:, 0:HWS], func=AF.Exp, scale=-1.0),
                      r=[tk("hB")], w=[tk("hB")])
                S.add("dve", lambda e: e.tensor_scalar(out=hB[:, 0:HWS], in0=hB[:, 0:HWS], scalar1=1.0, scalar2=None,
                                                       op0=ALU.add), r=[tk("hB")], w=[tk("hB")])
                S.add("dve", lambda e: e.reciprocal(out=hB[:, 0:HWS], in_=hB[:, 0:HWS]), r=[tk("hB")], w=[tk("hB")])
                S.add("dve", lambda e, cc=cc: e.tensor_tensor(out=UT[:, cc, 0, 0:HWS], in0=hB[:, 0:HWS],
                                                              in1=hA[:, 0:HWS], op=ALU.mult),
                      r=[tk("hB"), tk("hA")], w=[tk(f"UTh{cc}")])
            utoks = [tk(f"UT{cc}"), tk(f"UTh{cc}")]
            o0 = HWS - (CW - 1)
            ceng = "dve"
            C = Cs[cc]
            S.add(ceng, lambda e, cc=cc, C=C: e.tensor_scalar(out=C, in0=UT[:, cc, :, o0:o0 + T],
                                                               scalar1=dww[:, cc, 0:1], scalar2=dwb[:, cc:cc + 1],
                                                               op0=ALU.mult, op1=ALU.add),
                  r=utoks + ["vec"], w=[tk(f"C{cc}")])
            for wi in range(1, CW):
                S.add(ceng, lambda e, cc=cc, wi=wi, C=C: e.scalar_tensor_tensor(
                    out=C, in0=UT[:, cc, :, o0 + wi:o0 + wi + T], scalar=dww[:, cc, wi:wi + 1], in1=C,
                    op0=ALU.mult, op1=ALU.add), r=utoks + [tk(f"C{cc}"), "vec"], w=[tk(f"C{cc}")])
        mean_ps = bank(mb, ntok)
        sq_ps = bank(sb, ntok)
        for cc in range(4):
            Cf = Cs[cc].rearrange("p s t -> p (s t)")
            S.add("dve", lambda e, Cf=Cf: e.tensor_copy(out=cb, in_=Cf), r=[tk(f"C{cc}")], w=[tk("cb")])
            S.add("dve", lambda e, Cf=Cf: e.tensor_tensor(out=csq, in0=Cf, in1=Cf, op=ALU.mult), r=[tk(f"C{cc}")],
                  w=[tk("csq")])
            S.add("pe", lambda e, cc=cc: e.matmul(mean_ps, lhsT=onesb, rhs=cb, start=(cc == 0), stop=(cc == 3)),
                  r=[tk("cb"), "onesb"], w=[mbt])
            S.add("pe", lambda e, cc=cc: e.matmul(sq_ps, lhsT=onesb, rhs=csq, start=(cc == 0), stop=(cc == 3)),
                  r=[tk("csq"), "onesb"], w=[sbt])
        S.add("dve", lambda e: e.tensor_scalar(out=tA, in0=mean_ps, scalar1=1.0 / 512, scalar2=None, op0=ALU.mult),
              r=[mbt], w=[tk("tA")])
        S.add("dve", lambda e: e.tensor_tensor(out=tB, in0=tA, in1=tA, op=ALU.mult), r=[tk("tA")], w=[tk("tB")])
        S.add("dve", lambda e: e.scalar_tensor_tensor(out=tB, in0=sq_ps, scalar=1.0 / 512, in1=tB, op0=ALU.mult,
                                                      op1=ALU.subtract), r=[sbt, tk("tB")], w=[tk("tB")])
        S.add("dve", lambda e: e.tensor_scalar(out=tB, in0=tB, scalar1=EPS, scalar2=None, op0=ALU.add),
              r=[tk("tB")], w=[tk("tB")])
        S.add("act", lambda e: e.activation(out=tB, in_=tB, func=AF.Ln), r=[tk("tB")], w=[tk("tB")])
        S.add("act", lambda e: e.activation(out=tB, in_=tB, func=AF.Exp, scale=-0.5), r=[tk("tB")], w=[tk("tB")])
        for cc in range(4):
            Cf = Cs[cc].rearrange("p s t -> p (s t)")
            S.add("dve", lambda e, Cf=Cf: e.tensor_tensor(out=tD, in0=Cf, in1=tA, op=ALU.subtract),
                  r=[tk(f"C{cc}"), tk("tA")], w=[tk("tD")])
            S.add("dve", lambda e: e.tensor_tensor(out=tD, in0=tD, in1=tB, op=ALU.mult), r=[tk("tD"), tk("tB")],
                  w=[tk("tD")])
            S.add("dve", lambda e, cc=cc: e.tensor_scalar(out=tD, in0=tD, scalar1=lng[:, cc:cc + 1],
                                                          scalar2=lnb[:, cc:cc + 1], op0=ALU.mult, op1=ALU.add),
                  r=[tk("tD"), "vec"], w=[tk("tD")])
            silu_mul(tD, tk("tD"), CN[:, cc, :], tk(f"CN{cc}"), None, None, tE[:, 0:ntok], tk("tE"))
        for oc in range(4):
            wg, wgt = wload(3072 + oc * 128)
            proj_T(wg, wgt, 0, xb, xbtok, ntok, rrep, rreptok, tA, tk("tA"))
            b, bt = misc_bank()
            ps = bank(b, ntok)
            for cc in range(4):
                S.add("pe", lambda e, cc=cc, oc=oc, ps=ps: e.matmul(ps, lhsT=wpw2[:, cc, oc * 128:(oc + 1) * 128],
                                                                    rhs=CN[:, cc, :], start=(cc == 0), stop=(cc == 3)),
                      r=["wpw2", tk(f"CN{cc}")], w=[bt])
            S.add("dve", lambda e, oc=oc, ps=ps: e.tensor_scalar(out=tB, in0=ps, scalar1=bp2[:, oc:oc + 1], scalar2=None,
                                                                 op0=ALU.add), r=[bt, "vec"], w=[tk("tB")])
            silu_mul(tA, tk("tA"), mconv_out(oc), f"{mconvtok}{oc}", tB, tk("tB"), tE[:, 0:ntok], tk("tE"))
        if ctail is not None:
            b, bt = misc_bank()
            ps = bank(b, 512)
            for cc in range(4):
                S.add("pe", lambda e, cc=cc, ps=ps: e.transpose(ps[0:HW, cc * 128:(cc + 1) * 128],
                                                                UT[:, cc, 0, HWS + T - HW:HWS + T], ident),
                      r=[tk(f"UT{cc}"), "ident"], w=[bt])
            ctb = tE[0:HW, 0:512]
            S.add("dve", lambda e, ps=ps: e.tensor_copy(out=ctb, in_=ps[0:HW, :]), r=[bt], w=[tk("tE")])
            S.add("sp", lambda e: e.dma_start(out=ctail, in_=ctb), r=[tk("tE")], w=["OUT_ctail"], dma=True)
        if post is not None:
            post(UT, tk, tE)
        A.release(m0)

    def out_phase(tag, otag, nblk, blk, mT, mTtoks, wout, xnat, yout, xr_alias=None, alias_tok=None):
        m0 = A.mark()
        if xr_alias is None:
            xr = [A.alloc([128, D], F32) for _ in range(2)]
        else:
            xr = xr_alias
        yr = [A.alloc([128, D], F32) for _ in range(2)]
        ssq = A.alloc([128, 2], F32)
        tk = lambda n: f"{tag}_{n}"
        for tb in range(nblk):
            k = tb % 2
            al_r = [alias_tok] if (alias_tok and tb > 0) else []
            al_w = [alias_tok] if (alias_tok and tb == 0) else []
            S.add("sp", lambda e, tb=tb, k=k: e.dma_start(out=xr[k][0:blk, :], in_=xnat(tb)), r=al_r,
                  w=[tk(f"xr{k}")] + al_w, dma=True)
            for half in range(2):
                b, bt = misc_bank()
                ps = bank(b, 512)
                for kc in range(8):
                    S.add("pe", lambda e, kc=kc, half=half, tb=tb, ps=ps: e.matmul(
                        ps[0:blk, :], lhsT=mT(kc, tb), rhs=wout[:, kc, half * 512:(half + 1) * 512],
                        start=(kc == 0), stop=(kc == 7)), r=list(mTtoks[kc]) + ["wout"], w=[bt])
                S.add("dve", lambda e, half=half, k=k, ps=ps: e.tensor_tensor(
                    out=yr[k][0:blk, half * 512:(half + 1) * 512], in0=ps[0:blk, :],
                    in1=xr[k][0:blk, half * 512:(half + 1) * 512], op=ALU.add),
                    r=[bt, tk(f"xr{k}")] + ([alias_tok] if alias_tok else []),
                    w=[tk(f"yr{k}")])
            S.add("dve", lambda e, k=k: e.tensor_tensor(out=xr[k][0:blk, :], in0=yr[k][0:blk, :], in1=yr[k][0:blk, :],
                                                        op=ALU.mult),
                  r=[tk(f"yr{k}")] + ([alias_tok] if alias_tok else []), w=[tk(f"xr{k}")])
            S.add("dve", lambda e, k=k: e.reduce_sum(out=ssq[0:blk, k:k + 1], in_=xr[k][0:blk, :],
                                                     axis=mybir.AxisListType.X),
                  r=[tk(f"xr{k}")] + ([alias_tok] if alias_tok else []), w=[tk(f"ssq{k}")])
            rstd_from("dve", ssq[0:blk, k:k + 1], ssq[0:blk, k:k + 1], D, [tk(f"ssq{k}")], tk(f"ssq{k}"))
            S.add("dve", lambda e, k=k: e.scalar_tensor_tensor(out=yr[k][0:blk, :], in0=yr[k][0:blk, :],
                                                               scalar=ssq[0:blk, k:k + 1], in1=fgr[0:blk, :],
                                                               op0=ALU.mult, op1=ALU.mult),
                  r=[tk(f"yr{k}"), tk(f"ssq{k}"), "rep"], w=[tk(f"yr{k}")])
            S.add("sp", lambda e, tb=tb, k=k: e.dma_start(out=yout(tb), in_=yr[k][0:blk, :]), r=[tk(f"yr{k}")],
                  w=[f"OUT_{otag}y{tb}"], dma=True)
        A.release(m0)

    wpw2 = A.alloc([128, 4, 512], BF16)
    load_w(wpw2, w_pw2_v, "wpw2")
    MATT = A.alloc([128, 4, NSL * U], BF16)
    MCONV = A.alloc([128, 4, NSL * U], BF16)
    mP = A.mark()
    WKV = A.alloc([128, DC, 512], BF16)
    WQG = A.alloc([128, DC, 512], BF16)
    KT = A.alloc([128, 2, NU * U], BF16)
    VA = A.alloc([128, NU * NKB, 2, 129], BF16)
    xf = A.alloc([128, DC, U], F32)
    xb = A.alloc([128, DC, U], BF16)
    xsq = A.alloc([128, DC, U], BF16)
    rrep = A.alloc([128, U], F32)
    rnat = A.alloc([128, NKB], F32)
    QT = A.alloc([128, 2, 2, U], BF16)
    PT = [A.alloc([128, 2, QW], BF16) for _ in range(3)]
    sgatt = A.alloc([128, NKB, 256], F32)
    kvo = [A.alloc([128, 512], F32) for _ in range(2)]
    xhf = A.alloc([128, DC, HW], F32)
    xhb = A.alloc([128, DC, HW], BF16)
    xhsq = A.alloc([128, DC, HW], BF16)
    rreph = A.alloc([128, HW], F32)
    ep = [A.alloc([128, 128], F32) for _ in range(4)]
    epl = A.alloc([128, 8], F32)
    gtmp = A.alloc([128, 256], F32)
    gtmp2 = A.alloc([128, 256], F32)
    mP2 = A.mark()
    xsq_f = xsq.rearrange("p a b -> p (a b)").bitcast(F32)
    xr_al = [xsq_f[:, 0:D], xsq_f[:, D:2 * D]] if 4 * U >= 2 * D else None

    S.add("pool", lambda e: e.memset(VA[:, :, :, 128:129], 1.0), w=["VAones"])
    S.add("pool", lambda e: e.memset(QT, 0.0), w=["QT0", "QT1"])

    def attention(i, hl, h):
        nun = 4 * i + 4
        blocks = []
        for n in range(nun):
            for kb in range(NKB):
                diag = (n == nun - 1)
                for qg in range(NQG):
                    qlo = qg * QW
                    q0 = max(qlo, kb * 128) if diag else qlo
                    if q0 >= qlo + QW:
                        continue
                    blocks.append((n, kb, qg, q0, diag))
        npt = [0]
        otoks = {}
        for m in range(2):
            for qs in range(NKB):
                a = m * NKB + qs
                otoks[(m, qs)] = f"OB{2 + a // 3}"

        def oacc(m, qs):
            a = m * NKB + qs
            return bank(2 + a // 3, 129, (a % 3) * 160)

        pend = []
        started = set()

        def emit_pv(item):
            (n, kb, qg, q0, diag, ptk, pt) = item
            g = n * NKB + kb
            for m in range(2):
                for qs in range(q0 // 128, (qg * QW + QW) // 128):
                    bk = 2 + (m * NKB + qs) // 3
                    first = bk not in started
                    started.add(bk)
                    last = diag and (kb == qs)
                    c0 = qs * 128 - qg * QW
                    S.add("pe", lambda e, m=m, qs=qs, c0=c0, g=g, pt=pt, first=first, last=last: e.matmul(
                        oacc(m, qs)[:, 0:129], lhsT=pt[:, m, c0:c0 + 128], rhs=VA[:, g, hl, :], start=first, stop=last,
                        skip_group_check=True),
                        r=[f"PT{ptk}", f"VA{n}_{hl}", "VAones"], w=[otoks[(m, qs)]])

        for (n, kb, qg, q0, diag) in blocks:
            g = n * NKB + kb
            sbk = npt[0] % 2
            ptk = npt[0] % 3
            npt[0] += 1
            st = bank(sbk, 2 * QW).rearrange("p (m q) -> p m q", m=2)
            c0 = q0 - qg * QW
            wq = qg * QW + QW - q0

            def mm(e, st=st, c0=c0, wq=wq, g=g, q0=q0):
                return e.matmul(st[:, :, c0:c0 + wq], lhsT=KT[:, hl, g * 128:(g + 1) * 128],
                                rhs=QT[:, hl, :, q0:q0 + wq], start=True, stop=True)
            S.add("pe", mm, r=[f"KT{n}_{hl}", f"QT{hl}"], w=[f"ST{sbk}"])
            pt = PT[ptk]
            unc = (n >= 4 * i) and not diag
            S.add("act", lambda e, st=st, pt=pt, c0=c0, wq=wq: e.activation(
                out=pt[:, :, c0:c0 + wq], in_=st[:, :, c0:c0 + wq], func=AF.Exp, scale=0.125),
                r=[f"ST{sbk}"], w=[f"PT{ptk}"])
            if unc:
                sel = bs[:, i * 3 + (n - 4 * i): i * 3 + (n - 4 * i) + 1]
                S.add("pool", lambda e, pt=pt, c0=c0, wq=wq, sel=sel: e.tensor_scalar(
                    out=pt[:, :, c0:c0 + wq], in0=pt[:, :, c0:c0 + wq], scalar1=sel, scalar2=None, op0=ALU.mult),
                    r=[f"PT{ptk}", "bs"], w=[f"PT{ptk}"])
            if diag and q0 == kb * 128:
                for m in range(2):
                    S.add("pool", lambda e, pt=pt, c0=c0, m=m: e.tensor_tensor(out=pt[:, m, c0:c0 + 128],
                                                                               in0=pt[:, m, c0:c0 + 128], in1=tri,
                                                                               op=ALU.mult),
                          r=[f"PT{ptk}", "tri"], w=[f"PT{ptk}"])
            pend.append((n, kb, qg, q0, diag, ptk, pt))
            if len(pend) > 1:
                emit_pv(pend.pop(0))
        while pend:
            emit_pv(pend.pop(0))
        allo = sorted(set(otoks.values()))
        for qs in range(NKB):
            o0 = oacc(0, qs)
            o1 = oacc(1, qs)
            par = qs % 2
            e0 = ep[par * 2]
            e1 = ep[par * 2 + 1]
            el = epl[:, par * 4:par * 4 + 4]
            te0, te1, tel = f"ep_e0{par}", f"ep_e1{par}", f"ep_el{par}"
            S.add("dve", lambda e, o0=o0, el=el: e.reciprocal(out=el[:, 0:1], in_=o0[:, 128:129]), r=allo,
                  w=[tel])
            S.add("dve", lambda e, o1=o1, el=el: e.reciprocal(out=el[:, 1:2], in_=o1[:, 128:129]),
                  r=allo + [tel], w=[tel])
            S.add("dve", lambda e, el=el: e.tensor_tensor(out=el[:, 1:2], in0=el[:, 1:2], in1=lam[:, 1:2], op=ALU.mult),
                  r=[tel, "lam"], w=[tel])
            S.add("dve", lambda e, o0=o0, e0=e0, el=el: e.tensor_scalar(out=e0, in0=o0[:, 0:128], scalar1=el[:, 0:1],
                                                                        scalar2=None, op0=ALU.mult),
                  r=allo + [tel], w=[te0])
            S.add("dve", lambda e, o1=o1, e0=e0, el=el: e.scalar_tensor_tensor(out=e0, in0=o1[:, 0:128],
                                                                               scalar=el[:, 1:2], in1=e0,
                                                                               op0=ALU.mult, op1=ALU.add),
                  r=allo + [tel, te0], w=[te0])
            S.add("dve", lambda e, e0=e0, e1=e1: e.tensor_tensor(out=e1, in0=e0, in1=e0, op=ALU.mult), r=[te0], w=[te1])
            S.add("dve", lambda e, e1=e1, el=el: e.reduce_sum(out=el[:, 2:3], in_=e1, axis=mybir.AxisListType.X),
                  r=[te1, tel], w=[tel])
            rstd_from("dve", el[:, 2:3], el[:, 2:3], 128, [tel], tel)
            S.add("dve", lambda e, e0=e0, el=el: e.scalar_tensor_tensor(out=e0, in0=e0, scalar=el[:, 2:3], in1=sgr,
                                                                        op0=ALU.mult, op1=ALU.mult),
                  r=[te0, tel, "sgr"], w=[te0])
            S.add("dve", lambda e, e0=e0, qs=qs: e.tensor_tensor(out=e0, in0=e0, in1=sgatt[:, qs, hl * 128:(hl + 1) * 128],
                                                                 op=ALU.mult), r=[te0, f"sgatt{qs}"], w=[te0])
            b, bt = misc_bank()
            ps = bank(b, 128)
            S.add("pe", lambda e, e0=e0, ps=ps: e.transpose(ps, e0, ident), r=[te0, "ident"], w=[bt])
            S.add("dve", lambda e, ps=ps, qs=qs: e.tensor_copy(out=MATT[:, h, i * U + qs * 128: i * U + (qs + 1) * 128],
                                                               in_=ps), r=[bt], w=[f"MATT{h}_{i}_{qs}"])

    for hp in (range(2) if 'P' in cfg.get('phases', 'PS') else []):
        load_w(WKV[:, :, 0:256], w_in_v[:, :, 512 + hp * 256: 512 + hp * 256 + 256], "WK")
        load_w(WKV[:, :, 256:512], w_in_v[:, :, 1024 + hp * 256: 1024 + hp * 256 + 256], "WV")
        load_w(WQG[:, :, 0:256], w_in_v[:, :, hp * 256: hp * 256 + 256], "WQ")
        load_w(WQG[:, :, 256:512], w_in_v[:, :, 1536 + hp * 256: 1536 + hp * 256 + 256], "WG")
        if hp == 1:
            S.barrier()
            A.release(mP2)
            wout = A.alloc([128, DC, D], BF16)
            load_w(wout, w_out_v, "wout")
        for n in range(NU):
            own = (n % 4 == 3)
            i = n // 4
            S.add("sp", lambda e, n=n: e.dma_start(out=xf, in_=xT[n]), w=["xf"], dma=True)
            prep_x(xf, "xf", xb, "xb", xsq, "xsq", U)
            stats_rep(xsq, "xsq", U, rrep, "rrep")
            stats_nat(xsq, "xsq", NKB, rnat, "rnat")
            for hl in range(2):
                proj_T(WKV, "WK", hl * 128, xb, "xb", U, rrep, "rrep", KT[:, hl, n * U:(n + 1) * U], f"KT{n}_{hl}")
            for tb in range(NKB):
                b, bt = misc_bank()
                ps = bank(b, 512)
                for dc in range(DC):
                    S.add("pe", lambda e, dc=dc, tb=tb, ps=ps: e.matmul(ps[:, 0:256], lhsT=xb[:, dc, tb * 128:(tb + 1) * 128],
                                                                       rhs=WKV[:, dc, 256:512], start=(dc == 0),
                                                                       stop=(dc == DC - 1)),
                          r=xb_toks("xb") + ["WV"], w=[bt])
                g = n * NKB + tb
                S.add("act", lambda e, ps=ps, g=g, tb=tb: e.activation(
                    out=VA[:, g, :, 0:128], in_=ps[:, 0:256].rearrange("p (h e) -> p h e", h=2), func=AF.Copy,
                    scale=rnat[:, tb:tb + 1]), r=[bt, "rnat"], w=[f"VA{n}_0", f"VA{n}_1"])
                if own:
                    k = tb % 2
                    S.add("dve", lambda e, ps=ps, k=k, tb=tb: e.tensor_scalar(out=kvo[k][:, 256:512], in0=ps[:, 0:256],
                                                                              scalar1=rnat[:, tb:tb + 1], scalar2=None,
                                                                              op0=ALU.mult), r=[bt, "rnat"], w=[f"kvoV{k}"])
                    S.add("sp", lambda e, k=k, tb=tb, i=i, hp=hp: e.dma_start(
                        out=nv_o[i, tb * 128:(tb + 1) * 128, hp * 256:(hp + 1) * 256], in_=kvo[k][:, 256:512]),
                        r=[f"kvoV{k}"], w=[f"OUT_nv{hp}_{i}_{tb}"], dma=True)
                    b2, bt2 = misc_bank()
                    ps2 = bank(b2, 512)
                    for dc in range(DC):
                        S.add("pe", lambda e, dc=dc, tb=tb, ps2=ps2: e.matmul(
                            ps2[:, 0:256], lhsT=xb[:, dc, tb * 128:(tb + 1) * 128], rhs=WKV[:, dc, 0:256],
                            start=(dc == 0), stop=(dc == DC - 1)), r=xb_toks("xb") + ["WK"], w=[bt2])
                    for dc in range(DC):
                        S.add("pe", lambda e, dc=dc, tb=tb, ps2=ps2: e.matmul(
                            ps2[:, 256:512], lhsT=xb[:, dc, tb * 128:(tb + 1) * 128], rhs=WQG[:, dc, 256:512],
                            start=(dc == 0), stop=(dc == DC - 1)), r=xb_toks("xb") + ["WG"], w=[bt2])
                    S.add("dve", lambda e, ps2=ps2, k=k, tb=tb: e.tensor_scalar(out=kvo[k][:, 0:256], in0=ps2[:, 0:256],
                                                                                scalar1=rnat[:, tb:tb + 1], scalar2=None,
                                                                                op0=ALU.mult), r=[bt2, "rnat"], w=[f"kvoK{k}"])
                    S.add("sp", lambda e, k=k, tb=tb, i=i, hp=hp: e.dma_start(
                        out=nk_o[i, tb * 128:(tb + 1) * 128, hp * 256:(hp + 1) * 256], in_=kvo[k][:, 0:256]),
                        r=[f"kvoK{k}"], w=[f"OUT_nk{hp}_{i}_{tb}"], dma=True)
                    S.add("dve", lambda e, ps2=ps2, tb=tb: e.tensor_scalar(out=gtmp, in0=ps2[:, 256:512],
                                                                           scalar1=rnat[:, tb:tb + 1], scalar2=None,
                                                                           op0=ALU.mult), r=[bt2, "rnat"], w=["gtmp"])
                    silu_mul(gtmp, "gtmp", sgatt[:, tb, :], f"sgatt{tb}", None, None, gtmp2, "gtmp2")
            if own:
                for hl in range(2):
                    b, bt = misc_bank()
                    psq = bank(b, U)
                    for dc in range(DC):
                        S.add("pe", lambda e, dc=dc, hl=hl, psq=psq: e.matmul(
                            psq, lhsT=WQG[:, dc, hl * 128:(hl + 1) * 128], rhs=xb[:, dc, :], start=(dc == 0),
                            stop=(dc == DC - 1)), r=["WQ"] + xb_toks("xb"), w=[bt])
                    S.add("dve", lambda e, hl=hl, psq=psq: e.tensor_tensor(out=QT[0:64, hl, 0, :], in0=psq[0:64, :],
                                                                          in1=rrep[0:64, :], op=ALU.mult),
                          r=[bt, "rrep"], w=[f"QT{hl}"])
                    S.add("dve", lambda e, hl=hl, psq=psq: e.tensor_tensor(out=QT[64:128, hl, 1, :], in0=psq[64:128, :],
                                                                          in1=rrep[64:128, :], op=ALU.mult),
                          r=[bt, "rrep", f"QT{hl}"], w=[f"QT{hl}"])
                if hp == 0:
                    S.add("sp", lambda e, i=i: e.dma_start(out=xhf, in_=xh[i]), w=["xhf"], dma=True)
                    prep_x(xhf, "xhf", xhb, "xhb", xhsq, "xhsq", HW)
                    stats_rep(xhsq, "xhsq", HW, rreph, "rreph")
                    row_phase("rp", xb, "xb", xhb, "xhb", rrep, "rrep", rreph, "rreph", 1, U, HW, None,
                              lambda oc, i=i: MCONV[:, oc, i * U:(i + 1) * U], f"MCONV{i}_", wpw2,
                              ctail=(ctail_o if i == NSL - 1 else None), scr_f32=xf, scr_f32_tok="xf",
                              scr_bf=xsq, scr_bf_tok="xsq")
                for hl in range(2):
                    attention(i, hl, hp * 2 + hl)
                if hp == 1:
                    def mT(kc, tb, i=i):
                        if kc < 4:
                            return MATT[:, kc, i * U + tb * 128: i * U + (tb + 1) * 128]
                        return MCONV[:, kc - 4, i * U + tb * 128: i * U + (tb + 1) * 128]
                    mtoks = []
                    for kc in range(8):
                        if kc < 4:
                            mtoks.append([f"MATT{kc}_{i}_{qs}" for qs in range(NKB)])
                        else:
                            mtoks.append([f"MCONV{i}_{kc - 4}"])
                    out_phase("op", f"p{i}", NKB, 128, mT, mtoks, wout,
                              lambda tb, i=i: xn[i, tb * 128:(tb + 1) * 128, :],
                              lambda tb, i=i: y_o[i, tb * 128:(tb + 1) * 128, :],
                              xr_alias=xr_al, alias_tok="xsq")

    S.barrier()
    A.release(mP)
    BARR = []

    if 'S' in cfg.get('phases', 'PS'):
        WS = A.alloc([128, DC, 2048], BF16)
        S.add("pool", lambda e: e.dma_start(out=WS[:, :, 0:1024], in_=w_in_v[:, :, 0:1024]), w=["WSa"], dma=True)
        S.add("pool", lambda e: e.dma_start(out=WS[:, :, 1024:2048], in_=w_in_v[:, :, 1024:2048]), w=["WSb"], dma=True)
        wout2 = A.alloc([128, DC, D], BF16)
        S.add("pool", lambda e: e.dma_start(out=wout2, in_=w_out_v), w=["wout"], dma=True)
        sxf = A.alloc([128, DC, NTS], F32)
        sxb = A.alloc([128, DC, NTS], BF16)
        sxsq = A.alloc([128, DC, NTS], BF16)
        srrep = A.alloc([128, NTS], F32)
        srnat = A.alloc([128, 1], F32)
        S.add("sp", lambda e: e.dma_start(out=sxf, in_=xsT), w=["sxf"], dma=True)
        prep_x(sxf, "sxf", sxb, "sxb", sxsq, "sxsq", NTS)
        stats_rep(sxsq, "sxsq", NTS, srrep, "srrep")
        stats_nat(sxsq, "sxsq", 1, srnat, "srnat", blk=NTS)
        skv = A.alloc([128, 1024], F32)
        VSb = A.alloc([128, 512], BF16)
        for which, c0 in (("k", 512), ("v", 1024)):
            b, bt = misc_bank()
            ps = bank(b, 512)
            for dc in range(DC):
                S.add("pe", lambda e, dc=dc, ps=ps, c0=c0: e.matmul(ps[0:NTS, :], lhsT=sxb[:, dc, :], rhs=WS[:, dc, c0:c0 + 512],
                                                                   start=(dc == 0), stop=(dc == DC - 1)),
                      r=xb_toks("sxb") + ["WSa", "WSb"], w=[bt])
            o = skv[0:NTS, 0:512] if which == "k" else skv[0:NTS, 512:1024]
            S.add("dve", lambda e, ps=ps, o=o: e.tensor_scalar(out=o, in0=ps[0:NTS, :], scalar1=srnat[0:NTS, 0:1],
                                                               scalar2=None, op0=ALU.mult), r=[bt, "srnat"], w=[f"skv{which}"])
            dst = nks_o if which == "k" else nvs_o
            S.add("sp", lambda e, o=o, dst=dst: e.dma_start(out=dst, in_=o), r=[f"skv{which}"], w=[f"OUT_s{which}"], dma=True)
            if which == "v":
                S.add("dve", lambda e, o=o: e.tensor_copy(out=VSb[0:NTS, :], in_=o),
                      r=["skvv"], w=["VSb"])
        KTs = A.alloc([128, NH, NTS], BF16)
        Qblk = A.alloc([128, NH, NSQ, 2 * DSEQ], BF16)
        SGT = A.alloc([128, NH, NTS], F32)
        stmp = A.alloc([128, NTS], F32)
        stmp2 = A.alloc([128, NTS], F32)
        stmp3 = A.alloc([128, NTS], F32)
        S.add("pool", lambda e: e.memset(Qblk, 0.0), w=["Qblk"])
        for h in range(NH):
            proj_T(WS, "WSa", 512 + h * 128, sxb, "sxb", NTS, srrep, "srrep", KTs[:, h, :], f"KTs{h}")
            proj_T(WS, "WSa", h * 128, sxb, "sxb", NTS, srrep, "srrep", stmp, "stmp")
            S.add("dve", lambda e, h=h: e.tensor_copy(out=Qblk[0:64, h, :, 0:DSEQ],
                                                      in_=stmp[0:64, :].rearrange("p (s t) -> p s t", s=NSQ)),
                  r=["stmp"], w=["Qblk"])
            S.add("dve", lambda e, h=h: e.tensor_copy(out=Qblk[64:128, h, :, DSEQ:2 * DSEQ],
                                                      in_=stmp[64:128, :].rearrange("p (s t) -> p s t", s=NSQ)),
                  r=["stmp"], w=["Qblk"])
            proj_T(WS, "WSb", 1536 + h * 128, sxb, "sxb", NTS, srrep, "srrep", stmp2, "stmp2")
            silu_mul(stmp2, "stmp2", SGT[:, h, :], f"SGT{h}", None, None, stmp3, "stmp3")
        pts = A.alloc([128, NSQ * NPG], I32)
        ptf = A.alloc([128, NSQ * NPG], F32)
        IDX = A.alloc([128, NSQ * NPG], I32)
        S.add("pool", lambda e: e.dma_start(out=pts, in_=ptab), w=["pts"], dma=True)
        S.add("pool", lambda e: e.tensor_copy(out=ptf, in_=pts), r=["pts"], w=["ptf"])
        S.add("pool", lambda e: e.tensor_scalar(out=ptf, in0=ptf, scalar1=128.0,
                                                scalar2=vec[:, 8 + 4 * CW + 17:8 + 4 * CW + 18], op0=ALU.mult, op1=ALU.add),
              r=["ptf", "vec"], w=["ptf"])
        S.add("pool", lambda e: e.tensor_copy(out=IDX, in_=ptf), r=["ptf"], w=["IDX"])
        smask = A.alloc([128, NSQ, 2 * DSEQ], BF16)
        S.add("pool", lambda e: e.dma_start(out=smask, in_=smask_in), w=["smask"], dma=True)
        GP = 512 // (NH * 2 * DSEQ)
        NBK = 4
        NBV = GP + 4
        KP = [A.alloc([128, 512], BF16) for _ in range(NBK)]
        VP = [A.alloc([128, 512], BF16) for _ in range(NBV)]
        KTP = [A.alloc([128, NH, 128], BF16) for _ in range(2)]
        NBLK = NPG + 1
        PTs = A.alloc([128, NBLK, NH, 2 * DSEQ], BF16)
        OT = A.alloc([128, NSQ, NH, 2 * DSEQ], F32)
        LT = A.alloc([128, NSQ, NH, 2 * DSEQ], F32)
        cnt = [0]
        W16 = 2 * DSEQ
        for s in range(NSQ):
            vbuf = {}
            for pg in range(NBLK):
                stb = (pg // GP) % 2
                stc = (pg % GP) * NH * W16
                if pg < NPG:
                    k = cnt[0] % NBK
                    kv = cnt[0] % NBV
                    k2 = cnt[0] % 2
                    cnt[0] += 1
                    vbuf[pg] = kv
                    idx = s * NPG + pg

                    def ldk(e, k=k, idx=idx):
                        return e.indirect_dma_start(out=KP[k], out_offset=None, in_=cache_k,
                                                    in_offset=bass.IndirectOffsetOnAxis(ap=IDX[:, idx:idx + 1], axis=0))

                    def ldv(e, kv=kv, idx=idx):
                        return e.indirect_dma_start(out=VP[kv], out_offset=None, in_=cache_v,
                                                    in_offset=bass.IndirectOffsetOnAxis(ap=IDX[:, idx:idx + 1], axis=0))
                    S.add("pool", ldk, r=["IDX"], w=[f"KP{k}"], dma=True)
                    S.add("pool", ldv, r=["IDX"], w=[f"VP{kv}"], dma=True)
                    tp = bank(4, 256).bitcast(BF16).rearrange("p (h t) -> p h t", h=NH)
                    for h in range(NH):
                        S.add("pe", lambda e, h=h, k=k, tp=tp: e.transpose(tp[:, h, :], KP[k][:, h * 128:(h + 1) * 128], identb),
                              r=[f"KP{k}", "identb"], w=["PSB4"])
                    S.add("dve", lambda e, k2=k2, tp=tp: e.tensor_copy(out=KTP[k2], in_=tp), r=["PSB4"], w=[f"KTP{k2}"])
                    for h in range(NH):
                        S.add("pe", lambda e, h=h, k2=k2, stb=stb, stc=stc, s=s: e.matmul(
                            bank(stb, W16, stc + h * W16), lhsT=KTP[k2][:, h, :], rhs=Qblk[:, h, s, :],
                            start=True, stop=True), r=[f"KTP{k2}", "Qblk"], w=[f"SST{stb}"])
                else:
                    for h in range(NH):
                        S.add("pe", lambda e, h=h, stb=stb, stc=stc, s=s: e.matmul(
                            bank(stb, W16, stc + h * W16)[0:NTS, :], lhsT=KTs[:, h, :], rhs=Qblk[:, h, s, :],
                            start=True, stop=True), r=[f"KTs{h}", "Qblk"], w=[f"SST{stb}"])
                lastin = (pg % GP == GP - 1) or (pg == NBLK - 1)
                if not lastin:
                    continue
                g0 = (pg // GP) * GP
                ng = pg - g0 + 1
                nfull = ng if pg < NPG else ng - 1
                if nfull > 0:
                    S.add("act", lambda e, stb=stb, g0=g0, nfull=nfull: e.activation(
                        out=PTs[:, g0:g0 + nfull, :, :].rearrange("p a h q -> p (a h q)"),
                        in_=bank(stb, nfull * NH * W16), func=AF.Exp, scale=0.125), r=[f"SST{stb}"], w=["PTs"])
                if pg == NBLK - 1:
                    cl = (pg % GP) * NH * W16
                    S.add("act", lambda e, stb=stb, cl=cl: e.activation(
                        out=PTs[0:NTS, NPG, :, :].rearrange("p h q -> p (h q)"),
                        in_=bank(stb, NH * W16, cl)[0:NTS, :], func=AF.Exp, scale=0.125), r=[f"SST{stb}"], w=["PTs"])
                    for h in range(NH):
                        S.add("dve", lambda e, s=s, h=h: e.tensor_tensor(
                            out=PTs[0:NTS, NPG, h, :], in0=PTs[0:NTS, NPG, h, :], in1=smask[0:NTS, s, :], op=ALU.mult),
                            r=["PTs", "smask"], w=["PTs"])
                for p2 in range(g0, pg + 1):
                    if p2 == NPG:
                        vt, vk, np_ = VSb, "VSb", NTS
                    else:
                        vt, vk, np_ = VP[vbuf[p2]], f"VP{vbuf[p2]}", 128
                    for h in range(NH):
                        S.add("pe", lambda e, h=h, s=s, p2=p2, vt=vt, np_=np_: e.matmul(
                            bank(2, W16, (s % 8) * 64 + h * W16), lhsT=vt[0:np_, h * 128:(h + 1) * 128],
                            rhs=PTs[0:np_, p2, h, :], start=(p2 == 0 and h == 0), stop=(p2 == NBLK - 1),
                            skip_group_check=True),
                            r=[vk, "PTs"], w=["SOT"])
                    S.add("pe", lambda e, s=s, p2=p2, np_=np_: e.matmul(
                        bank(3, NH * W16, (s % 8) * 64), lhsT=onesb[0:np_, :],
                        rhs=PTs[0:np_, p2, :, :].rearrange("p h q -> p (h q)"), start=(p2 == 0), stop=(p2 == NBLK - 1)),
                        r=["PTs", "onesb"], w=["SLT"])
            S.add("dve", lambda e, s=s: e.tensor_copy(out=OT[:, s, :, :].rearrange("p h q -> p (h q)"),
                                                      in_=bank(2, 64, (s % 8) * 64)), r=["SOT"], w=["OT"])
            S.add("dve", lambda e, s=s: e.tensor_copy(out=LT[:, s, :, :].rearrange("p h q -> p (h q)"),
                                                      in_=bank(3, 64, (s % 8) * 64)), r=["SLT"], w=["LT"])
        NQ = NSQ * NH * DSEQ
        so = A.alloc([128, NSQ, NH, DSEQ], F32)
        so2 = A.alloc([128, NSQ, NH, DSEQ], F32)
        S.add("dve", lambda e: e.reciprocal(out=LT, in_=LT), r=["LT"], w=["LT"])
        S.add("dve", lambda e: e.tensor_tensor(out=OT, in0=OT, in1=LT, op=ALU.mult), r=["LT", "OT"], w=["OT"])
        S.add("dve", lambda e: e.scalar_tensor_tensor(out=so, in0=OT[:, :, :, DSEQ:2 * DSEQ], scalar=lam[:, 1:2],
                                                      in1=OT[:, :, :, 0:DSEQ], op0=ALU.mult, op1=ALU.add),
              r=["OT", "lam"], w=["so"])
        S.add("dve", lambda e: e.tensor_tensor(out=so2, in0=so, in1=so, op=ALU.mult), r=["so"], w=["so2"])
        b, bt = misc_bank()
        ps = bank(b, NQ)
        S.add("pe", lambda e, ps=ps: e.matmul(ps, lhsT=onesf, rhs=so2.rearrange("p s h q -> p (s h q)"), start=True,
                                              stop=True), r=["so2", "onesf"], w=[bt])
        rstd_from("dve", so2.rearrange("p s h q -> p (s h q)"), ps, 128, [bt, "so2"], "so2")
        S.add("dve", lambda e: e.tensor_tensor(out=so, in0=so, in1=so2, op=ALU.mult), r=["so", "so2"], w=["so"])
        MATs = A.alloc([128, NH, NTS], BF16)
        for h in range(NH):
            S.add("dve", lambda e, h=h: e.scalar_tensor_tensor(
                out=stmp.rearrange("p (s q) -> p s q", s=NSQ), in0=so[:, :, h, :], scalar=sgcol[:, 0:1],
                in1=SGT[:, h, :].rearrange("p (s q) -> p s q", s=NSQ), op0=ALU.mult, op1=ALU.mult),
                r=["so", "sgcol", f"SGT{h}"], w=["stmp"])
            S.add("dve", lambda e, h=h: e.tensor_copy(out=MATs[:, h, :], in_=stmp), r=["stmp"], w=[f"MATs{h}"])
        MCs = A.alloc([128, 4, NTS], BF16)
        S.add("sp", lambda e: e.dma_start(out=cs_o[:, 0:CW - 1 - DSEQ, :], in_=stn[:, DSEQ:CW - 1, :]),
              w=["OUT_cs_a"], dma=True)

        def post(UTs, tk, tE):
            ucm = A.alloc([128, 4, NTS], F32)
            unat = A.alloc([128, 512], F32)
            b, bt = misc_bank()
            psu = bank(b, 512)
            for cc in range(4):
                S.add("dve", lambda e, cc=cc: e.tensor_copy(out=ucm[:, cc, :].rearrange("p (s t) -> p s t", s=NSQ),
                                                            in_=UTs[:, cc, :, CW - 1:CW - 1 + DSEQ]),
                      r=[tk(f"UT{cc}")], w=[f"ucm{cc}"])
                S.add("pe", lambda e, cc=cc: e.transpose(psu[0:NTS, cc * 128:(cc + 1) * 128], ucm[:, cc, :], ident),
                      r=[f"ucm{cc}", "ident"], w=[bt])
            S.add("dve", lambda e: e.tensor_copy(out=unat[0:NTS, :], in_=psu[0:NTS, :]), r=[bt], w=["unat"])
            for s in range(NSQ):
                S.add("sp", lambda e, s=s: e.dma_start(out=cs_o[s, CW - 1 - DSEQ:CW - 1, :],
                                                       in_=unat[s * DSEQ:(s + 1) * DSEQ, :]),
                      r=["unat"], w=[f"OUT_cs_b{s}"], dma=True)

        row_phase("srp", sxb, "sxb", None, None, srrep, "srrep", None, None, NSQ, DSEQ, CW - 1, stT,
                  lambda oc: MCs[:, oc, :], "MCs", wpw2, post=post, scr_f32=sxf, scr_f32_tok="sxf",
                  scr_bf=sxsq, scr_bf_tok="sxsq")
        S.barrier()

        def mTs(kc, tb):
            return MATs[:, kc, :] if kc < 4 else MCs[:, kc - 4, :]
        out_phase("ops", "s", 1, NTS, mTs, [[f"MATs{h}"] for h in range(4)] + [[f"MCs{oc}"] for oc in range(4)], wout2,
                  lambda tb: xsn, lambda tb: ys_o)


    outs = [t for t in S.last_w.keys() if t.startswith("OUT_")]
    S.add("sp", lambda e: None, r=outs, w=["END"])

    from contextlib import ExitStack
    with ExitStack() as stack:
        S.finalize(nc, stack)
        with nc.Block() as block:
            @block.tensor
            def _(e):
                S.emit("pe", e)

            @block.scalar
            def _(e):
                S.emit("act", e)

            @block.vector
            def _(e):
                S.emit("dve", e)

            @block.gpsimd
            def _(e):
                S.emit("pool", e)

            @block.sync
            def _(e):
                S.emit("sp", e)
    return nc


def chunk_of(j, i):
    return [j, 7 - j, 8 + j, 15 - j][i]


def unit_perm(j):
    perm = []
    for i in range(4):
        c = chunk_of(j, i)
        grp = [u for u in range(4 * i, 4 * i + 4) if u != c]
        perm += grp + [c]
    return perm


def run(cfg, x_prompt, x_sample, cache_k, cache_v, state_conv, page_table, norm_g, w_in, lambda_q1, lambda_k1,
        lambda_q2, lambda_k2, subln_g, dw_w, dw_b, conv_ln_g, conv_ln_b, w_pw2, b_pw2, w_out, final_norm_g):
    U = cfg["U"]; NSQ = cfg["NSQ"]; NPG = cfg["NPG"]; NPOOL = cfg["NPOOL"]; DSEQ = cfg["DSEQ"]
    NTS = NSQ * DSEQ
    HW = 32
    f = lambda a: np.ascontiguousarray(np.asarray(a, dtype=np.float32))
    x_prompt = f(x_prompt); x_sample = f(x_sample)
    B, SEQ, _ = x_prompt.shape
    assert SEQ == 16 * U
    nc = build_nc(cfg)
    vecs = np.zeros((128, 8 + 4 * CW + 18), np.float32)
    vecs[:, 0:8] = f(norm_g)[0].reshape(8, 128).T
    dw = f(dw_w)[0]
    vecs[:, 8:8 + 4 * CW] = dw.T.reshape(4, 128, CW).transpose(1, 0, 2).reshape(128, 4 * CW)
    o = 8 + 4 * CW
    vecs[:, o:o + 4] = f(dw_b)[0].reshape(4, 128).T
    vecs[:, o + 4:o + 8] = f(conv_ln_g)[0].reshape(4, 128).T
    vecs[:, o + 8:o + 12] = f(conv_ln_b)[0].reshape(4, 128).T
    vecs[:, o + 12:o + 16] = f(b_pw2)[0].reshape(4, 128).T
    vecs[:, o + 16] = f(subln_g)[0]
    vecs[:, o + 17] = np.arange(128, dtype=np.float32)
    reps = np.zeros((128, D + 128 + 256), np.float32)
    reps[:, 0:D] = f(final_norm_g)[None, :]
    reps[:, D:D + 128] = f(subln_g)[0][None, :]
    reps[:, D + 128:D + 128 + 256] = np.concatenate([f(lambda_q1)[0], f(lambda_k1)[0], f(lambda_q2)[0],
                                                     f(lambda_k2)[0]])[None, :]
    ident = np.eye(128, dtype=np.float32)
    tri = np.triu(np.ones((128, 128), np.float32))
    smask = np.zeros((128, NSQ, 2 * DSEQ), np.float32)
    for t in range(NTS):
        s, kk = divmod(t, DSEQ)
        for q in range(DSEQ):
            if kk <= q:
                smask[t, s, q] = 1.0
                smask[t, s, DSEQ + q] = 1.0
    ck = f(cache_k)[0].reshape(NPOOL * 128, 512)
    cv = f(cache_v)[0].reshape(NPOOL * 128, 512)
    w_in0 = f(w_in)[0]; w_out0 = f(w_out)[0]; w_pw20 = f(w_pw2)[0]
    st = f(state_conv)[0]
    pt = np.asarray(page_table, dtype=np.int32)
    in_maps = []
    for c in range(8):
        b, j = divmod(c, 4)
        perm = unit_perm(j)
        xb_ = x_prompt[b].reshape(16, U, 8, 128)
        xT = np.ascontiguousarray(xb_[perm].transpose(0, 3, 2, 1))
        xn = np.stack([x_prompt[b, chunk_of(j, i) * U:(chunk_of(j, i) + 1) * U] for i in range(4)])
        xh = np.zeros((4, 128, 8, HW), np.float32)
        bsel = np.ones((128, 12), np.float32)
        for i in range(4):
            cpos = chunk_of(j, i)
            if cpos > 0:
                xh[i] = x_prompt[b, cpos * U - HW:cpos * U].reshape(HW, 8, 128).transpose(2, 1, 0)
            for t in range(3):
                if perm[4 * i + t] > cpos:
                    bsel[:, i * 3 + t] = 0.0
        xs = x_sample[c * NSQ:(c + 1) * NSQ].reshape(NTS, D)
        xsT = np.ascontiguousarray(xs.reshape(NTS, 8, 128).transpose(2, 1, 0))
        stc = st[c * NSQ:(c + 1) * NSQ]
        stT = np.ascontiguousarray(stc.reshape(NSQ, CW - 1, 4, 128).transpose(3, 2, 0, 1))
        in_maps.append(dict(
            xT=xT, xn=np.ascontiguousarray(xn), xh=xh, bsel=bsel, w_in=w_in0, w_out=w_out0, w_pw2=w_pw20, vecs=vecs,
            reps=reps, ident=ident, tri=tri, smask=smask, xsT=xsT, xsn=np.ascontiguousarray(xs), stT=stT,
            stn=np.ascontiguousarray(stc), cache_k=ck, cache_v=cv,
            ptab=np.ascontiguousarray(np.tile(pt[c * NSQ:(c + 1) * NSQ].reshape(1, NSQ * NPG), (128, 1)))))
    res = run_bass_kernel_spmd(nc, in_maps, core_ids=list(range(8)))
    R = res.results
    DB = 8 * NSQ
    y_prompt = np.zeros((B, SEQ, D), np.float32)
    nk = np.zeros((1, B, SEQ, NH, 128), np.float32)
    nv = np.zeros((1, B, SEQ, NH, 128), np.float32)
    ncp = np.zeros((1, B, CW - 1, 512), np.float32)
    y_sample = np.zeros((DB, DSEQ, D), np.float32)
    nks = np.zeros((1, DB, DSEQ, NH, 128), np.float32)
    nvs = np.zeros((1, DB, DSEQ, NH, 128), np.float32)
    ncs = np.zeros((1, DB, CW - 1, 512), np.float32)
    for c in range(8):
        b, j = divmod(c, 4)
        for i in range(4):
            cp = chunk_of(j, i)
            y_prompt[b, cp * U:(cp + 1) * U] = R[c]["y"][i]
            nk[0, b, cp * U:(cp + 1) * U] = R[c]["nk"][i].reshape(U, NH, 128)
            nv[0, b, cp * U:(cp + 1) * U] = R[c]["nv"][i].reshape(U, NH, 128)
            if cp == 15:
                ncp[0, b] = R[c]["ctail"][HW - (CW - 1):HW]
        y_sample[c * NSQ:(c + 1) * NSQ] = R[c]["ys"].reshape(NSQ, DSEQ, D)
        nks[0, c * NSQ:(c + 1) * NSQ] = R[c]["nks"].reshape(NSQ, DSEQ, NH, 128)
        nvs[0, c * NSQ:(c + 1) * NSQ] = R[c]["nvs"].reshape(NSQ, DSEQ, NH, 128)
        ncs[0, c * NSQ:(c + 1) * NSQ] = R[c]["cs"]
    return (y_prompt, y_sample, nk, nv, ncp, nks, nvs, ncs)


def kernel(**inputs):
    return run(CFG_FULL, **inputs)
```

```python
import math
import numpy as np
import concourse.bass as bass
import concourse.mybir as mybir
from concourse.bass_utils import run_bass_kernel_spmd

F32 = mybir.dt.float32
BF16 = mybir.dt.bfloat16
I32 = mybir.dt.int32
ALU = mybir.AluOpType
AF = mybir.ActivationFunctionType

D = 1024
DC = 8
NH = 4
CW = 31
EPS = 1e-5
NEG = -30000.0
LAM_INIT = 0.8 - 0.6 * math.exp(-0.3 * 0)

CFG_FULL = dict(U=512, NSQ=16, NPG=16, NPOOL=2560, DSEQ=8)


class Sched:
    ENG = ("pe", "act", "dve", "pool", "sp")
    CAP = 20000
    NDS = 80

    def __init__(self):
        self.ops = []
        self.last_w = {}
        self.readers = {}
        self.bars = []

    EXCL = ("PSB", "ST", "OB", "SST", "SOT", "SLT")

    def add(self, eng, fn, r=(), w=(), dma=False):
        r = list(r)
        w = list(w)
        for t in list(r):
            if t.startswith(self.EXCL):
                r.remove(t)
                if t not in w:
                    w.append(t)
        deps = set()
        for t in r:
            if t in self.last_w:
                deps.add(self.last_w[t])
        for t in w:
            if t in self.last_w:
                deps.add(self.last_w[t])
            for x in self.readers.get(t, ()):
                deps.add(x)
        idx = len(self.ops)
        deps.discard(idx)
        self.ops.append(dict(eng=eng, fn=fn, deps=deps, dma=dma, sig=False, waits=[]))
        for t in r:
            self.readers.setdefault(t, []).append(idx)
        for t in w:
            self.last_w[t] = idx
            self.readers[t] = []
        return idx

    def barrier(self):
        self.bars.append(len(self.ops))

    def finalize(self, nc, stack):
        import os
        mx = int(os.environ.get("KMAXOPS", "0"))
        if mx:
            self.ops = self.ops[:mx]
            self.bars = [b for b in self.bars if b <= mx]
            self.bars.append(len(self.ops))
            self.ops.append(dict(eng="sp", fn=lambda e: None, deps=set(), dma=False, sig=False, waits=[]))
        ops = self.ops
        seq = {e: 0 for e in self.ENG}
        ndma = {e: 0 for e in self.ENG}
        half = self.NDS // 2
        for o in ops:
            if o["dma"]:
                base = 0 if o["eng"] == "pool" else half
                k = ndma[o["eng"]]
                o["dsem"] = base + k % half
                o["dval"] = 16 * (k // half + 1)
                ndma[o["eng"]] += 1
                o["sig"] = True
            else:
                seq[o["eng"]] += 1
                o["seq"] = seq[o["eng"]]
        known = {e: {x: 0 for x in self.ENG} for e in self.ENG}
        kdma = {e: [0] * self.NDS for e in self.ENG}
        snaps = []
        for bi in self.bars:
            sc = {x: 0 for x in self.ENG}
            sd = {}
            for p in ops[:bi]:
                if p["dma"]:
                    sd[p["dsem"]] = max(sd.get(p["dsem"], 0), p["dval"])
                else:
                    sc[p["eng"]] = max(sc[p["eng"]], p["seq"])
            snaps.append((bi, sc, sd))
        for oi, o in enumerate(ops):
            e = o["eng"]
            need = {}
            needd = {}
            for (bi, sc, sd) in snaps:
                if oi >= bi:
                    for x, v in sc.items():
                        if v > 0:
                            need[x] = (max(need.get(x, (0, None))[0], v), None)
                    for x, v in sd.items():
                        needd[x] = max(needd.get(x, 0), v)
            if o["dma"] and o["dval"] > 16:
                needd[o["dsem"]] = max(needd.get(o["dsem"], 0), o["dval"] - 16)
            for di in o["deps"]:
                p = ops[di]
                if p["dma"]:
                    needd[p["dsem"]] = max(needd.get(p["dsem"], 0), p["dval"])
                else:
                    if p["eng"] == "pe" and e == "pe":
                        continue
                    need[p["eng"]] = (max(need.get(p["eng"], (0, None))[0], p["seq"]), di)
            for pe_, (sq, _) in need.items():
                if known[e][pe_] < sq:
                    known[e][pe_] = sq
                    o["waits"].append(("c", pe_, sq))
            for ds, dv in needd.items():
                if kdma[e][ds] < dv:
                    kdma[e][ds] = dv
                    o["waits"].append(("d", ds, dv))
        byseq = {e: {} for e in self.ENG}
        for o in ops:
            if not o["dma"]:
                byseq[o["eng"]][o["seq"]] = o
        for o in ops:
            for wt in o["waits"]:
                if wt[0] == "c":
                    byseq[wt[1]][wt[2]]["sig"] = True
        cnt = {e: 0 for e in self.ENG}
        for o in ops:
            if not o["dma"] and o["sig"]:
                cnt[o["eng"]] += 1
                o["cval"] = cnt[o["eng"]]
        nsem = {e: cnt[e] // self.CAP + 1 for e in self.ENG}
        self.csems = {e: [stack.enter_context(nc.semaphore(f"c_{e}_{k}")) for k in range(nsem[e])]
                      for e in self.ENG}
        self.dsems = [stack.enter_context(nc.semaphore(f"d_{k}")) for k in range(self.NDS)]
        self.byseq = byseq

    def emit(self, engname, e):
        for o in self.ops:
            if o["eng"] != engname:
                continue
            for wt in o["waits"]:
                if wt[0] == "c":
                    cv = self.byseq[wt[1]][wt[2]]["cval"]
                    e.wait_ge(self.csems[wt[1]][(cv - 1) // self.CAP], (cv - 1) % self.CAP + 1)
                else:
                    e.wait_ge(self.dsems[wt[1]], wt[2])
            try:
                ins = o["fn"](e)
            except Exception:
                print("EMIT FAIL", engname, "op#", self.ops.index(o), "of", len(self.ops), flush=True)
                raise
            if ins is None:
                continue
            if o["dma"]:
                ins.then_inc(self.dsems[o["dsem"]], 16)
            elif o["sig"]:
                cv = o["cval"]
                ins.then_inc(self.csems[engname][(cv - 1) // self.CAP], 1)


class Arena:
    def __init__(self, nc, name, words):
        self.t = nc.alloc_sbuf_tensor(name, [128, words], F32)
        self.words = words
        self.off = 0

    def mark(self):
        return self.off

    def release(self, m):
        self.off = m

    def alloc(self, shape, dtype):
        n = int(np.prod(shape[1:]))
        w = (n + 1) // 2 if dtype == BF16 else n
        w = (w + 7) // 8 * 8
        assert self.off + w <= self.words, f"SBUF arena overflow {self.off + w} > {self.words}"
        ap = self.t.ap()[:, self.off:self.off + w]
        self.off += w
        if dtype != F32:
            ap = ap.bitcast(dtype)
        ap = ap[0:shape[0], 0:n]
        if len(shape) == 3:
            ap = ap.rearrange("p (a b) -> p a b", a=shape[1])
        elif len(shape) == 4:
            ap = ap.rearrange("p (a b c) -> p a b c", a=shape[1], b=shape[2])
        elif len(shape) == 5:
            ap = ap.rearrange("p (a b c d) -> p a b c d", a=shape[1], b=shape[2], c=shape[3])
        return ap


def build_nc(cfg):
    U = cfg["U"]; NSQ = cfg["NSQ"]; NPG = cfg["NPG"]; NPOOL = cfg["NPOOL"]; DSEQ = cfg["DSEQ"]
    NKB = U // 128
    QW = min(U, 256)
    NQG = U // QW
    NU = 16
    NSL = 4
    NTS = NSQ * DSEQ
    HW = 32
    nc = bass.Bass("TRN2", target_bir_lowering=False)

    def din(name, shape, dt=F32):
        return nc.dram_tensor(name, list(shape), dt, kind="ExternalInput").ap()

    def dout(name, shape, dt=F32):
        return nc.dram_tensor(name, list(shape), dt, kind="ExternalOutput").ap()

    xT = din("xT", [NU, 128, DC, U])
    xn = din("xn", [NSL, U, D])
    xh = din("xh", [NSL, 128, DC, HW])
    bsel = din("bsel", [128, NSL * 3])
    w_in = din("w_in", [D, 3584])
    w_out = din("w_out", [D, D])
    w_pw2 = din("w_pw2", [512, 512])
    vecs = din("vecs", [128, 8 + 4 * CW + 4 * 4 + 2])
    reps = din("reps", [128, D + 128 + 4 * 64])
    ident_in = din("ident", [128, 128])
    tri_in = din("tri", [128, 128])
    smask_in = din("smask", [128, NSQ, 2 * DSEQ])
    xsT = din("xsT", [128, DC, NTS])
    xsn = din("xsn", [NTS, D])
    stT = din("stT", [128, 4, NSQ, CW - 1])
    stn = din("stn", [NSQ, CW - 1, 512])
    cache_k = din("cache_k", [NPOOL * 128, 512])
    cache_v = din("cache_v", [NPOOL * 128, 512])
    ptab = din("ptab", [128, NSQ * NPG], I32)

    y_o = dout("y", [NSL, U, D])
    nk_o = dout("nk", [NSL, U, 512])
    nv_o = dout("nv", [NSL, U, 512])
    ctail_o = dout("ctail", [HW, 512])
    ys_o = dout("ys", [NTS, D])
    nks_o = dout("nks", [NTS, 512])
    nvs_o = dout("nvs", [NTS, 512])
    cs_o = dout("cs", [NSQ, CW - 1, 512])

    w_in_v = w_in.rearrange("(dc p) c -> p dc c", p=128)
    w_out_v = w_out.rearrange("(dc p) c -> p dc c", p=128)
    w_pw2_v = w_pw2.rearrange("(dc p) c -> p dc c", p=128)

    S = Sched()
    A = Arena(nc, "arena", 53000)
    PS = nc.alloc_psum_tensor("psum", [128, 8 * 512], F32).ap()

    def bank(b, n=512, off=0):
        return PS[:, b * 512 + off: b * 512 + off + n]

    misc_rr = [0]

    def misc_bank():
        b = 5 + (misc_rr[0] % 3)
        misc_rr[0] += 1
        return b, f"PSB{b}"

    ident = A.alloc([128, 128], F32)
    identb = A.alloc([128, 128], BF16)
    tri = A.alloc([128, 128], BF16)
    onesb = A.alloc([128, 128], BF16)
    onesf = A.alloc([128, 128], F32)
    vec = A.alloc([128, 8 + 4 * CW + 18], F32)
    sgcol = A.alloc([128, 1], F32)
    rep = A.alloc([128, D + 128 + 256], F32)
    bs = A.alloc([128, NSL * 3], F32)
    lam = A.alloc([128, 4], F32)
    sgr = A.alloc([128, 128], F32)
    zero1 = A.alloc([128, 1], F32)
    g_nrm = vec[:, 0:8]
    dww = vec[:, 8:8 + 4 * CW].rearrange("p (c w) -> p c w", c=4)
    dwb = vec[:, 8 + 4 * CW: 8 + 4 * CW + 4]
    lng = vec[:, 8 + 4 * CW + 4: 8 + 4 * CW + 8]
    lnb = vec[:, 8 + 4 * CW + 8: 8 + 4 * CW + 12]
    bp2 = vec[:, 8 + 4 * CW + 12: 8 + 4 * CW + 16]
    fgr = rep[:, 0:D]

    S.add("sp", lambda e: e.dma_start(out=ident, in_=ident_in), w=["ident"], dma=True)
    S.add("sp", lambda e: e.dma_start(out=vec, in_=vecs), w=["vec"], dma=True)
    S.add("sp", lambda e: e.dma_start(out=rep, in_=reps), w=["rep"], dma=True)
    S.add("sp", lambda e: e.dma_start(out=bs, in_=bsel), w=["bs"], dma=True)
    S.add("pool", lambda e: e.dma_start(out=tri, in_=tri_in), w=["tri"], dma=True)
    S.add("pool", lambda e: e.dma_start(out=identb, in_=ident_in), w=["identb"], dma=True)
    S.add("dve", lambda e: e.memset(onesb, 1.0), w=["onesb"])
    S.add("dve", lambda e: e.memset(onesf, 1.0), w=["onesf"])
    S.add("dve", lambda e: e.memset(zero1, 0.0), w=["zero1"])
    lq = rep[:, D + 128: D + 128 + 256].rearrange("p (a b) -> p a b", a=4)
    lscr = A.alloc([128, 2, 64], F32)
    S.add("dve", lambda e: e.tensor_tensor(out=lscr[:, 0, :], in0=lq[:, 0, :], in1=lq[:, 1, :], op=ALU.mult),
          r=["rep"], w=["lscr0"])
    S.add("dve", lambda e: e.tensor_tensor(out=lscr[:, 1, :], in0=lq[:, 2, :], in1=lq[:, 3, :], op=ALU.mult),
          r=["rep"], w=["lscr1"])
    S.add("dve", lambda e: e.reduce_sum(out=lam[:, 2:3], in_=lscr[:, 0, :], axis=mybir.AxisListType.X),
          r=["lscr0"], w=["lam2"])
    S.add("dve", lambda e: e.reduce_sum(out=lam[:, 3:4], in_=lscr[:, 1, :], axis=mybir.AxisListType.X),
          r=["lscr1"], w=["lam3"])
    S.add("act", lambda e: e.activation(out=lam[:, 2:4], in_=lam[:, 2:4], func=AF.Exp), r=["lam2", "lam3"],
          w=["lam23"])
    S.add("dve", lambda e: e.tensor_tensor(out=lam[:, 0:1], in0=lam[:, 2:3], in1=lam[:, 3:4], op=ALU.subtract),
          r=["lam23"], w=["lam0a"])
    S.add("dve", lambda e: e.tensor_scalar(out=lam[:, 0:1], in0=lam[:, 0:1], scalar1=LAM_INIT, scalar2=None,
                                           op0=ALU.add), r=["lam0a"], w=["lam0"])
    S.add("dve", lambda e: e.tensor_scalar(out=lam[:, 1:2], in0=lam[:, 0:1], scalar1=-1.0, scalar2=None,
                                           op0=ALU.mult), r=["lam0"], w=["lam"])
    S.add("dve", lambda e: e.tensor_scalar(out=sgcol, in0=vec[:, 8 + 4 * CW + 16:8 + 4 * CW + 17], scalar1=1.0 - LAM_INIT,
                                           scalar2=None, op0=ALU.mult), r=["vec"], w=["sgcol"])
    S.add("dve", lambda e: e.tensor_scalar(out=sgr, in0=rep[:, D:D + 128], scalar1=1.0 - LAM_INIT, scalar2=None,
                                           op0=ALU.mult), r=["rep"], w=["sgr"])

    def rstd_from(eng, out, in_, n, rtok, wtok):
        S.add(eng, lambda e: e.tensor_scalar(out=out, in0=in_, scalar1=1.0 / n, scalar2=EPS, op0=ALU.mult,
                                             op1=ALU.add), r=rtok, w=[wtok])
        S.add("act", lambda e: e.activation(out=out, in_=out, func=AF.Ln), r=[wtok], w=[wtok])
        S.add("act", lambda e: e.activation(out=out, in_=out, func=AF.Exp, scale=-0.5), r=[wtok], w=[wtok])

    def silu_mul(src, srctok, dst, dsttok, other, othertok, tmp, tmptok, shape_eng="dve"):
        S.add("act", lambda e: e.activation(out=tmp, in_=src, func=AF.Exp, scale=-1.0), r=[srctok], w=[tmptok])
        S.add(shape_eng, lambda e: e.tensor_scalar(out=tmp, in0=tmp, scalar1=1.0, scalar2=None, op0=ALU.add),
              r=[tmptok], w=[tmptok])
        S.add("dve", lambda e: e.reciprocal(out=tmp, in_=tmp), r=[tmptok], w=[tmptok])
        if other is None:
            S.add(shape_eng, lambda e: e.tensor_tensor(out=dst, in0=tmp, in1=src, op=ALU.mult),
                  r=[tmptok, srctok], w=[dsttok])
        else:
            S.add(shape_eng, lambda e: e.tensor_tensor(out=tmp, in0=tmp, in1=src, op=ALU.mult),
                  r=[tmptok, srctok], w=[tmptok])
            S.add(shape_eng, lambda e: e.tensor_tensor(out=dst, in0=tmp, in1=other, op=ALU.mult),
                  r=[tmptok, othertok], w=[dsttok])

    def load_w(dst, src, tok):
        S.add("pool", lambda e: e.dma_start(out=dst, in_=src), w=[tok], dma=True)

    def prep_x(xf, xftok, xb, xbtok, xsq, xsqtok, ntok):
        for dc in range(DC):
            S.add("act", lambda e, dc=dc: e.activation(out=xb[:, dc, :], in_=xf[:, dc, :], func=AF.Copy,
                                                       scale=g_nrm[:, dc:dc + 1]),
                  r=[xftok, "vec"], w=[f"{xbtok}_{dc}"])
        S.add("dve", lambda e: e.tensor_tensor(out=xsq, in0=xf, in1=xf, op=ALU.mult), r=[xftok], w=[xsqtok])

    def xb_toks(xbtok):
        return [f"{xbtok}_{dc}" for dc in range(DC)]

    def stats_rep(xsq, xsqtok, ntok, out, outtok):
        b, bt = misc_bank()
        ps = bank(b, ntok)
        for dc in range(DC):
            S.add("pe", lambda e, dc=dc: e.matmul(ps, lhsT=onesb, rhs=xsq[:, dc, :], start=(dc == 0),
                                                   stop=(dc == DC - 1)), r=[xsqtok, "onesb"], w=[bt])
        rstd_from("dve", out, ps, D, [bt], outtok)

    def stats_nat(xsq, xsqtok, nblk, out, outtok, blk=128):
        b, bt = misc_bank()
        ps = bank(b, nblk)
        for tb in range(nblk):
            for dc in range(DC):
                S.add("pe", lambda e, dc=dc, tb=tb: e.matmul(ps[0:blk, tb:tb + 1], lhsT=xsq[:, dc, tb * blk:(tb + 1) * blk],
                                                             rhs=onesb[:, 0:1], start=(dc == 0), stop=(dc == DC - 1)),
                      r=[xsqtok, "onesb"], w=[bt])
        rstd_from("dve", out[0:blk, :], ps[0:blk, :], D, [bt], outtok)

    def proj_T(wt, wtok, c0, xb, xbtok, ntok, rrep, rreptok, out, outtok, out_eng="dve"):
        b, bt = misc_bank()
        ps = bank(b, ntok)
        for dc in range(DC):
            S.add("pe", lambda e, dc=dc: e.matmul(ps, lhsT=wt[:, dc, c0:c0 + 128], rhs=xb[:, dc, :],
                                                   start=(dc == 0), stop=(dc == DC - 1)),
                  r=[wtok] + xb_toks(xbtok), w=[bt])
        S.add(out_eng, lambda e: e.tensor_tensor(out=out, in0=ps, in1=rrep, op=ALU.mult), r=[bt, rreptok],
              w=[outtok])

    def row_phase(tag, xb, xbtok, xhb, xhbtok, rrep, rreptok, rreph, rrephtok, nseq, T, HWS,
                  state_ap, mconv_out, mconvtok, wpw2, ctail=None, post=None, scr_f32=None, scr_f32_tok=None,
                  scr_bf=None, scr_bf_tok=None):
        ntok = nseq * T
        m0 = A.mark()
        UT = A.alloc([128, 4, nseq, HWS + T], F32)
        CN = scr_bf[:, 0:4, :]
        wst = [A.alloc([128, DC, 128], BF16) for _ in range(2)]
        tA = A.alloc([128, ntok], F32)
        tB = A.alloc([128, ntok], F32)
        tD = A.alloc([128, ntok], F32)
        tE = A.alloc([128, max(ntok, 512)], F32)
        cb = A.alloc([128, ntok], BF16)
        csq = A.alloc([128, ntok], BF16)
        hA = A.alloc([128, max(HWS, 8)], F32)
        hB = A.alloc([128, max(HWS, 8)], F32)
        Cs = [scr_f32[:, cc, :].rearrange("p (s t) -> p s t", s=nseq) for cc in range(4)]
        def tk(n):
            if n.startswith("CN"):
                return scr_bf_tok
            if n.startswith("C") and n[1:].isdigit():
                return scr_f32_tok
            return f"{tag}_{n}"
        wrr = [0]

        def wload(c0):
            k = wrr[0] % 2
            wrr[0] += 1
            load_w(wst[k], w_in_v[:, :, c0:c0 + 128], tk(f"wst{k}"))
            return wst[k], tk(f"wst{k}")

        mb, mbt = misc_bank()
        sb, sbt = misc_bank()
        for cc in range(4):
            wa, wat = wload(2048 + cc * 128)
            wb, wbt = wload(2560 + cc * 128)
            proj_T(wa, wat, 0, xb, xbtok, ntok, rrep, rreptok, tA, tk("tA"))
            proj_T(wb, wbt, 0, xb, xbtok, ntok, rrep, rreptok, tB, tk("tB"))
            S.add("act", lambda e: e.activation(out=tD, in_=tB, func=AF.Exp, scale=-1.0), r=[tk("tB")], w=[tk("tD")])
            S.add("dve", lambda e: e.tensor_scalar(out=tD, in0=tD, scalar1=1.0, scalar2=None, op0=ALU.add),
                  r=[tk("tD")], w=[tk("tD")])
            S.add("dve", lambda e: e.reciprocal(out=tD, in_=tD), r=[tk("tD")], w=[tk("tD")])
            S.add("dve", lambda e, cc=cc: e.tensor_tensor(out=UT[:, cc, :, HWS:HWS + T],
                                                          in0=tD.rearrange("p (s t) -> p s t", s=nseq),
                                                          in1=tA.rearrange("p (s t) -> p s t", s=nseq), op=ALU.mult),
                  r=[tk("tD"), tk("tA")], w=[tk(f"UT{cc}")])
            if state_ap is not None:
                S.add("sp", lambda e, cc=cc: e.dma_start(out=UT[:, cc, :, 0:HWS], in_=state_ap[:, cc, :, :]),
                      w=[tk(f"UTh{cc}")], dma=True)
            else:
                proj_T(wa, wat, 0, xhb, xhbtok, HWS, rreph, rrephtok, hA[:, 0:HWS], tk("hA"))
                proj_T(wb, wbt, 0, xhb, xhbtok, HWS, rreph, rrephtok, hB[:, 0:HWS], tk("hB"))
                S.add("act", lambda e: e.activation(out=hB[:, 0:HWS], in_=hB[:, 0:HWS], func=AF.Exp, scale=-1.0),
                      r=[tk("hB")], w=[tk("hB")])
                S.add("dve", lambda e: e.tensor_scalar(out=hB[:, 0:HWS], in0=hB[:, 0:HWS], scalar1=1.0, scalar2=None,
                                                       op0=ALU.add), r=[tk("hB")], w=[tk("hB")])
                S.add("dve", lambda e: e.reciprocal(out=hB[:, 0:HWS], in_=hB[:, 0:HWS]), r=[tk("hB")], w=[tk("hB")])
                S.add("dve", lambda e, cc=cc: e.tensor_tensor(out=UT[:, cc, 0, 0:HWS], in0=hB[:, 0:HWS],
                                                              in1=hA[:, 0:HWS], op=ALU.mult),
                      r=[tk("hB"), tk("hA")], w=[tk(f"UTh{cc}")])
            utoks = [tk(f"UT{cc}"), tk(f"UTh{cc}")]
            o0 = HWS - (CW - 1)
            ceng = "dve"
            C = Cs[cc]
            S.add(ceng, lambda e, cc=cc, C=C: e.tensor_scalar(out=C, in0=UT[:, cc, :, o0:o0 + T],
                                                               scalar1=dww[:, cc, 0:1], scalar2=dwb[:, cc:cc + 1],
                                                               op0=ALU.mult, op1=ALU.add),
                  r=utoks + ["vec"], w=[tk(f"C{cc}")])
            for wi in range(1, CW):
                S.add(ceng, lambda e, cc=cc, wi=wi, C=C: e.scalar_tensor_tensor(
                    out=C, in0=UT[:, cc, :, o0 + wi:o0 + wi + T], scalar=dww[:, cc, wi:wi + 1], in1=C,
                    op0=ALU.mult, op1=ALU.add), r=utoks + [tk(f"C{cc}"), "vec"], w=[tk(f"C{cc}")])
        mean_ps = bank(mb, ntok)
        sq_ps = bank(sb, ntok)
        for cc in range(4):
            Cf = Cs[cc].rearrange("p s t -> p (s t)")
            S.add("dve", lambda e, Cf=Cf: e.tensor_copy(out=cb, in_=Cf), r=[tk(f"C{cc}")], w=[tk("cb")])
            S.add("dve", lambda e, Cf=Cf: e.tensor_tensor(out=csq, in0=Cf, in1=Cf, op=ALU.mult), r=[tk(f"C{cc}")],
                  w=[tk("csq")])
            S.add("pe", lambda e, cc=cc: e.matmul(mean_ps, lhsT=onesb, rhs=cb, start=(cc == 0), stop=(cc == 3)),
                  r=[tk("cb"), "onesb"], w=[mbt])
            S.add("pe", lambda e, cc=cc: e.matmul(sq_ps, lhsT=onesb, rhs=csq, start=(cc == 0), stop=(cc == 3)),
                  r=[tk("csq"), "onesb"], w=[sbt])
        S.add("dve", lambda e: e.tensor_scalar(out=tA, in0=mean_ps, scalar1=1.0 / 512, scalar2=None, op0=ALU.mult),
              r=[mbt], w=[tk("tA")])
        S.add("dve", lambda e: e.tensor_tensor(out=tB, in0=tA, in1=tA, op=ALU.mult), r=[tk("tA")], w=[tk("tB")])
        S.add("dve", lambda e: e.scalar_tensor_tensor(out=tB, in0=sq_ps, scalar=1.0 / 512, in1=tB, op0=ALU.mult,
                                                      op1=ALU.subtract), r=[sbt, tk("tB")], w=[tk("tB")])
        S.add("dve", lambda e: e.tensor_scalar(out=tB, in0=tB, scalar1=EPS, scalar2=None, op0=ALU.add),
              r=[tk("tB")], w=[tk("tB")])
        S.add("act", lambda e: e.activation(out=tB, in_=tB, func=AF.Ln), r=[tk("tB")], w=[tk("tB")])
        S.add("act", lambda e: e.activation(out=tB, in_=tB, func=AF.Exp, scale=-0.5), r=[tk("tB")], w=[tk("tB")])
        for cc in range(4):
            Cf = Cs[cc].rearrange("p s t -> p (s t)")
            S.add("dve", lambda e, Cf=Cf: e.tensor_tensor(out=tD, in0=Cf, in1=tA, op=ALU.subtract),
                  r=[tk(f"C{cc}"), tk("tA")], w=[tk("tD")])
            S.add("dve", lambda e: e.tensor_tensor(out=tD, in0=tD, in1=tB, op=ALU.mult), r=[tk("tD"), tk("tB")],
                  w=[tk("tD")])
            S.add("dve", lambda e, cc=cc: e.tensor_scalar(out=tD, in0=tD, scalar1=lng[:, cc:cc + 1],
                                                          scalar2=lnb[:, cc:cc + 1], op0=ALU.mult, op1=ALU.add),
                  r=[tk("tD"), "vec"], w=[tk("tD")])
            silu_mul(tD, tk("tD"), CN[:, cc, :], tk(f"CN{cc}"), None, None, tE[:, 0:ntok], tk("tE"))
        for oc in range(4):
            wg, wgt = wload(3072 + oc * 128)
            proj_T(wg, wgt, 0, xb, xbtok, ntok, rrep, rreptok, tA, tk("tA"))
            b, bt = misc_bank()
            ps = bank(b, ntok)
            for cc in range(4):
                S.add("pe", lambda e, cc=cc, oc=oc, ps=ps: e.matmul(ps, lhsT=wpw2[:, cc, oc * 128:(oc + 1) * 128],
                                                                    rhs=CN[:, cc, :], start=(cc == 0), stop=(cc == 3)),
                      r=["wpw2", tk(f"CN{cc}")], w=[bt])
            S.add("dve", lambda e, oc=oc, ps=ps: e.tensor_scalar(out=tB, in0=ps, scalar1=bp2[:, oc:oc + 1], scalar2=None,
                                                                 op0=ALU.add), r=[bt, "vec"], w=[tk("tB")])
            silu_mul(tA, tk("tA"), mconv_out(oc), f"{mconvtok}{oc}", tB, tk("tB"), tE[:, 0:ntok], tk("tE"))
        if ctail is not None:
            b, bt = misc_bank()
            ps = bank(b, 512)
            for cc in range(4):
                S.add("pe", lambda e, cc=cc, ps=ps: e.transpose(ps[0:HW, cc * 128:(cc + 1) * 128],
                                                                UT[:, cc, 0, HWS + T - HW:HWS + T], ident),
                      r=[tk(f"UT{cc}"), "ident"], w=[bt])
            ctb = tE[0:HW, 0:512]
            S.add("dve", lambda e, ps=ps: e.tensor_copy(out=ctb, in_=ps[0:HW, :]), r=[bt], w=[tk("tE")])
            S.add("sp", lambda e: e.dma_start(out=ctail, in_=ctb), r=[tk("tE")], w=["OUT_ctail"], dma=True)
        if post is not None:
            post(UT, tk, tE)
        A.release(m0)

    def out_phase(tag, otag, nblk, blk, mT, mTtoks, wout, xnat, yout, xr_alias=None, alias_tok=None):
        m0 = A.mark()
        if xr_alias is None:
            xr = [A.alloc([128, D], F32) for _ in range(2)]
        else:
            xr = xr_alias
        yr = [A.alloc([128, D], F32) for _ in range(2)]
        ssq = A.alloc([128, 2], F32)
        tk = lambda n: f"{tag}_{n}"
        for tb in range(nblk):
            k = tb % 2
            al_r = [alias_tok] if (alias_tok and tb > 0) else []
            al_w = [alias_tok] if (alias_tok and tb == 0) else []
            S.add("sp", lambda e, tb=tb, k=k: e.dma_start(out=xr[k][0:blk, :], in_=xnat(tb)), r=al_r,
                  w=[tk(f"xr{k}")] + al_w, dma=True)
            for half in range(2):
                b, bt = misc_bank()
                ps = bank(b, 512)
                for kc in range(8):
                    S.add("pe", lambda e, kc=kc, half=half, tb=tb, ps=ps: e.matmul(
                        ps[0:blk, :], lhsT=mT(kc, tb), rhs=wout[:, kc, half * 512:(half + 1) * 512],
                        start=(kc == 0), stop=(kc == 7)), r=list(mTtoks[kc]) + ["wout"], w=[bt])
                S.add("dve", lambda e, half=half, k=k, ps=ps: e.tensor_tensor(
                    out=yr[k][0:blk, half * 512:(half + 1) * 512], in0=ps[0:blk, :],
                    in1=xr[k][0:blk, half * 512:(half + 1) * 512], op=ALU.add),
                    r=[bt, tk(f"xr{k}")] + ([alias_tok] if alias_tok else []),
                    w=[tk(f"yr{k}")])
            S.add("dve", lambda e, k=k: e.tensor_tensor(out=xr[k][0:blk, :], in0=yr[k][0:blk, :], in1=yr[k][0:blk, :],
                                                        op=ALU.mult),
                  r=[tk(f"yr{k}")] + ([alias_tok] if alias_tok else []), w=[tk(f"xr{k}")])
            S.add("dve", lambda e, k=k: e.reduce_sum(out=ssq[0:blk, k:k + 1], in_=xr[k][0:blk, :],
                                                     axis=mybir.AxisListType.X),
                  r=[tk(f"xr{k}")] + ([alias_tok] if alias_tok else []), w=[tk(f"ssq{k}")])
            rstd_from("dve", ssq[0:blk, k:k + 1], ssq[0:blk, k:k + 1], D, [tk(f"ssq{k}")], tk(f"ssq{k}"))
            S.add("dve", lambda e, k=k: e.scalar_tensor_tensor(out=yr[k][0:blk, :], in0=yr[k][0:blk, :],
                                                               scalar=ssq[0:blk, k:k + 1], in1=fgr[0:blk, :],
                                                               op0=ALU.mult, op1=ALU.mult),
                  r=[tk(f"yr{k}"), tk(f"ssq{k}"), "rep"], w=[tk(f"yr{k}")])
            S.add("sp", lambda e, tb=tb, k=k: e.dma_start(out=yout(tb), in_=yr[k][0:blk, :]), r=[tk(f"yr{k}")],
                  w=[f"OUT_{otag}y{tb}"], dma=True)
        A.release(m0)

    wpw2 = A.alloc([128, 4, 512], BF16)
    load_w(wpw2, w_pw2_v, "wpw2")
    MATT = A.alloc([128, 4, NSL * U], BF16)
    MCONV = A.alloc([128, 4, NSL * U], BF16)
    mP = A.mark()
    WKV = A.alloc([128, DC, 512], BF16)
    WQG = A.alloc([128, DC, 512], BF16)
    KT = A.alloc([128, 2, NU * U], BF16)
    VA = A.alloc([128, NU * NKB, 2, 129], BF16)
    xf = A.alloc([128, DC, U], F32)
    xb = A.alloc([128, DC, U], BF16)
    xsq = A.alloc([128, DC, U], BF16)
    rrep = A.alloc([128, U], F32)
    rnat = A.alloc([128, NKB], F32)
    QT = A.alloc([128, 2, 2, U], BF16)
    PT = [A.alloc([128, 2, QW], BF16) for _ in range(3)]
    sgatt = A.alloc([128, NKB, 256], F32)
    kvo = [A.alloc([128, 512], F32) for _ in range(2)]
    xhf = A.alloc([128, DC, HW], F32)
    xhb = A.alloc([128, DC, HW], BF16)
    xhsq = A.alloc([128, DC, HW], BF16)
    rreph = A.alloc([128, HW], F32)
    ep = [A.alloc([128, 128], F32) for _ in range(4)]
    epl = A.alloc([128, 8], F32)
    gtmp = A.alloc([128, 256], F32)
    gtmp2 = A.alloc([128, 256], F32)
    mP2 = A.mark()
    xsq_f = xsq.rearrange("p a b -> p (a b)").bitcast(F32)
    xr_al = [xsq_f[:, 0:D], xsq_f[:, D:2 * D]] if 4 * U >= 2 * D else None

    S.add("pool", lambda e: e.memset(VA[:, :, :, 128:129], 1.0), w=["VAones"])
    S.add("pool", lambda e: e.memset(QT, 0.0), w=["QT0", "QT1"])

    def attention(i, hl, h):
        nun = 4 * i + 4
        blocks = []
        for n in range(nun):
            for kb in range(NKB):
                diag = (n == nun - 1)
                for qg in range(NQG):
                    qlo = qg * QW
                    q0 = max(qlo, kb * 128) if diag else qlo
                    if q0 >= qlo + QW:
                        continue
                    blocks.append((n, kb, qg, q0, diag))
        npt = [0]
        otoks = {}
        for m in range(2):
            for qs in range(NKB):
                a = m * NKB + qs
                otoks[(m, qs)] = f"OB{2 + a // 3}"

        def oacc(m, qs):
            a = m * NKB + qs
            return bank(2 + a // 3, 129, (a % 3) * 160)

        pend = []
        started = set()

        def emit_pv(item):
            (n, kb, qg, q0, diag, ptk, pt) = item
            g = n * NKB + kb
            for m in range(2):
                for qs in range(q0 // 128, (qg * QW + QW) // 128):
                    bk = 2 + (m * NKB + qs) // 3
                    first = bk not in started
                    started.add(bk)
                    last = diag and (kb == qs)
                    c0 = qs * 128 - qg * QW
                    S.add("pe", lambda e, m=m, qs=qs, c0=c0, g=g, pt=pt, first=first, last=last: e.matmul(
                        oacc(m, qs)[:, 0:129], lhsT=pt[:, m, c0:c0 + 128], rhs=VA[:, g, hl, :], start=first, stop=last,
                        skip_group_check=True),
                        r=[f"PT{ptk}", f"VA{n}_{hl}", "VAones"], w=[otoks[(m, qs)]])

        for (n, kb, qg, q0, diag) in blocks:
            g = n * NKB + kb
            sbk = npt[0] % 2
            ptk = npt[0] % 3
            npt[0] += 1
            st = bank(sbk, 2 * QW).rearrange("p (m q) -> p m q", m=2)
            c0 = q0 - qg * QW
            wq = qg * QW + QW - q0

            def mm(e, st=st, c0=c0, wq=wq, g=g, q0=q0):
                return e.matmul(st[:, :, c0:c0 + wq], lhsT=KT[:, hl, g * 128:(g + 1) * 128],
                                rhs=QT[:, hl, :, q0:q0 + wq], start=True, stop=True)
            S.add("pe", mm, r=[f"KT{n}_{hl}", f"QT{hl}"], w=[f"ST{sbk}"])
            pt = PT[ptk]
            unc = (n >= 4 * i) and not diag
            S.add("act", lambda e, st=st, pt=pt, c0=c0, wq=wq: e.activation(
                out=pt[:, :, c0:c0 + wq], in_=st[:, :, c0:c0 + wq], func=AF.Exp, scale=0.125),
                r=[f"ST{sbk}"], w=[f"PT{ptk}"])
            if unc:
                sel = bs[:, i * 3 + (n - 4 * i): i * 3 + (n - 4 * i) + 1]
                S.add("dve", lambda e, pt=pt, c0=c0, wq=wq, sel=sel: e.tensor_scalar(
                    out=pt[:, :, c0:c0 + wq], in0=pt[:, :, c0:c0 + wq], scalar1=sel, scalar2=None, op0=ALU.mult),
                    r=[f"PT{ptk}", "bs"], w=[f"PT{ptk}"])
            if diag and q0 == kb * 128:
                for m in range(2):
                    S.add("dve", lambda e, pt=pt, c0=c0, m=m: e.tensor_tensor(out=pt[:, m, c0:c0 + 128],
                                                                               in0=pt[:, m, c0:c0 + 128], in1=tri,
                                                                               op=ALU.mult),
                          r=[f"PT{ptk}", "tri"], w=[f"PT{ptk}"])
            pend.append((n, kb, qg, q0, diag, ptk, pt))
            if len(pend) > 1:
                emit_pv(pend.pop(0))
        while pend:
            emit_pv(pend.pop(0))
        allo = sorted(set(otoks.values()))
        for qs in range(NKB):
            o0 = oacc(0, qs)
            o1 = oacc(1, qs)
            par = qs % 2
            e0 = ep[par * 2]
            e1 = ep[par * 2 + 1]
            el = epl[:, par * 4:par * 4 + 4]
            te0, te1, tel = f"ep_e0{par}", f"ep_e1{par}", f"ep_el{par}"
            S.add("dve", lambda e, o0=o0, el=el: e.reciprocal(out=el[:, 0:1], in_=o0[:, 128:129]), r=allo,
                  w=[tel])
            S.add("dve", lambda e, o1=o1, el=el: e.reciprocal(out=el[:, 1:2], in_=o1[:, 128:129]),
                  r=allo + [tel], w=[tel])
            S.add("dve", lambda e, el=el: e.tensor_tensor(out=el[:, 1:2], in0=el[:, 1:2], in1=lam[:, 1:2], op=ALU.mult),
                  r=[tel, "lam"], w=[tel])
            S.add("dve", lambda e, o0=o0, e0=e0, el=el: e.tensor_scalar(out=e0, in0=o0[:, 0:128], scalar1=el[:, 0:1],
                                                                        scalar2=None, op0=ALU.mult),
                  r=allo + [tel], w=[te0])
            S.add("dve", lambda e, o1=o1, e0=e0, el=el: e.scalar_tensor_tensor(out=e0, in0=o1[:, 0:128],
                                                                               scalar=el[:, 1:2], in1=e0,
                                                                               op0=ALU.mult, op1=ALU.add),
                  r=allo + [tel, te0], w=[te0])
            S.add("dve", lambda e, e0=e0, e1=e1: e.tensor_tensor(out=e1, in0=e0, in1=e0, op=ALU.mult), r=[te0], w=[te1])
            S.add("dve", lambda e, e1=e1, el=el: e.reduce_sum(out=el[:, 2:3], in_=e1, axis=mybir.AxisListType.X),
                  r=[te1, tel], w=[tel])
            rstd_from("dve", el[:, 2:3], el[:, 2:3], 128, [tel], tel)
            S.add("dve", lambda e, e0=e0, el=el: e.scalar_tensor_tensor(out=e0, in0=e0, scalar=el[:, 2:3], in1=sgr,
                                                                        op0=ALU.mult, op1=ALU.mult),
                  r=[te0, tel, "sgr"], w=[te0])
            S.add("dve", lambda e, e0=e0, qs=qs: e.tensor_tensor(out=e0, in0=e0, in1=sgatt[:, qs, hl * 128:(hl + 1) * 128],
                                                                 op=ALU.mult), r=[te0, f"sgatt{qs}"], w=[te0])
            b, bt = misc_bank()
            ps = bank(b, 128)
            S.add("pe", lambda e, e0=e0, ps=ps: e.transpose(ps, e0, ident), r=[te0, "ident"], w=[bt])
            S.add("dve", lambda e, ps=ps, qs=qs: e.tensor_copy(out=MATT[:, h, i * U + qs * 128: i * U + (qs + 1) * 128],
                                                               in_=ps), r=[bt], w=[f"MATT{h}_{i}_{qs}"])

    for hp in (range(2) if 'P' in cfg.get('phases', 'PS') else []):
        load_w(WKV[:, :, 0:256], w_in_v[:, :, 512 + hp * 256: 512 + hp * 256 + 256], "WK")
        load_w(WKV[:, :, 256:512], w_in_v[:, :, 1024 + hp * 256: 1024 + hp * 256 + 256], "WV")
        load_w(WQG[:, :, 0:256], w_in_v[:, :, hp * 256: hp * 256 + 256], "WQ")
        load_w(WQG[:, :, 256:512], w_in_v[:, :, 1536 + hp * 256: 1536 + hp * 256 + 256], "WG")
        if hp == 1:
            S.barrier()
            A.release(mP2)
            wout = A.alloc([128, DC, D], BF16)
            load_w(wout, w_out_v, "wout")
        for n in range(NU):
            own = (n % 4 == 3)
            i = n // 4
            S.add("sp", lambda e, n=n: e.dma_start(out=xf, in_=xT[n]), w=["xf"], dma=True)
            prep_x(xf, "xf", xb, "xb", xsq, "xsq", U)
            stats_rep(xsq, "xsq", U, rrep, "rrep")
            stats_nat(xsq, "xsq", NKB, rnat, "rnat")
            for hl in range(2):
                proj_T(WKV, "WK", hl * 128, xb, "xb", U, rrep, "rrep", KT[:, hl, n * U:(n + 1) * U], f"KT{n}_{hl}")
            for tb in range(NKB):
                b, bt = misc_bank()
                ps = bank(b, 512)
                for dc in range(DC):
                    S.add("pe", lambda e, dc=dc, tb=tb, ps=ps: e.matmul(ps[:, 0:256], lhsT=xb[:, dc, tb * 128:(tb + 1) * 128],
                                                                       rhs=WKV[:, dc, 256:512], start=(dc == 0),
                                                                       stop=(dc == DC - 1)),
                          r=xb_toks("xb") + ["WV"], w=[bt])
                g = n * NKB + tb
                S.add("act", lambda e, ps=ps, g=g, tb=tb: e.activation(
                    out=VA[:, g, :, 0:128], in_=ps[:, 0:256].rearrange("p (h e) -> p h e", h=2), func=AF.Copy,
                    scale=rnat[:, tb:tb + 1]), r=[bt, "rnat"], w=[f"VA{n}_0", f"VA{n}_1"])
                if own:
                    k = tb % 2
                    S.add("dve", lambda e, ps=ps, k=k, tb=tb: e.tensor_scalar(out=kvo[k][:, 256:512], in0=ps[:, 0:256],
                                                                              scalar1=rnat[:, tb:tb + 1], scalar2=None,
                                                                              op0=ALU.mult), r=[bt, "rnat"], w=[f"kvoV{k}"])
                    S.add("sp", lambda e, k=k, tb=tb, i=i, hp=hp: e.dma_start(
                        out=nv_o[i, tb * 128:(tb + 1) * 128, hp * 256:(hp + 1) * 256], in_=kvo[k][:, 256:512]),
                        r=[f"kvoV{k}"], w=[f"OUT_nv{hp}_{i}_{tb}"], dma=True)
                    b2, bt2 = misc_bank()
                    ps2 = bank(b2, 512)
                    for dc in range(DC):
                        S.add("pe", lambda e, dc=dc, tb=tb, ps2=ps2: e.matmul(
                            ps2[:, 0:256], lhsT=xb[:, dc, tb * 128:(tb + 1) * 128], rhs=WKV[:, dc, 0:256],
                            start=(dc == 0), stop=(dc == DC - 1)), r=xb_toks("xb") + ["WK"], w=[bt2])
                    for dc in range(DC):
                        S.add("pe", lambda e, dc=dc, tb=tb, ps2=ps2: e.matmul(
                            ps2[:, 256:512], lhsT=xb[:, dc, tb * 128:(tb + 1) * 128], rhs=WQG[:, dc, 256:512],
                            start=(dc == 0), stop=(dc == DC - 1)), r=xb_toks("xb") + ["WG"], w=[bt2])
                    S.add("dve", lambda e, ps2=ps2, k=k, tb=tb: e.tensor_scalar(out=kvo[k][:, 0:256], in0=ps2[:, 0:256],
                                                                                scalar1=rnat[:, tb:tb + 1], scalar2=None,
                                                                                op0=ALU.mult), r=[bt2, "rnat"], w=[f"kvoK{k}"])
                    S.add("sp", lambda e, k=k, tb=tb, i=i, hp=hp: e.dma_start(
                        out=nk_o[i, tb * 128:(tb + 1) * 128, hp * 256:(hp + 1) * 256], in_=kvo[k][:, 0:256]),
                        r=[f"kvoK{k}"], w=[f"OUT_nk{hp}_{i}_{tb}"], dma=True)
                    S.add("dve", lambda e, ps2=ps2, tb=tb: e.tensor_scalar(out=gtmp, in0=ps2[:, 256:512],
                                                                           scalar1=rnat[:, tb:tb + 1], scalar2=None,
                                                                           op0=ALU.mult), r=[bt2, "rnat"], w=["gtmp"])
                    silu_mul(gtmp, "gtmp", sgatt[:, tb, :], f"sgatt{tb}", None, None, gtmp2, "gtmp2")
            if own:
                for hl in range(2):
                    b, bt = misc_bank()
                    psq = bank(b, U)
                    for dc in range(DC):
                        S.add("pe", lambda e, dc=dc, hl=hl, psq=psq: e.matmul(
                            psq, lhsT=WQG[:, dc, hl * 128:(hl + 1) * 128], rhs=xb[:, dc, :], start=(dc == 0),
                            stop=(dc == DC - 1)), r=["WQ"] + xb_toks("xb"), w=[bt])
                    S.add("dve", lambda e, hl=hl, psq=psq: e.tensor_tensor(out=QT[0:64, hl, 0, :], in0=psq[0:64, :],
                                                                          in1=rrep[0:64, :], op=ALU.mult),
                          r=[bt, "rrep"], w=[f"QT{hl}"])
                    S.add("dve", lambda e, hl=hl, psq=psq: e.tensor_tensor(out=QT[64:128, hl, 1, :], in0=psq[64:128, :],
                                                                          in1=rrep[64:128, :], op=ALU.mult),
                          r=[bt, "rrep", f"QT{hl}"], w=[f"QT{hl}"])
                if hp == 0:
                    S.add("sp", lambda e, i=i: e.dma_start(out=xhf, in_=xh[i]), w=["xhf"], dma=True)
                    prep_x(xhf, "xhf", xhb, "xhb", xhsq, "xhsq", HW)
                    stats_rep(xhsq, "xhsq", HW, rreph, "rreph")
                    row_phase("rp", xb, "xb", xhb, "xhb", rrep, "rrep", rreph, "rreph", 1, U, HW, None,
                              lambda oc, i=i: MCONV[:, oc, i * U:(i + 1) * U], f"MCONV{i}_", wpw2,
                              ctail=(ctail_o if i == NSL - 1 else None), scr_f32=xf, scr_f32_tok="xf",
                              scr_bf=xsq, scr_bf_tok="xsq")
                for hl in range(2):
                    attention(i, hl, hp * 2 + hl)
                if hp == 1:
                    def mT(kc, tb, i=i):
                        if kc < 4:
                            return MATT[:, kc, i * U + tb * 128: i * U + (tb + 1) * 128]
                        return MCONV[:, kc - 4, i * U + tb * 128: i * U + (tb + 1) * 128]
                    mtoks = []
                    for kc in range(8):
                        if kc < 4:
                            mtoks.append([f"MATT{kc}_{i}_{qs}" for qs in range(NKB)])
                        else:
                            mtoks.append([f"MCONV{i}_{kc - 4}"])
                    out_phase("op", f"p{i}", NKB, 128, mT, mtoks, wout,
                              lambda tb, i=i: xn[i, tb * 128:(tb + 1) * 128, :],
                              lambda tb, i=i: y_o[i, tb * 128:(tb + 1) * 128, :],
                              xr_alias=xr_al, alias_tok="xsq")

    S.barrier()
    A.release(mP)
    BARR = []

    if 'S' in cfg.get('phases', 'PS'):
        WS = A.alloc([128, DC, 2048], BF16)
        S.add("pool", lambda e: e.dma_start(out=WS[:, :, 0:1024], in_=w_in_v[:, :, 0:1024]), w=["WSa"], dma=True)
        S.add("pool", lambda e: e.dma_start(out=WS[:, :, 1024:2048], in_=w_in_v[:, :, 1024:2048]), w=["WSb"], dma=True)
        wout2 = A.alloc([128, DC, D], BF16)
        S.add("pool", lambda e: e.dma_start(out=wout2, in_=w_out_v), w=["wout"], dma=True)
        sxf = A.alloc([128, DC, NTS], F32)
        sxb = A.alloc([128, DC, NTS], BF16)
        sxsq = A.alloc([128, DC, NTS], BF16)
        srrep = A.alloc([128, NTS], F32)
        srnat = A.alloc([128, 1], F32)
        S.add("sp", lambda e: e.dma_start(out=sxf, in_=xsT), w=["sxf"], dma=True)
        prep_x(sxf, "sxf", sxb, "sxb", sxsq, "sxsq", NTS)
        stats_rep(sxsq, "sxsq", NTS, srrep, "srrep")
        stats_nat(sxsq, "sxsq", 1, srnat, "srnat", blk=NTS)
        skv = A.alloc([128, 1024], F32)
        VSb = A.alloc([128, 512], BF16)
        for which, c0 in (("k", 512), ("v", 1024)):
            b, bt = misc_bank()
            ps = bank(b, 512)
            for dc in range(DC):
                S.add("pe", lambda e, dc=dc, ps=ps, c0=c0: e.matmul(ps[0:NTS, :], lhsT=sxb[:, dc, :], rhs=WS[:, dc, c0:c0 + 512],
                                                                   start=(dc == 0), stop=(dc == DC - 1)),
                      r=xb_toks("sxb") + ["WSa", "WSb"], w=[bt])
            o = skv[0:NTS, 0:512] if which == "k" else skv[0:NTS, 512:1024]
            S.add("dve", lambda e, ps=ps, o=o: e.tensor_scalar(out=o, in0=ps[0:NTS, :], scalar1=srnat[0:NTS, 0:1],
                                                               scalar2=None, op0=ALU.mult), r=[bt, "srnat"], w=[f"skv{which}"])
            dst = nks_o if which == "k" else nvs_o
            S.add("sp", lambda e, o=o, dst=dst: e.dma_start(out=dst, in_=o), r=[f"skv{which}"], w=[f"OUT_s{which}"], dma=True)
            if which == "v":
                S.add("dve", lambda e, o=o: e.tensor_copy(out=VSb[0:NTS, :], in_=o),
                      r=["skvv"], w=["VSb"])
        KTs = A.alloc([128, NH, NTS], BF16)
        Qblk = A.alloc([128, NH, NSQ, 2 * DSEQ], BF16)
        SGT = A.alloc([128, NH, NTS], F32)
        stmp = A.alloc([128, NTS], F32)
        stmp2 = A.alloc([128, NTS], F32)
        stmp3 = A.alloc([128, NTS], F32)
        S.add("pool", lambda e: e.memset(Qblk, 0.0), w=["Qblk"])
        for h in range(NH):
            proj_T(WS, "WSa", 512 + h * 128, sxb, "sxb", NTS, srrep, "srrep", KTs[:, h, :], f"KTs{h}")
            proj_T(WS, "WSa", h * 128, sxb, "sxb", NTS, srrep, "srrep", stmp, "stmp")
            S.add("dve", lambda e, h=h: e.tensor_copy(out=Qblk[0:64, h, :, 0:DSEQ],
                                                      in_=stmp[0:64, :].rearrange("p (s t) -> p s t", s=NSQ)),
                  r=["stmp"], w=["Qblk"])
            S.add("dve", lambda e, h=h: e.tensor_copy(out=Qblk[64:128, h, :, DSEQ:2 * DSEQ],
                                                      in_=stmp[64:128, :].rearrange("p (s t) -> p s t", s=NSQ)),
                  r=["stmp"], w=["Qblk"])
            proj_T(WS, "WSb", 1536 + h * 128, sxb, "sxb", NTS, srrep, "srrep", stmp2, "stmp2")
            silu_mul(stmp2, "stmp2", SGT[:, h, :], f"SGT{h}", None, None, stmp3, "stmp3")
        pts = A.alloc([128, NSQ * NPG], I32)
        ptf = A.alloc([128, NSQ * NPG], F32)
        IDX = A.alloc([128, NSQ * NPG], I32)
        S.add("pool", lambda e: e.dma_start(out=pts, in_=ptab), w=["pts"], dma=True)
        S.add("pool", lambda e: e.tensor_copy(out=ptf, in_=pts), r=["pts"], w=["ptf"])
        S.add("pool", lambda e: e.tensor_scalar(out=ptf, in0=ptf, scalar1=128.0,
                                                scalar2=vec[:, 8 + 4 * CW + 17:8 + 4 * CW + 18], op0=ALU.mult, op1=ALU.add),
              r=["ptf", "vec"], w=["ptf"])
        S.add("pool", lambda e: e.tensor_copy(out=IDX, in_=ptf), r=["ptf"], w=["IDX"])
        smask = A.alloc([128, NSQ, 2 * DSEQ], BF16)
        S.add("pool", lambda e: e.dma_start(out=smask, in_=smask_in), w=["smask"], dma=True)
        GP = 512 // (NH * 2 * DSEQ)
        NBK = 8
        NBV = GP + 8
        KP = [A.alloc([128, 512], BF16) for _ in range(NBK)]
        VP = [A.alloc([128, 512], BF16) for _ in range(NBV)]
        KTP = [A.alloc([128, NH, 128], BF16) for _ in range(2)]
        NBLK = NPG + 1
        PTs = A.alloc([128, NBLK, NH, 2 * DSEQ], BF16)
        OT = A.alloc([128, NSQ, NH, 2 * DSEQ], F32)
        LT = A.alloc([128, NSQ, NH, 2 * DSEQ], F32)
        cnt = [0]
        W16 = 2 * DSEQ
        for s in range(NSQ):
            vbuf = {}
            for pg in range(NBLK):
                stb = (pg // GP) % 2
                stc = (pg % GP) * NH * W16
                if pg < NPG:
                    k = cnt[0] % NBK
                    kv = cnt[0] % NBV
                    k2 = cnt[0] % 2
                    cnt[0] += 1
                    vbuf[pg] = kv
                    idx = s * NPG + pg

                    def ldk(e, k=k, idx=idx):
                        return e.indirect_dma_start(out=KP[k], out_offset=None, in_=cache_k,
                                                    in_offset=bass.IndirectOffsetOnAxis(ap=IDX[:, idx:idx + 1], axis=0))

                    def ldv(e, kv=kv, idx=idx):
                        return e.indirect_dma_start(out=VP[kv], out_offset=None, in_=cache_v,
                                                    in_offset=bass.IndirectOffsetOnAxis(ap=IDX[:, idx:idx + 1], axis=0))
                    S.add("pool", ldk, r=["IDX"], w=[f"KP{k}"], dma=True)
                    S.add("pool", ldv, r=["IDX"], w=[f"VP{kv}"], dma=True)
                    tp = bank(4, 256).bitcast(BF16).rearrange("p (h t) -> p h t", h=NH)
                    for h in range(NH):
                        S.add("pe", lambda e, h=h, k=k, tp=tp: e.transpose(tp[:, h, :], KP[k][:, h * 128:(h + 1) * 128], identb),
                              r=[f"KP{k}", "identb"], w=["PSB4"])
                    S.add("dve", lambda e, k2=k2, tp=tp: e.tensor_copy(out=KTP[k2], in_=tp), r=["PSB4"], w=[f"KTP{k2}"])
                    for h in range(NH):
                        S.add("pe", lambda e, h=h, k2=k2, stb=stb, stc=stc, s=s: e.matmul(
                            bank(stb, W16, stc + h * W16), lhsT=KTP[k2][:, h, :], rhs=Qblk[:, h, s, :],
                            start=True, stop=True), r=[f"KTP{k2}", "Qblk"], w=[f"SST{stb}"])
                else:
                    for h in range(NH):
                        S.add("pe", lambda e, h=h, stb=stb, stc=stc, s=s: e.matmul(
                            bank(stb, W16, stc + h * W16)[0:NTS, :], lhsT=KTs[:, h, :], rhs=Qblk[:, h, s, :],
                            start=True, stop=True), r=[f"KTs{h}", "Qblk"], w=[f"SST{stb}"])
                lastin = (pg % GP == GP - 1) or (pg == NBLK - 1)
                if not lastin:
                    continue
                g0 = (pg // GP) * GP
                ng = pg - g0 + 1
                nfull = ng if pg < NPG else ng - 1
                if nfull > 0:
                    S.add("act", lambda e, stb=stb, g0=g0, nfull=nfull: e.activation(
                        out=PTs[:, g0:g0 + nfull, :, :].rearrange("p a h q -> p (a h q)"),
                        in_=bank(stb, nfull * NH * W16), func=AF.Exp, scale=0.125), r=[f"SST{stb}"], w=["PTs"])
                if pg == NBLK - 1:
                    cl = (pg % GP) * NH * W16
                    S.add("act", lambda e, stb=stb, cl=cl: e.activation(
                        out=PTs[0:NTS, NPG, :, :].rearrange("p h q -> p (h q)"),
                        in_=bank(stb, NH * W16, cl)[0:NTS, :], func=AF.Exp, scale=0.125), r=[f"SST{stb}"], w=["PTs"])
                    for h in range(NH):
                        S.add("dve", lambda e, s=s, h=h: e.tensor_tensor(
                            out=PTs[0:NTS, NPG, h, :], in0=PTs[0:NTS, NPG, h, :], in1=smask[0:NTS, s, :], op=ALU.mult),
                            r=["PTs", "smask"], w=["PTs"])
                for p2 in range(g0, pg + 1):
                    if p2 == NPG:
                        vt, vk, np_ = VSb, "VSb", NTS
                    else:
                        vt, vk, np_ = VP[vbuf[p2]], f"VP{vbuf[p2]}", 128
                    for h in range(NH):
                        S.add("pe", lambda e, h=h, s=s, p2=p2, vt=vt, np_=np_: e.matmul(
                            bank(2, W16, (s % 8) * 64 + h * W16), lhsT=vt[0:np_, h * 128:(h + 1) * 128],
                            rhs=PTs[0:np_, p2, h, :], start=(p2 == 0 and h == 0), stop=(p2 == NBLK - 1),
                            skip_group_check=True),
                            r=[vk, "PTs"], w=["SOT"])
                    S.add("pe", lambda e, s=s, p2=p2, np_=np_: e.matmul(
                        bank(3, NH * W16, (s % 8) * 64), lhsT=onesb[0:np_, :],
                        rhs=PTs[0:np_, p2, :, :].rearrange("p h q -> p (h q)"), start=(p2 == 0), stop=(p2 == NBLK - 1)),
                        r=["PTs", "onesb"], w=["SLT"])
            S.add("dve", lambda e, s=s: e.tensor_copy(out=OT[:, s, :, :].rearrange("p h q -> p (h q)"),
                                                      in_=bank(2, 64, (s % 8) * 64)), r=["SOT"], w=["OT"])
            S.add("dve", lambda e, s=s: e.tensor_copy(out=LT[:, s, :, :].rearrange("p h q -> p (h q)"),
                                                      in_=bank(3, 64, (s % 8) * 64)), r=["SLT"], w=["LT"])
        NQ = NSQ * NH * DSEQ
        so = A.alloc([128, NSQ, NH, DSEQ], F32)
        so2 = A.alloc([128, NSQ, NH, DSEQ], F32)
        S.add("dve", lambda e: e.reciprocal(out=LT, in_=LT), r=["LT"], w=["LT"])
        S.add("dve", lambda e: e.tensor_tensor(out=OT, in0=OT, in1=LT, op=ALU.mult), r=["LT", "OT"], w=["OT"])
        S.add("dve", lambda e: e.scalar_tensor_tensor(out=so, in0=OT[:, :, :, DSEQ:2 * DSEQ], scalar=lam[:, 1:2],
                                                      in1=OT[:, :, :, 0:DSEQ], op0=ALU.mult, op1=ALU.add),
              r=["OT", "lam"], w=["so"])
        S.add("dve", lambda e: e.tensor_tensor(out=so2, in0=so, in1=so, op=ALU.mult), r=["so"], w=["so2"])
        b, bt = misc_bank()
        ps = bank(b, NQ)
        S.add("pe", lambda e, ps=ps: e.matmul(ps, lhsT=onesf, rhs=so2.rearrange("p s h q -> p (s h q)"), start=True,
                                              stop=True), r=["so2", "onesf"], w=[bt])
        rstd_from("dve", so2.rearrange("p s h q -> p (s h q)"), ps, 128, [bt, "so2"], "so2")
        S.add("dve", lambda e: e.tensor_tensor(out=so, in0=so, in1=so2, op=ALU.mult), r=["so", "so2"], w=["so"])
        MATs = A.alloc([128, NH, NTS], BF16)
        for h in range(NH):
            S.add("dve", lambda e, h=h: e.scalar_tensor_tensor(
                out=stmp.rearrange("p (s q) -> p s q", s=NSQ), in0=so[:, :, h, :], scalar=sgcol[:, 0:1],
                in1=SGT[:, h, :].rearrange("p (s q) -> p s q", s=NSQ), op0=ALU.mult, op1=ALU.mult),
                r=["so", "sgcol", f"SGT{h}"], w=["stmp"])
            S.add("dve", lambda e, h=h: e.tensor_copy(out=MATs[:, h, :], in_=stmp), r=["stmp"], w=[f"MATs{h}"])
        MCs = A.alloc([128, 4, NTS], BF16)
        S.add("sp", lambda e: e.dma_start(out=cs_o[:, 0:CW - 1 - DSEQ, :], in_=stn[:, DSEQ:CW - 1, :]),
              w=["OUT_cs_a"], dma=True)

        def post(UTs, tk, tE):
            ucm = A.alloc([128, 4, NTS], F32)
            unat = A.alloc([128, 512], F32)
            b, bt = misc_bank()
            psu = bank(b, 512)
            for cc in range(4):
                S.add("dve", lambda e, cc=cc: e.tensor_copy(out=ucm[:, cc, :].rearrange("p (s t) -> p s t", s=NSQ),
                                                            in_=UTs[:, cc, :, CW - 1:CW - 1 + DSEQ]),
                      r=[tk(f"UT{cc}")], w=[f"ucm{cc}"])
                S.add("pe", lambda e, cc=cc: e.transpose(psu[0:NTS, cc * 128:(cc + 1) * 128], ucm[:, cc, :], ident),
                      r=[f"ucm{cc}", "ident"], w=[bt])
            S.add("dve", lambda e: e.tensor_copy(out=unat[0:NTS, :], in_=psu[0:NTS, :]), r=[bt], w=["unat"])
            for s in range(NSQ):
                S.add("sp", lambda e, s=s: e.dma_start(out=cs_o[s, CW - 1 - DSEQ:CW - 1, :],
                                                       in_=unat[s * DSEQ:(s + 1) * DSEQ, :]),
                      r=["unat"], w=[f"OUT_cs_b{s}"], dma=True)

        row_phase("srp", sxb, "sxb", None, None, srrep, "srrep", None, None, NSQ, DSEQ, CW - 1, stT,
                  lambda oc: MCs[:, oc, :], "MCs", wpw2, post=post, scr_f32=sxf, scr_f32_tok="sxf",
                  scr_bf=sxsq, scr_bf_tok="sxsq")
        S.barrier()

        def mTs(kc, tb):
            return MATs[:, kc, :] if kc < 4 else MCs[:, kc - 4, :]
        out_phase("ops", "s", 1, NTS, mTs, [[f"MATs{h}"] for h in range(4)] + [[f"MCs{oc}"] for oc in range(4)], wout2,
                  lambda tb: xsn, lambda tb: ys_o)


    outs = [t for t in S.last_w.keys() if t.startswith("OUT_")]
    S.add("sp", lambda e: None, r=outs, w=["END"])

    from contextlib import ExitStack
    with ExitStack() as stack:
        S.finalize(nc, stack)
        with nc.Block() as block:
            @block.tensor
            def _(e):
                S.emit("pe", e)

            @block.scalar
            def _(e):
                S.emit("act", e)

            @block.vector
            def _(e):
                S.emit("dve", e)

            @block.gpsimd
            def _(e):
                S.emit("pool", e)

            @block.sync
            def _(e):
                S.emit("sp", e)
    return nc


def chunk_of(j, i):
    return [j, 7 - j, 8 + j, 15 - j][i]


def unit_perm(j):
    perm = []
    for i in range(4):
        c = chunk_of(j, i)
        grp = [u for u in range(4 * i, 4 * i + 4) if u != c]
        perm += grp + [c]
    return perm


def run(cfg, x_prompt, x_sample, cache_k, cache_v, state_conv, page_table, norm_g, w_in, lambda_q1, lambda_k1,
        lambda_q2, lambda_k2, subln_g, dw_w, dw_b, conv_ln_g, conv_ln_b, w_pw2, b_pw2, w_out, final_norm_g):
    U = cfg["U"]; NSQ = cfg["NSQ"]; NPG = cfg["NPG"]; NPOOL = cfg["NPOOL"]; DSEQ = cfg["DSEQ"]
    NTS = NSQ * DSEQ
    HW = 32
    f = lambda a: np.ascontiguousarray(np.asarray(a, dtype=np.float32))
    x_prompt = f(x_prompt); x_sample = f(x_sample)
    B, SEQ, _ = x_prompt.shape
    assert SEQ == 16 * U
    nc = build_nc(cfg)
    vecs = np.zeros((128, 8 + 4 * CW + 18), np.float32)
    vecs[:, 0:8] = f(norm_g)[0].reshape(8, 128).T
    dw = f(dw_w)[0]
    vecs[:, 8:8 + 4 * CW] = dw.T.reshape(4, 128, CW).transpose(1, 0, 2).reshape(128, 4 * CW)
    o = 8 + 4 * CW
    vecs[:, o:o + 4] = f(dw_b)[0].reshape(4, 128).T
    vecs[:, o + 4:o + 8] = f(conv_ln_g)[0].reshape(4, 128).T
    vecs[:, o + 8:o + 12] = f(conv_ln_b)[0].reshape(4, 128).T
    vecs[:, o + 12:o + 16] = f(b_pw2)[0].reshape(4, 128).T
    vecs[:, o + 16] = f(subln_g)[0]
    vecs[:, o + 17] = np.arange(128, dtype=np.float32)
    reps = np.zeros((128, D + 128 + 256), np.float32)
    reps[:, 0:D] = f(final_norm_g)[None, :]
    reps[:, D:D + 128] = f(subln_g)[0][None, :]
    reps[:, D + 128:D + 128 + 256] = np.concatenate([f(lambda_q1)[0], f(lambda_k1)[0], f(lambda_q2)[0],
                                                     f(lambda_k2)[0]])[None, :]
    ident = np.eye(128, dtype=np.float32)
    tri = np.triu(np.ones((128, 128), np.float32))
    smask = np.zeros((128, NSQ, 2 * DSEQ), np.float32)
    for t in range(NTS):
        s, kk = divmod(t, DSEQ)
        for q in range(DSEQ):
            if kk <= q:
                smask[t, s, q] = 1.0
                smask[t, s, DSEQ + q] = 1.0
    ck = f(cache_k)[0].reshape(NPOOL * 128, 512)
    cv = f(cache_v)[0].reshape(NPOOL * 128, 512)
    w_in0 = f(w_in)[0]; w_out0 = f(w_out)[0]; w_pw20 = f(w_pw2)[0]
    st = f(state_conv)[0]
    pt = np.asarray(page_table, dtype=np.int32)
    in_maps = []
    for c in range(8):
        b, j = divmod(c, 4)
        perm = unit_perm(j)
        xb_ = x_prompt[b].reshape(16, U, 8, 128)
        xT = np.ascontiguousarray(xb_[perm].transpose(0, 3, 2, 1))
        xn = np.stack([x_prompt[b, chunk_of(j, i) * U:(chunk_of(j, i) + 1) * U] for i in range(4)])
        xh = np.zeros((4, 128, 8, HW), np.float32)
        bsel = np.ones((128, 12), np.float32)
        for i in range(4):
            cpos = chunk_of(j, i)
            if cpos > 0:
                xh[i] = x_prompt[b, cpos * U - HW:cpos * U].reshape(HW, 8, 128).transpose(2, 1, 0)
            for t in range(3):
                if perm[4 * i + t] > cpos:
                    bsel[:, i * 3 + t] = 0.0
        xs = x_sample[c * NSQ:(c + 1) * NSQ].reshape(NTS, D)
        xsT = np.ascontiguousarray(xs.reshape(NTS, 8, 128).transpose(2, 1, 0))
        stc = st[c * NSQ:(c + 1) * NSQ]
        stT = np.ascontiguousarray(stc.reshape(NSQ, CW - 1, 4, 128).transpose(3, 2, 0, 1))
        in_maps.append(dict(
            xT=xT, xn=np.ascontiguousarray(xn), xh=xh, bsel=bsel, w_in=w_in0, w_out=w_out0, w_pw2=w_pw20, vecs=vecs,
            reps=reps, ident=ident, tri=tri, smask=smask, xsT=xsT, xsn=np.ascontiguousarray(xs), stT=stT,
            stn=np.ascontiguousarray(stc), cache_k=ck, cache_v=cv,
            ptab=np.ascontiguousarray(np.tile(pt[c * NSQ:(c + 1) * NSQ].reshape(1, NSQ * NPG), (128, 1)))))
    res = run_bass_kernel_spmd(nc, in_maps, core_ids=list(range(8)))
    R = res.results
    DB = 8 * NSQ
    y_prompt = np.zeros((B, SEQ, D), np.float32)
    nk = np.zeros((1, B, SEQ, NH, 128), np.float32)
    nv = np.zeros((1, B, SEQ, NH, 128), np.float32)
    ncp = np.zeros((1, B, CW - 1, 512), np.float32)
    y_sample = np.zeros((DB, DSEQ, D), np.float32)
    nks = np.zeros((1, DB, DSEQ, NH, 128), np.float32)
    nvs = np.zeros((1, DB, DSEQ, NH, 128), np.float32)
    ncs = np.zeros((1, DB, CW - 1, 512), np.float32)
    for c in range(8):
        b, j = divmod(c, 4)
        for i in range(4):
            cp = chunk_of(j, i)
            y_prompt[b, cp * U:(cp + 1) * U] = R[c]["y"][i]
            nk[0, b, cp * U:(cp + 1) * U] = R[c]["nk"][i].reshape(U, NH, 128)
            nv[0, b, cp * U:(cp + 1) * U] = R[c]["nv"][i].reshape(U, NH, 128)
            if cp == 15:
                ncp[0, b] = R[c]["ctail"][HW - (CW - 1):HW]
        y_sample[c * NSQ:(c + 1) * NSQ] = R[c]["ys"].reshape(NSQ, DSEQ, D)
        nks[0, c * NSQ:(c + 1) * NSQ] = R[c]["nks"].reshape(NSQ, DSEQ, NH, 128)
        nvs[0, c * NSQ:(c + 1) * NSQ] = R[c]["nvs"].reshape(NSQ, DSEQ, NH, 128)
        ncs[0, c * NSQ:(c + 1) * NSQ] = R[c]["cs"]
    return (y_prompt, y_sample, nk, nv, ncp, nks, nvs, ncs)


def kernel(**inputs):
    return run(CFG_FULL, **inputs)
```

```python
import math
import numpy as np
import concourse.bass as bass
import concourse.mybir as mybir
from concourse.bass_utils import run_bass_kernel_spmd

F32 = mybir.dt.float32
BF16 = mybir.dt.bfloat16
I32 = mybir.dt.int32
ALU = mybir.AluOpType
AF = mybir.ActivationFunctionType

D = 1024
DC = 8
NH = 4
CW = 31
EPS = 1e-5
NEG = -30000.0
LAM_INIT = 0.8 - 0.6 * math.exp(-0.3 * 0)

CFG_FULL = dict(U=512, NSQ=16, NPG=16, NPOOL=2560, DSEQ=8)


class Sched:
    ENG = ("pe", "act", "dve", "pool", "sp")
    CAP = 20000
    NDS = 80

    def __init__(self):
        self.ops = []
        self.last_w = {}
        self.readers = {}
        self.bars = []

    EXCL = ("PSB", "ST", "OB", "SST", "SOT", "SLT")

    def add(self, eng, fn, r=(), w=(), dma=False):
        r = list(r)
        w = list(w)
        for t in list(r):
            if t.startswith(self.EXCL):
                r.remove(t)
                if t not in w:
                    w.append(t)
        deps = set()
        for t in r:
            if t in self.last_w:
                deps.add(self.last_w[t])
        for t in w:
            if t in self.last_w:
                deps.add(self.last_w[t])
            for x in self.readers.get(t, ()):
                deps.add(x)
        idx = len(self.ops)
        deps.discard(idx)
        self.ops.append(dict(eng=eng, fn=fn, deps=deps, dma=dma, sig=False, waits=[]))
        for t in r:
            self.readers.setdefault(t, []).append(idx)
        for t in w:
            self.last_w[t] = idx
            self.readers[t] = []
        return idx

    def barrier(self):
        self.bars.append(len(self.ops))

    def finalize(self, nc, stack):
        import os
        mx = int(os.environ.get("KMAXOPS", "0"))
        if mx:
            self.ops = self.ops[:mx]
            self.bars = [b for b in self.bars if b <= mx]
            self.bars.append(len(self.ops))
            self.ops.append(dict(eng="sp", fn=lambda e: None, deps=set(), dma=False, sig=False, waits=[]))
        ops = self.ops
        seq = {e: 0 for e in self.ENG}
        ndma = {e: 0 for e in self.ENG}
        half = self.NDS // 2
        for o in ops:
            if o["dma"]:
                base = 0 if o["eng"] == "pool" else half
                k = ndma[o["eng"]]
                o["dsem"] = base + k % half
                o["dval"] = 16 * (k // half + 1)
                ndma[o["eng"]] += 1
                o["sig"] = True
            else:
                seq[o["eng"]] += 1
                o["seq"] = seq[o["eng"]]
        known = {e: {x: 0 for x in self.ENG} for e in self.ENG}
        kdma = {e: [0] * self.NDS for e in self.ENG}
        snaps = []
        for bi in self.bars:
            sc = {x: 0 for x in self.ENG}
            sd = {}
            for p in ops[:bi]:
                if p["dma"]:
                    sd[p["dsem"]] = max(sd.get(p["dsem"], 0), p["dval"])
                else:
                    sc[p["eng"]] = max(sc[p["eng"]], p["seq"])
            snaps.append((bi, sc, sd))
        for oi, o in enumerate(ops):
            e = o["eng"]
            need = {}
            needd = {}
            for (bi, sc, sd) in snaps:
                if oi >= bi:
                    for x, v in sc.items():
                        if v > 0:
                            need[x] = (max(need.get(x, (0, None))[0], v), None)
                    for x, v in sd.items():
                        needd[x] = max(needd.get(x, 0), v)
            if o["dma"] and o["dval"] > 16:
                needd[o["dsem"]] = max(needd.get(o["dsem"], 0), o["dval"] - 16)
            for di in o["deps"]:
                p = ops[di]
                if p["dma"]:
                    needd[p["dsem"]] = max(needd.get(p["dsem"], 0), p["dval"])
                else:
                    if p["eng"] == "pe" and e == "pe":
                        continue
                    need[p["eng"]] = (max(need.get(p["eng"], (0, None))[0], p["seq"]), di)
            for pe_, (sq, _) in need.items():
                if known[e][pe_] < sq:
                    known[e][pe_] = sq
                    o["waits"].append(("c", pe_, sq))
            for ds, dv in needd.items():
                if kdma[e][ds] < dv:
                    kdma[e][ds] = dv
                    o["waits"].append(("d", ds, dv))
        byseq = {e: {} for e in self.ENG}
        for o in ops:
            if not o["dma"]:
                byseq[o["eng"]][o["seq"]] = o
        for o in ops:
            for wt in o["waits"]:
                if wt[0] == "c":
                    byseq[wt[1]][wt[2]]["sig"] = True
        cnt = {e: 0 for e in self.ENG}
        for o in ops:
            if not o["dma"] and o["sig"]:
                cnt[o["eng"]] += 1
                o["cval"] = cnt[o["eng"]]
        nsem = {e: cnt[e] // self.CAP + 1 for e in self.ENG}
        self.csems = {e: [stack.enter_context(nc.semaphore(f"c_{e}_{k}")) for k in range(nsem[e])]
                      for e in self.ENG}
        self.dsems = [stack.enter_context(nc.semaphore(f"d_{k}")) for k in range(self.NDS)]
        self.byseq = byseq

    def emit(self, engname, e):
        for o in self.ops:
            if o["eng"] != engname:
                continue
            for wt in o["waits"]:
                if wt[0] == "c":
                    cv = self.byseq[wt[1]][wt[2]]["cval"]
                    e.wait_ge(self.csems[wt[1]][(cv - 1) // self.CAP], (cv - 1) % self.CAP + 1)
                else:
                    e.wait_ge(self.dsems[wt[1]], wt[2])
            try:
                ins = o["fn"](e)
            except Exception:
                print("EMIT FAIL", engname, "op#", self.ops.index(o), "of", len(self.ops), flush=True)
                raise
            if ins is None:
                continue
            if o["dma"]:
                ins.then_inc(self.dsems[o["dsem"]], 16)
            elif o["sig"]:
                cv = o["cval"]
                ins.then_inc(self.csems[engname][(cv - 1) // self.CAP], 1)


class Arena:
    def __init__(self, nc, name, words):
        self.t = nc.alloc_sbuf_tensor(name, [128, words], F32)
        self.words = words
        self.off = 0

    def mark(self):
        return self.off

    def release(self, m):
        self.off = m

    def alloc(self, shape, dtype):
        n = int(np.prod(shape[1:]))
        w = (n + 1) // 2 if dtype == BF16 else n
        w = (w + 7) // 8 * 8
        assert self.off + w <= self.words, f"SBUF arena overflow {self.off + w} > {self.words}"
        ap = self.t.ap()[:, self.off:self.off + w]
        self.off += w
        if dtype != F32:
            ap = ap.bitcast(dtype)
        ap = ap[0:shape[0], 0:n]
        if len(shape) == 3:
            ap = ap.rearrange("p (a b) -> p a b", a=shape[1])
        elif len(shape) == 4:
            ap = ap.rearrange("p (a b c) -> p a b c", a=shape[1], b=shape[2])
        elif len(shape) == 5:
            ap = ap.rearrange("p (a b c d) -> p a b c d", a=shape[1], b=shape[2], c=shape[3])
        return ap


def build_nc(cfg):
    U = cfg["U"]; NSQ = cfg["NSQ"]; NPG = cfg["NPG"]; NPOOL = cfg["NPOOL"]; DSEQ = cfg["DSEQ"]
    NKB = U // 128
    QW = min(U, 256)
    NQG = U // QW
    NU = 16
    NSL = 4
    NTS = NSQ * DSEQ
    HW = 32
    nc = bass.Bass("TRN2", target_bir_lowering=False)

    def din(name, shape, dt=F32):
        return nc.dram_tensor(name, list(shape), dt, kind="ExternalInput").ap()

    def dout(name, shape, dt=F32):
        return nc.dram_tensor(name, list(shape), dt, kind="ExternalOutput").ap()

    xT = din("xT", [NU, 128, DC, U])
    xn = din("xn", [NSL, U, D])
    xh = din("xh", [NSL, 128, DC, HW])
    bsel = din("bsel", [128, NSL * 3])
    w_in = din("w_in", [D, 3584])
    w_out = din("w_out", [D, D])
    w_pw2 = din("w_pw2", [512, 512])
    vecs = din("vecs", [128, 8 + 4 * CW + 4 * 4 + 2])
    reps = din("reps", [128, D + 128 + 4 * 64])
    ident_in = din("ident", [128, 128])
    tri_in = din("tri", [128, 128])
    smask_in = din("smask", [128, NSQ, 2 * DSEQ])
    xsT = din("xsT", [128, DC, NTS])
    xsn = din("xsn", [NTS, D])
    stT = din("stT", [128, 4, NSQ, CW - 1])
    stn = din("stn", [NSQ, CW - 1, 512])
    cache_k = din("cache_k", [NPOOL * 128, 512])
    cache_v = din("cache_v", [NPOOL * 128, 512])
    ptab = din("ptab", [128, NSQ * NPG], I32)

    y_o = dout("y", [NSL, U, D])
    nk_o = dout("nk", [NSL, U, 512])
    nv_o = dout("nv", [NSL, U, 512])
    ctail_o = dout("ctail", [HW, 512])
    ys_o = dout("ys", [NTS, D])
    nks_o = dout("nks", [NTS, 512])
    nvs_o = dout("nvs", [NTS, 512])
    cs_o = dout("cs", [NSQ, CW - 1, 512])

    w_in_v = w_in.rearrange("(dc p) c -> p dc c", p=128)
    w_out_v = w_out.rearrange("(dc p) c -> p dc c", p=128)
    w_pw2_v = w_pw2.rearrange("(dc p) c -> p dc c", p=128)

    S = Sched()
    A = Arena(nc, "arena", 53000)
    PS = nc.alloc_psum_tensor("psum", [128, 8 * 512], F32).ap()

    def bank(b, n=512, off=0):
        return PS[:, b * 512 + off: b * 512 + off + n]

    misc_rr = [0]

    def misc_bank():
        b = 5 + (misc_rr[0] % 3)
        misc_rr[0] += 1
        return b, f"PSB{b}"

    ident = A.alloc([128, 128], F32)
    identb = A.alloc([128, 128], BF16)
    tri = A.alloc([128, 128], BF16)
    onesb = A.alloc([128, 128], BF16)
    onesf = A.alloc([128, 128], F32)
    vec = A.alloc([128, 8 + 4 * CW + 18], F32)
    sgcol = A.alloc([128, 1], F32)
    rep = A.alloc([128, D + 128 + 256], F32)
    bs = A.alloc([128, NSL * 3], F32)
    lam = A.alloc([128, 4], F32)
    sgr = A.alloc([128, 128], F32)
    zero1 = A.alloc([128, 1], F32)
    g_nrm = vec[:, 0:8]
    dww = vec[:, 8:8 + 4 * CW].rearrange("p (c w) -> p c w", c=4)
    dwb = vec[:, 8 + 4 * CW: 8 + 4 * CW + 4]
    lng = vec[:, 8 + 4 * CW + 4: 8 + 4 * CW + 8]
    lnb = vec[:, 8 + 4 * CW + 8: 8 + 4 * CW + 12]
    bp2 = vec[:, 8 + 4 * CW + 12: 8 + 4 * CW + 16]
    fgr = rep[:, 0:D]

    S.add("sp", lambda e: e.dma_start(out=ident, in_=ident_in), w=["ident"], dma=True)
    S.add("sp", lambda e: e.dma_start(out=vec, in_=vecs), w=["vec"], dma=True)
    S.add("sp", lambda e: e.dma_start(out=rep, in_=reps), w=["rep"], dma=True)
    S.add("sp", lambda e: e.dma_start(out=bs, in_=bsel), w=["bs"], dma=True)
    S.add("pool", lambda e: e.dma_start(out=tri, in_=tri_in), w=["tri"], dma=True)
    S.add("pool", lambda e: e.dma_start(out=identb, in_=ident_in), w=["identb"], dma=True)
    S.add("dve", lambda e: e.memset(onesb, 1.0), w=["onesb"])
    S.add("dve", lambda e: e.memset(onesf, 1.0), w=["onesf"])
    S.add("dve", lambda e: e.memset(zero1, 0.0), w=["zero1"])
    lq = rep[:, D + 128: D + 128 + 256].rearrange("p (a b) -> p a b", a=4)
    lscr = A.alloc([128, 2, 64], F32)
    S.add("dve", lambda e: e.tensor_tensor(out=lscr[:, 0, :], in0=lq[:, 0, :], in1=lq[:, 1, :], op=ALU.mult),
          r=["rep"], w=["lscr0"])
    S.add("dve", lambda e: e.tensor_tensor(out=lscr[:, 1, :], in0=lq[:, 2, :], in1=lq[:, 3, :], op=ALU.mult),
          r=["rep"], w=["lscr1"])
    S.add("dve", lambda e: e.reduce_sum(out=lam[:, 2:3], in_=lscr[:, 0, :], axis=mybir.AxisListType.X),
          r=["lscr0"], w=["lam2"])
    S.add("dve", lambda e: e.reduce_sum(out=lam[:, 3:4], in_=lscr[:, 1, :], axis=mybir.AxisListType.X),
          r=["lscr1"], w=["lam3"])
    S.add("act", lambda e: e.activation(out=lam[:, 2:4], in_=lam[:, 2:4], func=AF.Exp), r=["lam2", "lam3"],
          w=["lam23"])
    S.add("dve", lambda e: e.tensor_tensor(out=lam[:, 0:1], in0=lam[:, 2:3], in1=lam[:, 3:4], op=ALU.subtract),
          r=["lam23"], w=["lam0a"])
    S.add("dve", lambda e: e.tensor_scalar(out=lam[:, 0:1], in0=lam[:, 0:1], scalar1=LAM_INIT, scalar2=None,
                                           op0=ALU.add), r=["lam0a"], w=["lam0"])
    S.add("dve", lambda e: e.tensor_scalar(out=lam[:, 1:2], in0=lam[:, 0:1], scalar1=-1.0, scalar2=None,
                                           op0=ALU.mult), r=["lam0"], w=["lam"])
    S.add("dve", lambda e: e.tensor_scalar(out=sgcol, in0=vec[:, 8 + 4 * CW + 16:8 + 4 * CW + 17], scalar1=1.0 - LAM_INIT,
                                           scalar2=None, op0=ALU.mult), r=["vec"], w=["sgcol"])
    S.add("dve", lambda e: e.tensor_scalar(out=sgr, in0=rep[:, D:D + 128], scalar1=1.0 - LAM_INIT, scalar2=None,
                                           op0=ALU.mult), r=["rep"], w=["sgr"])

    def rstd_from(eng, out, in_, n, rtok, wtok):
        S.add(eng, lambda e: e.tensor_scalar(out=out, in0=in_, scalar1=1.0 / n, scalar2=EPS, op0=ALU.mult,
                                             op1=ALU.add), r=rtok, w=[wtok])
        S.add("act", lambda e: e.activation(out=out, in_=out, func=AF.Ln), r=[wtok], w=[wtok])
        S.add("act", lambda e: e.activation(out=out, in_=out, func=AF.Exp, scale=-0.5), r=[wtok], w=[wtok])

    def silu_mul(src, srctok, dst, dsttok, other, othertok, tmp, tmptok, shape_eng="dve"):
        S.add("act", lambda e: e.activation(out=tmp, in_=src, func=AF.Exp, scale=-1.0), r=[srctok], w=[tmptok])
        S.add(shape_eng, lambda e: e.tensor_scalar(out=tmp, in0=tmp, scalar1=1.0, scalar2=None, op0=ALU.add),
              r=[tmptok], w=[tmptok])
        S.add("dve", lambda e: e.reciprocal(out=tmp, in_=tmp), r=[tmptok], w=[tmptok])
        if other is None:
            S.add(shape_eng, lambda e: e.tensor_tensor(out=dst, in0=tmp, in1=src, op=ALU.mult),
                  r=[tmptok, srctok], w=[dsttok])
        else:
            S.add(shape_eng, lambda e: e.tensor_tensor(out=tmp, in0=tmp, in1=src, op=ALU.mult),
                  r=[tmptok, srctok], w=[tmptok])
            S.add(shape_eng, lambda e: e.tensor_tensor(out=dst, in0=tmp, in1=other, op=ALU.mult),
                  r=[tmptok, othertok], w=[dsttok])

    def load_w(dst, src, tok):
        S.add("pool", lambda e: e.dma_start(out=dst, in_=src), w=[tok], dma=True)

    def prep_x(xf, xftok, xb, xbtok, xsq, xsqtok, ntok):
        for dc in range(DC):
            S.add("act", lambda e, dc=dc: e.activation(out=xb[:, dc, :], in_=xf[:, dc, :], func=AF.Copy,
                                                       scale=g_nrm[:, dc:dc + 1]),
                  r=[xftok, "vec"], w=[f"{xbtok}_{dc}"])
        S.add("dve", lambda e: e.tensor_tensor(out=xsq, in0=xf, in1=xf, op=ALU.mult), r=[xftok], w=[xsqtok])

    def xb_toks(xbtok):
        return [f"{xbtok}_{dc}" for dc in range(DC)]

    def stats_rep(xsq, xsqtok, ntok, out, outtok):
        b, bt = misc_bank()
        ps = bank(b, ntok)
        for dc in range(DC):
            S.add("pe", lambda e, dc=dc: e.matmul(ps, lhsT=onesb, rhs=xsq[:, dc, :], start=(dc == 0),
                                                   stop=(dc == DC - 1)), r=[xsqtok, "onesb"], w=[bt])
        rstd_from("dve", out, ps, D, [bt], outtok)

    def stats_nat(xsq, xsqtok, nblk, out, outtok, blk=128):
        b, bt = misc_bank()
        ps = bank(b, nblk)
        for tb in range(nblk):
            for dc in range(DC):
                S.add("pe", lambda e, dc=dc, tb=tb: e.matmul(ps[0:blk, tb:tb + 1], lhsT=xsq[:, dc, tb * blk:(tb + 1) * blk],
                                                             rhs=onesb[:, 0:1], start=(dc == 0), stop=(dc == DC - 1)),
                      r=[xsqtok, "onesb"], w=[bt])
        rstd_from("dve", out[0:blk, :], ps[0:blk, :], D, [bt], outtok)

    def proj_T(wt, wtok, c0, xb, xbtok, ntok, rrep, rreptok, out, outtok, out_eng="dve"):
        b, bt = misc_bank()
        ps = bank(b, ntok)
        for dc in range(DC):
            S.add("pe", lambda e, dc=dc: e.matmul(ps, lhsT=wt[:, dc, c0:c0 + 128], rhs=xb[:, dc, :],
                                                   start=(dc == 0), stop=(dc == DC - 1)),
                  r=[wtok] + xb_toks(xbtok), w=[bt])
        S.add(out_eng, lambda e: e.tensor_tensor(out=out, in0=ps, in1=rrep, op=ALU.mult), r=[bt, rreptok],
              w=[outtok])

    def row_phase(tag, xb, xbtok, xhb, xhbtok, rrep, rreptok, rreph, rrephtok, nseq, T, HWS,
                  state_ap, mconv_out, mconvtok, wpw2, ctail=None, post=None, scr_f32=None, scr_f32_tok=None,
                  scr_bf=None, scr_bf_tok=None):
        ntok = nseq * T
        m0 = A.mark()
        UT = A.alloc([128, 4, nseq, HWS + T], F32)
        CN = scr_bf[:, 0:4, :]
        wst = [A.alloc([128, DC, 128], BF16) for _ in range(2)]
        tA = A.alloc([128, ntok], F32)
        tB = A.alloc([128, ntok], F32)
        tD = A.alloc([128, ntok], F32)
        tE = A.alloc([128, max(ntok, 512)], F32)
        cb = A.alloc([128, ntok], BF16)
        csq = A.alloc([128, ntok], BF16)
        hA = A.alloc([128, max(HWS, 8)], F32)
        hB = A.alloc([128, max(HWS, 8)], F32)
        Cs = [scr_f32[:, cc, :].rearrange("p (s t) -> p s t", s=nseq) for cc in range(4)]
        def tk(n):
            if n.startswith("CN"):
                return scr_bf_tok
            if n.startswith("C") and n[1:].isdigit():
                return scr_f32_tok
            return f"{tag}_{n}"
        wrr = [0]

        def wload(c0):
            k = wrr[0] % 2
            wrr[0] += 1
            load_w(wst[k], w_in_v[:, :, c0:c0 + 128], tk(f"wst{k}"))
            return wst[k], tk(f"wst{k}")

        mb, mbt = misc_bank()
        sb, sbt = misc_bank()
        for cc in range(4):
            wa, wat = wload(2048 + cc * 128)
            wb, wbt = wload(2560 + cc * 128)
            proj_T(wa, wat, 0, xb, xbtok, ntok, rrep, rreptok, tA, tk("tA"))
            yield
            proj_T(wb, wbt, 0, xb, xbtok, ntok, rrep, rreptok, tB, tk("tB"))
            yield
            S.add("act", lambda e: e.activation(out=tD, in_=tB, func=AF.Exp, scale=-1.0), r=[tk("tB")], w=[tk("tD")])
            S.add("dve", lambda e: e.tensor_scalar(out=tD, in0=tD, scalar1=1.0, scalar2=None, op0=ALU.add),
                  r=[tk("tD")], w=[tk("tD")])
            S.add("dve", lambda e: e.reciprocal(out=tD, in_=tD), r=[tk("tD")], w=[tk("tD")])
            S.add("dve", lambda e, cc=cc: e.tensor_tensor(out=UT[:, cc, :, HWS:HWS + T],
                                                          in0=tD.rearrange("p (s t) -> p s t", s=nseq),
                                                          in1=tA.rearrange("p (s t) -> p s t", s=nseq), op=ALU.mult),
                  r=[tk("tD"), tk("tA")], w=[tk(f"UT{cc}")])
            if state_ap is not None:
                S.add("sp", lambda e, cc=cc: e.dma_start(out=UT[:, cc, :, 0:HWS], in_=state_ap[:, cc, :, :]),
                      w=[tk(f"UTh{cc}")], dma=True)
            else:
                proj_T(wa, wat, 0, xhb, xhbtok, HWS, rreph, rrephtok, hA[:, 0:HWS], tk("hA"))
                yield
                proj_T(wb, wbt, 0, xhb, xhbtok, HWS, rreph, rrephtok, hB[:, 0:HWS], tk("hB"))
                yield
                S.add("act", lambda e: e.activation(out=hB[:, 0:HWS], in_=hB[:, 0:HWS], func=AF.Exp, scale=-1.0),
                      r=[tk("hB")], w=[tk("hB")])
                S.add("dve", lambda e: e.tensor_scalar(out=hB[:, 0:HWS], in0=hB[:, 0:HWS], scalar1=1.0, scalar2=None,
                                                       op0=ALU.add), r=[tk("hB")], w=[tk("hB")])
                S.add("dve", lambda e: e.reciprocal(out=hB[:, 0:HWS], in_=hB[:, 0:HWS]), r=[tk("hB")], w=[tk("hB")])
                S.add("dve", lambda e, cc=cc: e.tensor_tensor(out=UT[:, cc, 0, 0:HWS], in0=hB[:, 0:HWS],
                                                              in1=hA[:, 0:HWS], op=ALU.mult),
                      r=[tk("hB"), tk("hA")], w=[tk(f"UTh{cc}")])
            utoks = [tk(f"UT{cc}"), tk(f"UTh{cc}")]
            o0 = HWS - (CW - 1)
            ceng = "dve"
            C = Cs[cc]
            S.add(ceng, lambda e, cc=cc, C=C: e.tensor_scalar(out=C, in0=UT[:, cc, :, o0:o0 + T],
                                                               scalar1=dww[:, cc, 0:1], scalar2=dwb[:, cc:cc + 1],
                                                               op0=ALU.mult, op1=ALU.add),
                  r=utoks + ["vec"], w=[tk(f"C{cc}")])
            for wi in range(1, CW):
                S.add(ceng, lambda e, cc=cc, wi=wi, C=C: e.scalar_tensor_tensor(
                    out=C, in0=UT[:, cc, :, o0 + wi:o0 + wi + T], scalar=dww[:, cc, wi:wi + 1], in1=C,
                    op0=ALU.mult, op1=ALU.add), r=utoks + [tk(f"C{cc}"), "vec"], w=[tk(f"C{cc}")])
                if wi % 2 == 0:
                    yield
        mean_ps = bank(mb, ntok)
        sq_ps = bank(sb, ntok)
        for cc in range(4):
            Cf = Cs[cc].rearrange("p s t -> p (s t)")
            S.add("dve", lambda e, Cf=Cf: e.tensor_copy(out=cb, in_=Cf), r=[tk(f"C{cc}")], w=[tk("cb")])
            S.add("dve", lambda e, Cf=Cf: e.tensor_tensor(out=csq, in0=Cf, in1=Cf, op=ALU.mult), r=[tk(f"C{cc}")],
                  w=[tk("csq")])
            S.add("pe", lambda e, cc=cc: e.matmul(mean_ps, lhsT=onesb, rhs=cb, start=(cc == 0), stop=(cc == 3)),
                  r=[tk("cb"), "onesb"], w=[mbt])
            S.add("pe", lambda e, cc=cc: e.matmul(sq_ps, lhsT=onesb, rhs=csq, start=(cc == 0), stop=(cc == 3)),
                  r=[tk("csq"), "onesb"], w=[sbt])
            yield
        S.add("dve", lambda e: e.tensor_scalar(out=tA, in0=mean_ps, scalar1=1.0 / 512, scalar2=None, op0=ALU.mult),
              r=[mbt], w=[tk("tA")])
        S.add("dve", lambda e: e.tensor_tensor(out=tB, in0=tA, in1=tA, op=ALU.mult), r=[tk("tA")], w=[tk("tB")])
        S.add("dve", lambda e: e.scalar_tensor_tensor(out=tB, in0=sq_ps, scalar=1.0 / 512, in1=tB, op0=ALU.mult,
                                                      op1=ALU.subtract), r=[sbt, tk("tB")], w=[tk("tB")])
        S.add("dve", lambda e: e.tensor_scalar(out=tB, in0=tB, scalar1=EPS, scalar2=None, op0=ALU.add),
              r=[tk("tB")], w=[tk("tB")])
        S.add("act", lambda e: e.activation(out=tB, in_=tB, func=AF.Ln), r=[tk("tB")], w=[tk("tB")])
        S.add("act", lambda e: e.activation(out=tB, in_=tB, func=AF.Exp, scale=-0.5), r=[tk("tB")], w=[tk("tB")])
        for cc in range(4):
            Cf = Cs[cc].rearrange("p s t -> p (s t)")
            S.add("dve", lambda e, Cf=Cf: e.tensor_tensor(out=tD, in0=Cf, in1=tA, op=ALU.subtract),
                  r=[tk(f"C{cc}"), tk("tA")], w=[tk("tD")])
            S.add("dve", lambda e: e.tensor_tensor(out=tD, in0=tD, in1=tB, op=ALU.mult), r=[tk("tD"), tk("tB")],
                  w=[tk("tD")])
            S.add("dve", lambda e, cc=cc: e.tensor_scalar(out=tD, in0=tD, scalar1=lng[:, cc:cc + 1],
                                                          scalar2=lnb[:, cc:cc + 1], op0=ALU.mult, op1=ALU.add),
                  r=[tk("tD"), "vec"], w=[tk("tD")])
            silu_mul(tD, tk("tD"), CN[:, cc, :], tk(f"CN{cc}"), None, None, tE[:, 0:ntok], tk("tE"))
            yield
        for oc in range(4):
            wg, wgt = wload(3072 + oc * 128)
            proj_T(wg, wgt, 0, xb, xbtok, ntok, rrep, rreptok, tA, tk("tA"))
            yield
            b, bt = misc_bank()
            ps = bank(b, ntok)
            for cc in range(4):
                S.add("pe", lambda e, cc=cc, oc=oc, ps=ps: e.matmul(ps, lhsT=wpw2[:, cc, oc * 128:(oc + 1) * 128],
                                                                    rhs=CN[:, cc, :], start=(cc == 0), stop=(cc == 3)),
                      r=["wpw2", tk(f"CN{cc}")], w=[bt])
            S.add("dve", lambda e, oc=oc, ps=ps: e.tensor_scalar(out=tB, in0=ps, scalar1=bp2[:, oc:oc + 1], scalar2=None,
                                                                 op0=ALU.add), r=[bt, "vec"], w=[tk("tB")])
            silu_mul(tA, tk("tA"), mconv_out(oc), f"{mconvtok}{oc}", tB, tk("tB"), tE[:, 0:ntok], tk("tE"))
            yield
        if ctail is not None:
            b, bt = misc_bank()
            ps = bank(b, 512)
            for cc in range(4):
                S.add("pe", lambda e, cc=cc, ps=ps: e.transpose(ps[0:HW, cc * 128:(cc + 1) * 128],
                                                                UT[:, cc, 0, HWS + T - HW:HWS + T], ident),
                      r=[tk(f"UT{cc}"), "ident"], w=[bt])
            ctb = tE[0:HW, 0:512]
            S.add("dve", lambda e, ps=ps: e.tensor_copy(out=ctb, in_=ps[0:HW, :]), r=[bt], w=[tk("tE")])
            S.add("sp", lambda e: e.dma_start(out=ctail, in_=ctb), r=[tk("tE")], w=["OUT_ctail"], dma=True)
        if post is not None:
            post(UT, tk, tE)
        A.release(m0)

    def out_phase(tag, otag, nblk, blk, mT, mTtoks, wout, xnat, yout, xr_alias=None, alias_tok=None):
        m0 = A.mark()
        if xr_alias is None:
            xr = [A.alloc([128, D], F32) for _ in range(2)]
        else:
            xr = xr_alias
        yr = [A.alloc([128, D], F32) for _ in range(2)]
        ssq = A.alloc([128, 2], F32)
        tk = lambda n: f"{tag}_{n}"
        for tb in range(nblk):
            k = tb % 2
            al_r = [alias_tok] if (alias_tok and tb > 0) else []
            al_w = [alias_tok] if (alias_tok and tb == 0) else []
            S.add("sp", lambda e, tb=tb, k=k: e.dma_start(out=xr[k][0:blk, :], in_=xnat(tb)), r=al_r,
                  w=[tk(f"xr{k}")] + al_w, dma=True)
            for half in range(2):
                b, bt = misc_bank()
                ps = bank(b, 512)
                for kc in range(8):
                    S.add("pe", lambda e, kc=kc, half=half, tb=tb, ps=ps: e.matmul(
                        ps[0:blk, :], lhsT=mT(kc, tb), rhs=wout[:, kc, half * 512:(half + 1) * 512],
                        start=(kc == 0), stop=(kc == 7)), r=list(mTtoks[kc]) + ["wout"], w=[bt])
                S.add("dve", lambda e, half=half, k=k, ps=ps: e.tensor_tensor(
                    out=yr[k][0:blk, half * 512:(half + 1) * 512], in0=ps[0:blk, :],
                    in1=xr[k][0:blk, half * 512:(half + 1) * 512], op=ALU.add),
                    r=[bt, tk(f"xr{k}")] + ([alias_tok] if alias_tok else []),
                    w=[tk(f"yr{k}")])
            S.add("dve", lambda e, k=k: e.tensor_tensor(out=xr[k][0:blk, :], in0=yr[k][0:blk, :], in1=yr[k][0:blk, :],
                                                        op=ALU.mult),
                  r=[tk(f"yr{k}")] + ([alias_tok] if alias_tok else []), w=[tk(f"xr{k}")])
            S.add("dve", lambda e, k=k: e.reduce_sum(out=ssq[0:blk, k:k + 1], in_=xr[k][0:blk, :],
                                                     axis=mybir.AxisListType.X),
                  r=[tk(f"xr{k}")] + ([alias_tok] if alias_tok else []), w=[tk(f"ssq{k}")])
            rstd_from("dve", ssq[0:blk, k:k + 1], ssq[0:blk, k:k + 1], D, [tk(f"ssq{k}")], tk(f"ssq{k}"))
            S.add("dve", lambda e, k=k: e.scalar_tensor_tensor(out=yr[k][0:blk, :], in0=yr[k][0:blk, :],
                                                               scalar=ssq[0:blk, k:k + 1], in1=fgr[0:blk, :],
                                                               op0=ALU.mult, op1=ALU.mult),
                  r=[tk(f"yr{k}"), tk(f"ssq{k}"), "rep"], w=[tk(f"yr{k}")])
            S.add("sp", lambda e, tb=tb, k=k: e.dma_start(out=yout(tb), in_=yr[k][0:blk, :]), r=[tk(f"yr{k}")],
                  w=[f"OUT_{otag}y{tb}"], dma=True)
        A.release(m0)

    wpw2 = A.alloc([128, 4, 512], BF16)
    load_w(wpw2, w_pw2_v, "wpw2")
    MATT = A.alloc([128, 4, NSL * U], BF16)
    MCONV = A.alloc([128, 4, NSL * U], BF16)
    mP = A.mark()
    WKV = A.alloc([128, DC, 512], BF16)
    WQG = A.alloc([128, DC, 512], BF16)
    KT = A.alloc([128, 2, NU * U], BF16)
    VA = A.alloc([128, NU * NKB, 2, 129], BF16)
    xf = A.alloc([128, DC, U], F32)
    xb = A.alloc([128, DC, U], BF16)
    xsq = A.alloc([128, DC, U], BF16)
    rrep = A.alloc([128, U], F32)
    rnat = A.alloc([128, NKB], F32)
    QT = A.alloc([128, 2, 2, U], BF16)
    PT = [A.alloc([128, 2, QW], BF16) for _ in range(3)]
    sgatt = A.alloc([128, NKB, 256], F32)
    kvo = [A.alloc([128, 512], F32) for _ in range(2)]
    xhf = A.alloc([128, DC, HW], F32)
    xhb = A.alloc([128, DC, HW], BF16)
    xhsq = A.alloc([128, DC, HW], BF16)
    rreph = A.alloc([128, HW], F32)
    ep = [A.alloc([128, 128], F32) for _ in range(4)]
    epl = A.alloc([128, 8], F32)
    gtmp = A.alloc([128, 256], F32)
    gtmp2 = A.alloc([128, 256], F32)
    mP2 = A.mark()
    xsq_f = xsq.rearrange("p a b -> p (a b)").bitcast(F32)
    xr_al = [xsq_f[:, 0:D], xsq_f[:, D:2 * D]] if 4 * U >= 2 * D else None

    S.add("pool", lambda e: e.memset(VA[:, :, :, 128:129], 1.0), w=["VAones"])
    S.add("pool", lambda e: e.memset(QT, 0.0), w=["QT0", "QT1"])

    def attention(i, hl, h, bg=None):
        nun = 4 * i + 4
        blocks = []
        for n in range(nun):
            for kb in range(NKB):
                diag = (n == nun - 1)
                for qg in range(NQG):
                    qlo = qg * QW
                    q0 = max(qlo, kb * 128) if diag else qlo
                    if q0 >= qlo + QW:
                        continue
                    blocks.append((n, kb, qg, q0, diag))
        npt = [0]
        otoks = {}
        for m in range(2):
            for qs in range(NKB):
                a = m * NKB + qs
                otoks[(m, qs)] = f"OB{2 + a // 3}"

        def oacc(m, qs):
            a = m * NKB + qs
            return bank(2 + a // 3, 129, (a % 3) * 160)

        pend = []
        started = set()

        def emit_pv(item):
            (n, kb, qg, q0, diag, ptk, pt) = item
            g = n * NKB + kb
            for m in range(2):
                for qs in range(q0 // 128, (qg * QW + QW) // 128):
                    bk = 2 + (m * NKB + qs) // 3
                    first = bk not in started
                    started.add(bk)
                    last = diag and (kb == qs)
                    c0 = qs * 128 - qg * QW
                    S.add("pe", lambda e, m=m, qs=qs, c0=c0, g=g, pt=pt, first=first, last=last: e.matmul(
                        oacc(m, qs)[:, 0:129], lhsT=pt[:, m, c0:c0 + 128], rhs=VA[:, g, hl, :], start=first, stop=last,
                        skip_group_check=True),
                        r=[f"PT{ptk}", f"VA{n}_{hl}", "VAones"], w=[otoks[(m, qs)]])

        for (n, kb, qg, q0, diag) in blocks:
            g = n * NKB + kb
            sbk = npt[0] % 2
            ptk = npt[0] % 3
            npt[0] += 1
            st = bank(sbk, 2 * QW).rearrange("p (m q) -> p m q", m=2)
            c0 = q0 - qg * QW
            wq = qg * QW + QW - q0

            def mm(e, st=st, c0=c0, wq=wq, g=g, q0=q0):
                return e.matmul(st[:, :, c0:c0 + wq], lhsT=KT[:, hl, g * 128:(g + 1) * 128],
                                rhs=QT[:, hl, :, q0:q0 + wq], start=True, stop=True)
            S.add("pe", mm, r=[f"KT{n}_{hl}", f"QT{hl}"], w=[f"ST{sbk}"])
            pt = PT[ptk]
            unc = (n >= 4 * i) and not diag
            S.add("act", lambda e, st=st, pt=pt, c0=c0, wq=wq: e.activation(
                out=pt[:, :, c0:c0 + wq], in_=st[:, :, c0:c0 + wq], func=AF.Exp, scale=0.125),
                r=[f"ST{sbk}"], w=[f"PT{ptk}"])
            if unc:
                sel = bs[:, i * 3 + (n - 4 * i): i * 3 + (n - 4 * i) + 1]
                S.add("dve", lambda e, pt=pt, c0=c0, wq=wq, sel=sel: e.tensor_scalar(
                    out=pt[:, :, c0:c0 + wq], in0=pt[:, :, c0:c0 + wq], scalar1=sel, scalar2=None, op0=ALU.mult),
                    r=[f"PT{ptk}", "bs"], w=[f"PT{ptk}"])
            if diag and q0 == kb * 128:
                for m in range(2):
                    S.add("dve", lambda e, pt=pt, c0=c0, m=m: e.tensor_tensor(out=pt[:, m, c0:c0 + 128],
                                                                               in0=pt[:, m, c0:c0 + 128], in1=tri,
                                                                               op=ALU.mult),
                          r=[f"PT{ptk}", "tri"], w=[f"PT{ptk}"])
            pend.append((n, kb, qg, q0, diag, ptk, pt))
            if len(pend) > 1:
                emit_pv(pend.pop(0))
            if bg is not None:
                next(bg, None)
                next(bg, None)
                next(bg, None)
        while pend:
            emit_pv(pend.pop(0))
        allo = sorted(set(otoks.values()))
        for qs in range(NKB):
            o0 = oacc(0, qs)
            o1 = oacc(1, qs)
            par = qs % 2
            e0 = ep[par * 2]
            e1 = ep[par * 2 + 1]
            el = epl[:, par * 4:par * 4 + 4]
            te0, te1, tel = f"ep_e0{par}", f"ep_e1{par}", f"ep_el{par}"
            S.add("dve", lambda e, o0=o0, el=el: e.reciprocal(out=el[:, 0:1], in_=o0[:, 128:129]), r=allo,
                  w=[tel])
            S.add("dve", lambda e, o1=o1, el=el: e.reciprocal(out=el[:, 1:2], in_=o1[:, 128:129]),
                  r=allo + [tel], w=[tel])
            S.add("dve", lambda e, el=el: e.tensor_tensor(out=el[:, 1:2], in0=el[:, 1:2], in1=lam[:, 1:2], op=ALU.mult),
                  r=[tel, "lam"], w=[tel])
            S.add("dve", lambda e, o0=o0, e0=e0, el=el: e.tensor_scalar(out=e0, in0=o0[:, 0:128], scalar1=el[:, 0:1],
                                                                        scalar2=None, op0=ALU.mult),
                  r=allo + [tel], w=[te0])
            S.add("dve", lambda e, o1=o1, e0=e0, el=el: e.scalar_tensor_tensor(out=e0, in0=o1[:, 0:128],
                                                                               scalar=el[:, 1:2], in1=e0,
                                                                               op0=ALU.mult, op1=ALU.add),
                  r=allo + [tel, te0], w=[te0])
            S.add("dve", lambda e, e0=e0, e1=e1: e.tensor_tensor(out=e1, in0=e0, in1=e0, op=ALU.mult), r=[te0], w=[te1])
            S.add("dve", lambda e, e1=e1, el=el: e.reduce_sum(out=el[:, 2:3], in_=e1, axis=mybir.AxisListType.X),
                  r=[te1, tel], w=[tel])
            rstd_from("dve", el[:, 2:3], el[:, 2:3], 128, [tel], tel)
            S.add("dve", lambda e, e0=e0, el=el: e.scalar_tensor_tensor(out=e0, in0=e0, scalar=el[:, 2:3], in1=sgr,
                                                                        op0=ALU.mult, op1=ALU.mult),
                  r=[te0, tel, "sgr"], w=[te0])
            S.add("dve", lambda e, e0=e0, qs=qs: e.tensor_tensor(out=e0, in0=e0, in1=sgatt[:, qs, hl * 128:(hl + 1) * 128],
                                                                 op=ALU.mult), r=[te0, f"sgatt{qs}"], w=[te0])
            b, bt = misc_bank()
            ps = bank(b, 128)
            S.add("pe", lambda e, e0=e0, ps=ps: e.transpose(ps, e0, ident), r=[te0, "ident"], w=[bt])
            S.add("dve", lambda e, ps=ps, qs=qs: e.tensor_copy(out=MATT[:, h, i * U + qs * 128: i * U + (qs + 1) * 128],
                                                               in_=ps), r=[bt], w=[f"MATT{h}_{i}_{qs}"])

    for hp in (range(2) if 'P' in cfg.get('phases', 'PS') else []):
        load_w(WKV[:, :, 0:256], w_in_v[:, :, 512 + hp * 256: 512 + hp * 256 + 256], "WK")
        load_w(WKV[:, :, 256:512], w_in_v[:, :, 1024 + hp * 256: 1024 + hp * 256 + 256], "WV")
        load_w(WQG[:, :, 0:256], w_in_v[:, :, hp * 256: hp * 256 + 256], "WQ")
        load_w(WQG[:, :, 256:512], w_in_v[:, :, 1536 + hp * 256: 1536 + hp * 256 + 256], "WG")
        if hp == 1:
            S.barrier()
            A.release(mP2)
            wout = A.alloc([128, DC, D], BF16)
            load_w(wout, w_out_v, "wout")
        for n in range(NU):
            own = (n % 4 == 3)
            i = n // 4
            S.add("sp", lambda e, n=n: e.dma_start(out=xf, in_=xT[n]), w=["xf"], dma=True)
            prep_x(xf, "xf", xb, "xb", xsq, "xsq", U)
            stats_rep(xsq, "xsq", U, rrep, "rrep")
            stats_nat(xsq, "xsq", NKB, rnat, "rnat")
            for hl in range(2):
                proj_T(WKV, "WK", hl * 128, xb, "xb", U, rrep, "rrep", KT[:, hl, n * U:(n + 1) * U], f"KT{n}_{hl}")
            for tb in range(NKB):
                b, bt = misc_bank()
                ps = bank(b, 512)
                for dc in range(DC):
                    S.add("pe", lambda e, dc=dc, tb=tb, ps=ps: e.matmul(ps[:, 0:256], lhsT=xb[:, dc, tb * 128:(tb + 1) * 128],
                                                                       rhs=WKV[:, dc, 256:512], start=(dc == 0),
                                                                       stop=(dc == DC - 1)),
                          r=xb_toks("xb") + ["WV"], w=[bt])
                g = n * NKB + tb
                S.add("act", lambda e, ps=ps, g=g, tb=tb: e.activation(
                    out=VA[:, g, :, 0:128], in_=ps[:, 0:256].rearrange("p (h e) -> p h e", h=2), func=AF.Copy,
                    scale=rnat[:, tb:tb + 1]), r=[bt, "rnat"], w=[f"VA{n}_0", f"VA{n}_1"])
                if own:
                    k = tb % 2
                    S.add("dve", lambda e, ps=ps, k=k, tb=tb: e.tensor_scalar(out=kvo[k][:, 256:512], in0=ps[:, 0:256],
                                                                              scalar1=rnat[:, tb:tb + 1], scalar2=None,
                                                                              op0=ALU.mult), r=[bt, "rnat"], w=[f"kvoV{k}"])
                    S.add("sp", lambda e, k=k, tb=tb, i=i, hp=hp: e.dma_start(
                        out=nv_o[i, tb * 128:(tb + 1) * 128, hp * 256:(hp + 1) * 256], in_=kvo[k][:, 256:512]),
                        r=[f"kvoV{k}"], w=[f"OUT_nv{hp}_{i}_{tb}"], dma=True)
                    b2, bt2 = misc_bank()
                    ps2 = bank(b2, 512)
                    for dc in range(DC):
                        S.add("pe", lambda e, dc=dc, tb=tb, ps2=ps2: e.matmul(
                            ps2[:, 0:256], lhsT=xb[:, dc, tb * 128:(tb + 1) * 128], rhs=WKV[:, dc, 0:256],
                            start=(dc == 0), stop=(dc == DC - 1)), r=xb_toks("xb") + ["WK"], w=[bt2])
                    for dc in range(DC):
                        S.add("pe", lambda e, dc=dc, tb=tb, ps2=ps2: e.matmul(
                            ps2[:, 256:512], lhsT=xb[:, dc, tb * 128:(tb + 1) * 128], rhs=WQG[:, dc, 256:512],
                            start=(dc == 0), stop=(dc == DC - 1)), r=xb_toks("xb") + ["WG"], w=[bt2])
                    S.add("dve", lambda e, ps2=ps2, k=k, tb=tb: e.tensor_scalar(out=kvo[k][:, 0:256], in0=ps2[:, 0:256],
                                                                                scalar1=rnat[:, tb:tb + 1], scalar2=None,
                                                                                op0=ALU.mult), r=[bt2, "rnat"], w=[f"kvoK{k}"])
                    S.add("sp", lambda e, k=k, tb=tb, i=i, hp=hp: e.dma_start(
                        out=nk_o[i, tb * 128:(tb + 1) * 128, hp * 256:(hp + 1) * 256], in_=kvo[k][:, 0:256]),
                        r=[f"kvoK{k}"], w=[f"OUT_nk{hp}_{i}_{tb}"], dma=True)
                    S.add("dve", lambda e, ps2=ps2, tb=tb: e.tensor_scalar(out=gtmp, in0=ps2[:, 256:512],
                                                                           scalar1=rnat[:, tb:tb + 1], scalar2=None,
                                                                           op0=ALU.mult), r=[bt2, "rnat"], w=["gtmp"])
                    silu_mul(gtmp, "gtmp", sgatt[:, tb, :], f"sgatt{tb}", None, None, gtmp2, "gtmp2")
            if own:
                for hl in range(2):
                    b, bt = misc_bank()
                    psq = bank(b, U)
                    for dc in range(DC):
                        S.add("pe", lambda e, dc=dc, hl=hl, psq=psq: e.matmul(
                            psq, lhsT=WQG[:, dc, hl * 128:(hl + 1) * 128], rhs=xb[:, dc, :], start=(dc == 0),
                            stop=(dc == DC - 1)), r=["WQ"] + xb_toks("xb"), w=[bt])
                    S.add("dve", lambda e, hl=hl, psq=psq: e.tensor_tensor(out=QT[0:64, hl, 0, :], in0=psq[0:64, :],
                                                                          in1=rrep[0:64, :], op=ALU.mult),
                          r=[bt, "rrep"], w=[f"QT{hl}"])
                    S.add("dve", lambda e, hl=hl, psq=psq: e.tensor_tensor(out=QT[64:128, hl, 1, :], in0=psq[64:128, :],
                                                                          in1=rrep[64:128, :], op=ALU.mult),
                          r=[bt, "rrep", f"QT{hl}"], w=[f"QT{hl}"])
                if hp == 0:
                    S.add("sp", lambda e, i=i: e.dma_start(out=xhf, in_=xh[i]), w=["xhf"], dma=True)
                    prep_x(xhf, "xhf", xhb, "xhb", xhsq, "xhsq", HW)
                    stats_rep(xhsq, "xhsq", HW, rreph, "rreph")
                    bg = row_phase("rp", xb, "xb", xhb, "xhb", rrep, "rrep", rreph, "rreph", 1, U, HW, None,
                              lambda oc, i=i: MCONV[:, oc, i * U:(i + 1) * U], f"MCONV{i}_", wpw2,
                              ctail=(ctail_o if i == NSL - 1 else None), scr_f32=xf, scr_f32_tok="xf",
                              scr_bf=xsq, scr_bf_tok="xsq")
                for hl in range(2):
                    attention(i, hl, hp * 2 + hl, bg if hp == 0 else None)
                if hp == 0:
                    for _ in bg:
                        pass
                if hp == 1:
                    def mT(kc, tb, i=i):
                        if kc < 4:
                            return MATT[:, kc, i * U + tb * 128: i * U + (tb + 1) * 128]
                        return MCONV[:, kc - 4, i * U + tb * 128: i * U + (tb + 1) * 128]
                    mtoks = []
                    for kc in range(8):
                        if kc < 4:
                            mtoks.append([f"MATT{kc}_{i}_{qs}" for qs in range(NKB)])
                        else:
                            mtoks.append([f"MCONV{i}_{kc - 4}"])
                    out_phase("op", f"p{i}", NKB, 128, mT, mtoks, wout,
                              lambda tb, i=i: xn[i, tb * 128:(tb + 1) * 128, :],
                              lambda tb, i=i: y_o[i, tb * 128:(tb + 1) * 128, :],
                              xr_alias=xr_al, alias_tok="xsq")

    S.barrier()
    A.release(mP)
    BARR = []

    if 'S' in cfg.get('phases', 'PS'):
        WS = A.alloc([128, DC, 2048], BF16)
        S.add("pool", lambda e: e.dma_start(out=WS[:, :, 0:1024], in_=w_in_v[:, :, 0:1024]), w=["WSa"], dma=True)
        S.add("pool", lambda e: e.dma_start(out=WS[:, :, 1024:2048], in_=w_in_v[:, :, 1024:2048]), w=["WSb"], dma=True)
        wout2 = A.alloc([128, DC, D], BF16)
        S.add("pool", lambda e: e.dma_start(out=wout2, in_=w_out_v), w=["wout"], dma=True)
        sxf = A.alloc([128, DC, NTS], F32)
        sxb = A.alloc([128, DC, NTS], BF16)
        sxsq = A.alloc([128, DC, NTS], BF16)
        srrep = A.alloc([128, NTS], F32)
        srnat = A.alloc([128, 1], F32)
        S.add("sp", lambda e: e.dma_start(out=sxf, in_=xsT), w=["sxf"], dma=True)
        prep_x(sxf, "sxf", sxb, "sxb", sxsq, "sxsq", NTS)
        stats_rep(sxsq, "sxsq", NTS, srrep, "srrep")
        stats_nat(sxsq, "sxsq", 1, srnat, "srnat", blk=NTS)
        skv = A.alloc([128, 1024], F32)
        VSb = A.alloc([128, 512], BF16)
        for which, c0 in (("k", 512), ("v", 1024)):
            b, bt = misc_bank()
            ps = bank(b, 512)
            for dc in range(DC):
                S.add("pe", lambda e, dc=dc, ps=ps, c0=c0: e.matmul(ps[0:NTS, :], lhsT=sxb[:, dc, :], rhs=WS[:, dc, c0:c0 + 512],
                                                                   start=(dc == 0), stop=(dc == DC - 1)),
                      r=xb_toks("sxb") + ["WSa", "WSb"], w=[bt])
            o = skv[0:NTS, 0:512] if which == "k" else skv[0:NTS, 512:1024]
            S.add("dve", lambda e, ps=ps, o=o: e.tensor_scalar(out=o, in0=ps[0:NTS, :], scalar1=srnat[0:NTS, 0:1],
                                                               scalar2=None, op0=ALU.mult), r=[bt, "srnat"], w=[f"skv{which}"])
            dst = nks_o if which == "k" else nvs_o
            S.add("sp", lambda e, o=o, dst=dst: e.dma_start(out=dst, in_=o), r=[f"skv{which}"], w=[f"OUT_s{which}"], dma=True)
            if which == "v":
                S.add("dve", lambda e, o=o: e.tensor_copy(out=VSb[0:NTS, :], in_=o),
                      r=["skvv"], w=["VSb"])
        KTs = A.alloc([128, NH, NTS], BF16)
        Qblk = A.alloc([128, NH, NSQ, 2 * DSEQ], BF16)
        SGT = A.alloc([128, NH, NTS], F32)
        stmp = A.alloc([128, NTS], F32)
        stmp2 = A.alloc([128, NTS], F32)
        stmp3 = A.alloc([128, NTS], F32)
        S.add("pool", lambda e: e.memset(Qblk, 0.0), w=["Qblk"])
        for h in range(NH):
            proj_T(WS, "WSa", 512 + h * 128, sxb, "sxb", NTS, srrep, "srrep", KTs[:, h, :], f"KTs{h}")
            proj_T(WS, "WSa", h * 128, sxb, "sxb", NTS, srrep, "srrep", stmp, "stmp")
            S.add("dve", lambda e, h=h: e.tensor_copy(out=Qblk[0:64, h, :, 0:DSEQ],
                                                      in_=stmp[0:64, :].rearrange("p (s t) -> p s t", s=NSQ)),
                  r=["stmp"], w=["Qblk"])
            S.add("dve", lambda e, h=h: e.tensor_copy(out=Qblk[64:128, h, :, DSEQ:2 * DSEQ],
                                                      in_=stmp[64:128, :].rearrange("p (s t) -> p s t", s=NSQ)),
                  r=["stmp"], w=["Qblk"])
            proj_T(WS, "WSb", 1536 + h * 128, sxb, "sxb", NTS, srrep, "srrep", stmp2, "stmp2")
            silu_mul(stmp2, "stmp2", SGT[:, h, :], f"SGT{h}", None, None, stmp3, "stmp3")
        pts = A.alloc([128, NSQ * NPG], I32)
        ptf = A.alloc([128, NSQ * NPG], F32)
        IDX = A.alloc([128, NSQ * NPG], I32)
        S.add("pool", lambda e: e.dma_start(out=pts, in_=ptab), w=["pts"], dma=True)
        S.add("pool", lambda e: e.tensor_copy(out=ptf, in_=pts), r=["pts"], w=["ptf"])
        S.add("pool", lambda e: e.tensor_scalar(out=ptf, in0=ptf, scalar1=128.0,
                                                scalar2=vec[:, 8 + 4 * CW + 17:8 + 4 * CW + 18], op0=ALU.mult, op1=ALU.add),
              r=["ptf", "vec"], w=["ptf"])
        S.add("pool", lambda e: e.tensor_copy(out=IDX, in_=ptf), r=["ptf"], w=["IDX"])
        smask = A.alloc([128, NSQ, 2 * DSEQ], BF16)
        S.add("pool", lambda e: e.dma_start(out=smask, in_=smask_in), w=["smask"], dma=True)
        GP = 512 // (NH * 2 * DSEQ)
        NBK = 8
        NBV = GP + 8
        KP = [A.alloc([128, 512], BF16) for _ in range(NBK)]
        VP = [A.alloc([128, 512], BF16) for _ in range(NBV)]
        KTP = [A.alloc([128, NH, 128], BF16) for _ in range(2)]
        NBLK = NPG + 1
        PTs = A.alloc([128, NBLK, NH, 2 * DSEQ], BF16)
        OT = A.alloc([128, NSQ, NH, 2 * DSEQ], F32)
        LT = A.alloc([128, NSQ, NH, 2 * DSEQ], F32)
        cnt = [0]
        W16 = 2 * DSEQ
        for s in range(NSQ):
            vbuf = {}
            for pg in range(NBLK):
                stb = (pg // GP) % 2
                stc = (pg % GP) * NH * W16
                if pg < NPG:
                    k = cnt[0] % NBK
                    kv = cnt[0] % NBV
                    k2 = cnt[0] % 2
                    cnt[0] += 1
                    vbuf[pg] = kv
                    idx = s * NPG + pg

                    def ldk(e, k=k, idx=idx):
                        return e.indirect_dma_start(out=KP[k], out_offset=None, in_=cache_k,
                                                    in_offset=bass.IndirectOffsetOnAxis(ap=IDX[:, idx:idx + 1], axis=0))

                    def ldv(e, kv=kv, idx=idx):
                        return e.indirect_dma_start(out=VP[kv], out_offset=None, in_=cache_v,
                                                    in_offset=bass.IndirectOffsetOnAxis(ap=IDX[:, idx:idx + 1], axis=0))
                    S.add("pool", ldk, r=["IDX"], w=[f"KP{k}"], dma=True)
                    S.add("pool", ldv, r=["IDX"], w=[f"VP{kv}"], dma=True)
                    tp = bank(4, 256).bitcast(BF16).rearrange("p (h t) -> p h t", h=NH)
                    for h in range(NH):
                        S.add("pe", lambda e, h=h, k=k, tp=tp: e.transpose(tp[:, h, :], KP[k][:, h * 128:(h + 1) * 128], identb),
                              r=[f"KP{k}", "identb"], w=["PSB4"])
                    S.add("dve", lambda e, k2=k2, tp=tp: e.tensor_copy(out=KTP[k2], in_=tp), r=["PSB4"], w=[f"KTP{k2}"])
                    for h in range(NH):
                        S.add("pe", lambda e, h=h, k2=k2, stb=stb, stc=stc, s=s: e.matmul(
                            bank(stb, W16, stc + h * W16), lhsT=KTP[k2][:, h, :], rhs=Qblk[:, h, s, :],
                            start=True, stop=True), r=[f"KTP{k2}", "Qblk"], w=[f"SST{stb}"])
                else:
                    for h in range(NH):
                        S.add("pe", lambda e, h=h, stb=stb, stc=stc, s=s: e.matmul(
                            bank(stb, W16, stc + h * W16)[0:NTS, :], lhsT=KTs[:, h, :], rhs=Qblk[:, h, s, :],
                            start=True, stop=True), r=[f"KTs{h}", "Qblk"], w=[f"SST{stb}"])
                lastin = (pg % GP == GP - 1) or (pg == NBLK - 1)
                if not lastin:
                    continue
                g0 = (pg // GP) * GP
                ng = pg - g0 + 1
                nfull = ng if pg < NPG else ng - 1
                if nfull > 0:
                    S.add("act", lambda e, stb=stb, g0=g0, nfull=nfull: e.activation(
                        out=PTs[:, g0:g0 + nfull, :, :].rearrange("p a h q -> p (a h q)"),
                        in_=bank(stb, nfull * NH * W16), func=AF.Exp, scale=0.125), r=[f"SST{stb}"], w=["PTs"])
                if pg == NBLK - 1:
                    cl = (pg % GP) * NH * W16
                    S.add("act", lambda e, stb=stb, cl=cl: e.activation(
                        out=PTs[0:NTS, NPG, :, :].rearrange("p h q -> p (h q)"),
                        in_=bank(stb, NH * W16, cl)[0:NTS, :], func=AF.Exp, scale=0.125), r=[f"SST{stb}"], w=["PTs"])
                    for h in range(NH):
                        S.add("dve", lambda e, s=s, h=h: e.tensor_tensor(
                            out=PTs[0:NTS, NPG, h, :], in0=PTs[0:NTS, NPG, h, :], in1=smask[0:NTS, s, :], op=ALU.mult),
                            r=["PTs", "smask"], w=["PTs"])
                for p2 in range(g0, pg + 1):
                    if p2 == NPG:
                        vt, vk, np_ = VSb, "VSb", NTS
                    else:
                        vt, vk, np_ = VP[vbuf[p2]], f"VP{vbuf[p2]}", 128
                    for h in range(NH):
                        S.add("pe", lambda e, h=h, s=s, p2=p2, vt=vt, np_=np_: e.matmul(
                            bank(2, W16, (s % 8) * 64 + h * W16), lhsT=vt[0:np_, h * 128:(h + 1) * 128],
                            rhs=PTs[0:np_, p2, h, :], start=(p2 == 0 and h == 0), stop=(p2 == NBLK - 1),
                            skip_group_check=True),
                            r=[vk, "PTs"], w=["SOT"])
                    S.add("pe", lambda e, s=s, p2=p2, np_=np_: e.matmul(
                        bank(3, NH * W16, (s % 8) * 64), lhsT=onesb[0:np_, :],
                        rhs=PTs[0:np_, p2, :, :].rearrange("p h q -> p (h q)"), start=(p2 == 0), stop=(p2 == NBLK - 1)),
                        r=["PTs", "onesb"], w=["SLT"])
            S.add("dve", lambda e, s=s: e.tensor_copy(out=OT[:, s, :, :].rearrange("p h q -> p (h q)"),
                                                      in_=bank(2, 64, (s % 8) * 64)), r=["SOT"], w=["OT"])
            S.add("dve", lambda e, s=s: e.tensor_copy(out=LT[:, s, :, :].rearrange("p h q -> p (h q)"),
                                                      in_=bank(3, 64, (s % 8) * 64)), r=["SLT"], w=["LT"])
        NQ = NSQ * NH * DSEQ
        so = A.alloc([128, NSQ, NH, DSEQ], F32)
        so2 = A.alloc([128, NSQ, NH, DSEQ], F32)
        S.add("dve", lambda e: e.reciprocal(out=LT, in_=LT), r=["LT"], w=["LT"])
        S.add("dve", lambda e: e.tensor_tensor(out=OT, in0=OT, in1=LT, op=ALU.mult), r=["LT", "OT"], w=["OT"])
        S.add("dve", lambda e: e.scalar_tensor_tensor(out=so, in0=OT[:, :, :, DSEQ:2 * DSEQ], scalar=lam[:, 1:2],
                                                      in1=OT[:, :, :, 0:DSEQ], op0=ALU.mult, op1=ALU.add),
              r=["OT", "lam"], w=["so"])
        S.add("dve", lambda e: e.tensor_tensor(out=so2, in0=so, in1=so, op=ALU.mult), r=["so"], w=["so2"])
        b, bt = misc_bank()
        ps = bank(b, NQ)
        S.add("pe", lambda e, ps=ps: e.matmul(ps, lhsT=onesf, rhs=so2.rearrange("p s h q -> p (s h q)"), start=True,
                                              stop=True), r=["so2", "onesf"], w=[bt])
        rstd_from("dve", so2.rearrange("p s h q -> p (s h q)"), ps, 128, [bt, "so2"], "so2")
        S.add("dve", lambda e: e.tensor_tensor(out=so, in0=so, in1=so2, op=ALU.mult), r=["so", "so2"], w=["so"])
        MATs = A.alloc([128, NH, NTS], BF16)
        for h in range(NH):
            S.add("dve", lambda e, h=h: e.scalar_tensor_tensor(
                out=stmp.rearrange("p (s q) -> p s q", s=NSQ), in0=so[:, :, h, :], scalar=sgcol[:, 0:1],
                in1=SGT[:, h, :].rearrange("p (s q) -> p s q", s=NSQ), op0=ALU.mult, op1=ALU.mult),
                r=["so", "sgcol", f"SGT{h}"], w=["stmp"])
            S.add("dve", lambda e, h=h: e.tensor_copy(out=MATs[:, h, :], in_=stmp), r=["stmp"], w=[f"MATs{h}"])
        MCs = A.alloc([128, 4, NTS], BF16)
        S.add("sp", lambda e: e.dma_start(out=cs_o[:, 0:CW - 1 - DSEQ, :], in_=stn[:, DSEQ:CW - 1, :]),
              w=["OUT_cs_a"], dma=True)

        def post(UTs, tk, tE):
            ucm = A.alloc([128, 4, NTS], F32)
            unat = A.alloc([128, 512], F32)
            b, bt = misc_bank()
            psu = bank(b, 512)
            for cc in range(4):
                S.add("dve", lambda e, cc=cc: e.tensor_copy(out=ucm[:, cc, :].rearrange("p (s t) -> p s t", s=NSQ),
                                                            in_=UTs[:, cc, :, CW - 1:CW - 1 + DSEQ]),
                      r=[tk(f"UT{cc}")], w=[f"ucm{cc}"])
                S.add("pe", lambda e, cc=cc: e.transpose(psu[0:NTS, cc * 128:(cc + 1) * 128], ucm[:, cc, :], ident),
                      r=[f"ucm{cc}", "ident"], w=[bt])
            S.add("dve", lambda e: e.tensor_copy(out=unat[0:NTS, :], in_=psu[0:NTS, :]), r=[bt], w=["unat"])
            for s in range(NSQ):
                S.add("sp", lambda e, s=s: e.dma_start(out=cs_o[s, CW - 1 - DSEQ:CW - 1, :],
                                                       in_=unat[s * DSEQ:(s + 1) * DSEQ, :]),
                      r=["unat"], w=[f"OUT_cs_b{s}"], dma=True)

        for _ in row_phase("srp", sxb, "sxb", None, None, srrep, "srrep", None, None, NSQ, DSEQ, CW - 1, stT,
                  lambda oc: MCs[:, oc, :], "MCs", wpw2, post=post, scr_f32=sxf, scr_f32_tok="sxf",
                  scr_bf=sxsq, scr_bf_tok="sxsq"):
            pass
        S.barrier()

        def mTs(kc, tb):
            return MATs[:, kc, :] if kc < 4 else MCs[:, kc - 4, :]
        out_phase("ops", "s", 1, NTS, mTs, [[f"MATs{h}"] for h in range(4)] + [[f"MCs{oc}"] for oc in range(4)], wout2,
                  lambda tb: xsn, lambda tb: ys_o)


    outs = [t for t in S.last_w.keys() if t.startswith("OUT_")]
    S.add("sp", lambda e: None, r=outs, w=["END"])

    from contextlib import ExitStack
    with ExitStack() as stack:
        S.finalize(nc, stack)
        with nc.Block() as block:
            @block.tensor
            def _(e):
                S.emit("pe", e)

            @block.scalar
            def _(e):
                S.emit("act", e)

            @block.vector
            def _(e):
                S.emit("dve", e)

            @block.gpsimd
            def _(e):
                S.emit("pool", e)

            @block.sync
            def _(e):
                S.emit("sp", e)
    return nc


def chunk_of(j, i):
    return [j, 7 - j, 8 + j, 15 - j][i]


def unit_perm(j):
    perm = []
    for i in range(4):
        c = chunk_of(j, i)
        grp = [u for u in range(4 * i, 4 * i + 4) if u != c]
        perm += grp + [c]
    return perm


def run(cfg, x_prompt, x_sample, cache_k, cache_v, state_conv, page_table, norm_g, w_in, lambda_q1, lambda_k1,
        lambda_q2, lambda_k2, subln_g, dw_w, dw_b, conv_ln_g, conv_ln_b, w_pw2, b_pw2, w_out, final_norm_g):
    U = cfg["U"]; NSQ = cfg["NSQ"]; NPG = cfg["NPG"]; NPOOL = cfg["NPOOL"]; DSEQ = cfg["DSEQ"]
    NTS = NSQ * DSEQ
    HW = 32
    f = lambda a: np.ascontiguousarray(np.asarray(a, dtype=np.float32))
    x_prompt = f(x_prompt); x_sample = f(x_sample)
    B, SEQ, _ = x_prompt.shape
    assert SEQ == 16 * U
    nc = build_nc(cfg)
    vecs = np.zeros((128, 8 + 4 * CW + 18), np.float32)
    vecs[:, 0:8] = f(norm_g)[0].reshape(8, 128).T
    dw = f(dw_w)[0]
    vecs[:, 8:8 + 4 * CW] = dw.T.reshape(4, 128, CW).transpose(1, 0, 2).reshape(128, 4 * CW)
    o = 8 + 4 * CW
    vecs[:, o:o + 4] = f(dw_b)[0].reshape(4, 128).T
    vecs[:, o + 4:o + 8] = f(conv_ln_g)[0].reshape(4, 128).T
    vecs[:, o + 8:o + 12] = f(conv_ln_b)[0].reshape(4, 128).T
    vecs[:, o + 12:o + 16] = f(b_pw2)[0].reshape(4, 128).T
    vecs[:, o + 16] = f(subln_g)[0]
    vecs[:, o + 17] = np.arange(128, dtype=np.float32)
    reps = np.zeros((128, D + 128 + 256), np.float32)
    reps[:, 0:D] = f(final_norm_g)[None, :]
    reps[:, D:D + 128] = f(subln_g)[0][None, :]
    reps[:, D + 128:D + 128 + 256] = np.concatenate([f(lambda_q1)[0], f(lambda_k1)[0], f(lambda_q2)[0],
                                                     f(lambda_k2)[0]])[None, :]
    ident = np.eye(128, dtype=np.float32)
    tri = np.triu(np.ones((128, 128), np.float32))
    smask = np.zeros((128, NSQ, 2 * DSEQ), np.float32)
    for t in range(NTS):
        s, kk = divmod(t, DSEQ)
        for q in range(DSEQ):
            if kk <= q:
                smask[t, s, q] = 1.0
                smask[t, s, DSEQ + q] = 1.0
    ck = f(cache_k)[0].reshape(NPOOL * 128, 512)
    cv = f(cache_v)[0].reshape(NPOOL * 128, 512)
    w_in0 = f(w_in)[0]; w_out0 = f(w_out)[0]; w_pw20 = f(w_pw2)[0]
    st = f(state_conv)[0]
    pt = np.asarray(page_table, dtype=np.int32)
    in_maps = []
    for c in range(8):
        b, j = divmod(c, 4)
        perm = unit_perm(j)
        xb_ = x_prompt[b].reshape(16, U, 8, 128)
        xT = np.ascontiguousarray(xb_[perm].transpose(0, 3, 2, 1))
        xn = np.stack([x_prompt[b, chunk_of(j, i) * U:(chunk_of(j, i) + 1) * U] for i in range(4)])
        xh = np.zeros((4, 128, 8, HW), np.float32)
        bsel = np.ones((128, 12), np.float32)
        for i in range(4):
            cpos = chunk_of(j, i)
            if cpos > 0:
                xh[i] = x_prompt[b, cpos * U - HW:cpos * U].reshape(HW, 8, 128).transpose(2, 1, 0)
            for t in range(3):
                if perm[4 * i + t] > cpos:
                    bsel[:, i * 3 + t] = 0.0
        xs = x_sample[c * NSQ:(c + 1) * NSQ].reshape(NTS, D)
        xsT = np.ascontiguousarray(xs.reshape(NTS, 8, 128).transpose(2, 1, 0))
        stc = st[c * NSQ:(c + 1) * NSQ]
        stT = np.ascontiguousarray(stc.reshape(NSQ, CW - 1, 4, 128).transpose(3, 2, 0, 1))
        in_maps.append(dict(
            xT=xT, xn=np.ascontiguousarray(xn), xh=xh, bsel=bsel, w_in=w_in0, w_out=w_out0, w_pw2=w_pw20, vecs=vecs,
            reps=reps, ident=ident, tri=tri, smask=smask, xsT=xsT, xsn=np.ascontiguousarray(xs), stT=stT,
            stn=np.ascontiguousarray(stc), cache_k=ck, cache_v=cv,
            ptab=np.ascontiguousarray(np.tile(pt[c * NSQ:(c + 1) * NSQ].reshape(1, NSQ * NPG), (128, 1)))))
    res = run_bass_kernel_spmd(nc, in_maps, core_ids=list(range(8)))
    R = res.results
    DB = 8 * NSQ
    y_prompt = np.zeros((B, SEQ, D), np.float32)
    nk = np.zeros((1, B, SEQ, NH, 128), np.float32)
    nv = np.zeros((1, B, SEQ, NH, 128), np.float32)
    ncp = np.zeros((1, B, CW - 1, 512), np.float32)
    y_sample = np.zeros((DB, DSEQ, D), np.float32)
    nks = np.zeros((1, DB, DSEQ, NH, 128), np.float32)
    nvs = np.zeros((1, DB, DSEQ, NH, 128), np.float32)
    ncs = np.zeros((1, DB, CW - 1, 512), np.float32)
    for c in range(8):
        b, j = divmod(c, 4)
        for i in range(4):
            cp = chunk_of(j, i)
            y_prompt[b, cp * U:(cp + 1) * U] = R[c]["y"][i]
            nk[0, b, cp * U:(cp + 1) * U] = R[c]["nk"][i].reshape(U, NH, 128)
            nv[0, b, cp * U:(cp + 1) * U] = R[c]["nv"][i].reshape(U, NH, 128)
            if cp == 15:
                ncp[0, b] = R[c]["ctail"][HW - (CW - 1):HW]
        y_sample[c * NSQ:(c + 1) * NSQ] = R[c]["ys"].reshape(NSQ, DSEQ, D)
        nks[0, c * NSQ:(c + 1) * NSQ] = R[c]["nks"].reshape(NSQ, DSEQ, NH, 128)
        nvs[0, c * NSQ:(c + 1) * NSQ] = R[c]["nvs"].reshape(NSQ, DSEQ, NH, 128)
        ncs[0, c * NSQ:(c + 1) * NSQ] = R[c]["cs"]
    return (y_prompt, y_sample, nk, nv, ncp, nks, nvs, ncs)


def kernel(**inputs):
    return run(CFG_FULL, **inputs)
```
